# Optimizing a Trainium2 kernel written in Bass

```python
import jax, jax.numpy as jnp
from jax import lax
import numpy as np


D_MODEL = 1024
BATCH = 4
SEQ = 8192
DEPTH = 1
DEC_BATCH = 128
DEC_SEQ = 8
PAST_LEN = 8192
PAGE_SIZE = 128

D_CONV = D_MODEL // 4
D_ATTN = D_MODEL - D_CONV
HEAD_DIM = 64
N_HEADS = D_ATTN // HEAD_DIM
D_IN = 2 * D_CONV + 3 * D_ATTN
CONV_A_WIDTH = 31
DILATED_GROUPS = ((128, 1), (512, 4), (2048, 16))
MAX_WINDOW = 2048
BAND_BLOCK = 128
D_FF = ((8 * D_MODEL // 3 + 255) // 256) * 256
FFN_CONV_WIDTH = 3
EPS = 1e-6
NEG_INF = -1e30
SCALE = HEAD_DIM ** -0.5

kernel_name = 'hymba_conformer_dilated_convffn_step'


def rms_norm(x, g):
    xf = x.astype(jnp.float32)
    y = xf * lax.rsqrt(jnp.mean(xf * xf, axis=-1, keepdims=True) + EPS)
    return (y * g.astype(jnp.float32)).astype(x.dtype)


def layer_norm(x, g, b):
    xf = x.astype(jnp.float32)
    mu = jnp.mean(xf, axis=-1, keepdims=True)
    xc = xf - mu
    y = xc * lax.rsqrt(jnp.mean(xc * xc, axis=-1, keepdims=True) + EPS)
    return (y * g.astype(jnp.float32) + b.astype(jnp.float32)).astype(x.dtype)


def depthwise_conv_valid(x_full, w, b):
    c = x_full.shape[-1]
    y = lax.conv_general_dilated(x_full, w[:, None, :], window_strides=(1,), padding='VALID',
                                 dimension_numbers=('NWC', 'WIO', 'NWC'), feature_group_count=c)
    return y + b


def softmax_attend(s, v, spec):
    mx = jnp.max(s, axis=-1, keepdims=True)
    p = jnp.exp(s - mx)
    den = jnp.sum(p, axis=-1)
    o = jnp.einsum(spec, p, v) / den[..., None]
    return o, mx[..., 0] + jnp.log(den)


def merge_dilation_groups(outs, lses, dtype):
    w = jax.nn.softmax(jnp.stack(lses), axis=0)
    return jnp.sum(w[..., None] * jnp.stack(outs), axis=0).astype(dtype)


def dilated_group_prompt(q, k, v, window, dilation):
    B, S, H, Dh = q.shape
    n_neigh = window // dilation
    span = dilation * BAND_BLOCK
    s_pad = -(-S // span) * span
    m_len = s_pad // dilation
    nb = m_len // BAND_BLOCK

    def to_sub(t):
        t = jnp.pad(t.astype(jnp.float32), ((0, 0), (0, s_pad - S), (0, 0), (0, 0)))
        t = t.reshape(B, m_len, dilation, H, Dh).transpose(0, 2, 1, 3, 4)
        return t.reshape(B, dilation, nb, BAND_BLOCK, H, Dh)

    def with_prev(t):
        prev = jnp.pad(t, ((0, 0), (0, 0), (1, 0), (0, 0), (0, 0), (0, 0)))[:, :, :-1]
        return jnp.concatenate([prev, t], axis=3)

    qs = to_sub(q)
    kk = with_prev(to_sub(k))
    vv = with_prev(to_sub(v))
    s = jnp.einsum('brnqhd,brnkhd->brnhqk', qs, kk) * SCALE
    qi = jnp.arange(BAND_BLOCK)[:, None]
    kc = jnp.arange(2 * BAND_BLOCK)[None, :]
    dist = qi + BAND_BLOCK - kc
    band = (dist >= 0) & (dist <= n_neigh)
    blk = jnp.arange(nb)[:, None, None]
    mask = band[None] & ((blk > 0) | (kc[None] >= BAND_BLOCK))
    s = jnp.where(mask[None, None, :, None], s, NEG_INF)
    o, lse = softmax_attend(s, vv, 'brnhqk,brnkhd->brnhqd')
    o = o.transpose(0, 2, 4, 1, 3, 5).reshape(B, s_pad, H, Dh)[:, :S]
    lse = lse.transpose(0, 2, 4, 1, 3).reshape(B, s_pad, H)[:, :S]
    return o, lse


def dilated_attention_prompt(q, k, v):
    res = [dilated_group_prompt(q, k, v, w, d) for (w, d) in DILATED_GROUPS]
    return merge_dilation_groups([r[0] for r in res], [r[1] for r in res], q.dtype)


def dilated_group_sample(q, kk, vv, window, dilation):
    T = q.shape[1]
    L = kk.shape[1] - T
    n_neigh = window // dilation
    idx = (L + jnp.arange(T))[:, None] - dilation * jnp.arange(n_neigh + 1)[None, :]
    valid = idx >= 0
    idx = jnp.maximum(idx, 0)
    kg = kk[:, idx]
    vg = vv[:, idx]
    s = jnp.einsum('bthd,btjhd->bthj', q, kg) * SCALE
    s = jnp.where(valid[None, :, None, :], s, NEG_INF)
    return softmax_attend(s, vg, 'bthj,btjhd->bthd')


def dilated_attention_sample(q, k, v, k_buf, v_buf):
    kk = jnp.concatenate([k_buf, k], axis=1).astype(jnp.float32)
    vv = jnp.concatenate([v_buf, v], axis=1).astype(jnp.float32)
    qf = q.astype(jnp.float32)
    res = [dilated_group_sample(qf, kk, vv, w, d) for (w, d) in DILATED_GROUPS]
    return merge_dilation_groups([r[0] for r in res], [r[1] for r in res], q.dtype)


def trunk_layer(x, conv_a_past, ffn_past, attend, norm_mix_g, w_in, conv_a_w, conv_a_b, ln_a_g, ln_a_b,
                w_out, norm_ffn_g, w_ffn_gate, w_ffn_up, ffn_conv_w, ffn_conv_b, w_ffn_down):
    B, S, _ = x.shape
    h = rms_norm(x, norm_mix_g)
    proj = h @ w_in
    a_val, a_gate, q, k, v = jnp.split(
        proj, [D_CONV, 2 * D_CONV, 2 * D_CONV + D_ATTN, 2 * D_CONV + 2 * D_ATTN], axis=-1)
    q = q.reshape(B, S, N_HEADS, HEAD_DIM)
    k = k.reshape(B, S, N_HEADS, HEAD_DIM)
    v = v.reshape(B, S, N_HEADS, HEAD_DIM)
    a_full = jnp.concatenate([conv_a_past, a_val * jax.nn.sigmoid(a_gate)], axis=1)
    a = jax.nn.silu(layer_norm(depthwise_conv_valid(a_full, conv_a_w, conv_a_b), ln_a_g, ln_a_b))
    o = attend(q, k, v).reshape(B, S, D_ATTN)
    x = x + jnp.concatenate([a, o], axis=-1) @ w_out
    h = rms_norm(x, norm_ffn_g)
    g_full = jnp.concatenate([ffn_past, h @ w_ffn_gate], axis=1)
    g = depthwise_conv_valid(g_full, ffn_conv_w, ffn_conv_b)
    x = x + (jax.nn.silu(g) * (h @ w_ffn_up)) @ w_ffn_down
    return x, k, v, a_full[:, -(CONV_A_WIDTH - 1):], g_full[:, -(FFN_CONV_WIDTH - 1):]


def setup_inputs(seed: int = 0) -> dict:
    key = jax.random.key(seed)
    ks = jax.random.split(key, 24)
    f32 = jnp.float32

    def nrm(k, shape, scale):
        return jax.random.normal(k, shape, f32) * scale

    win = min(MAX_WINDOW, PAST_LEN)
    return {
        'x_prompt': nrm(ks[0], (BATCH, SEQ, D_MODEL), 1.0),
        'x_sample': nrm(ks[1], (DEC_BATCH, DEC_SEQ, D_MODEL), 1.0),
        'cache_k_win': nrm(ks[2], (DEPTH, DEC_BATCH, win, N_HEADS, HEAD_DIM), 1.0),
        'cache_v_win': nrm(ks[3], (DEPTH, DEC_BATCH, win, N_HEADS, HEAD_DIM), 1.0),
        'state_conv_a': nrm(ks[4], (DEPTH, DEC_BATCH, CONV_A_WIDTH - 1, D_CONV), 0.5),
        'state_conv_ffn': nrm(ks[5], (DEPTH, DEC_BATCH, FFN_CONV_WIDTH - 1, D_FF), 1.0),
        'norm_mix_g': 1.0 + nrm(ks[6], (DEPTH, D_MODEL), 0.1),
        'w_in': nrm(ks[7], (DEPTH, D_MODEL, D_IN), D_MODEL ** -0.5),
        'conv_a_w': nrm(ks[8], (DEPTH, CONV_A_WIDTH, D_CONV), CONV_A_WIDTH ** -0.5),
        'conv_a_b': nrm(ks[9], (DEPTH, D_CONV), 0.02),
        'ln_a_g': 1.0 + nrm(ks[10], (DEPTH, D_CONV), 0.1),
        'ln_a_b': nrm(ks[11], (DEPTH, D_CONV), 0.02),
        'w_out': nrm(ks[12], (DEPTH, D_CONV + D_ATTN, D_MODEL), (D_CONV + D_ATTN) ** -0.5),
        'norm_ffn_g': 1.0 + nrm(ks[13], (DEPTH, D_MODEL), 0.1),
        'w_ffn_gate': nrm(ks[14], (DEPTH, D_MODEL, D_FF), D_MODEL ** -0.5),
        'w_ffn_up': nrm(ks[15], (DEPTH, D_MODEL, D_FF), D_MODEL ** -0.5),
        'ffn_conv_w': nrm(ks[16], (DEPTH, FFN_CONV_WIDTH, D_FF), FFN_CONV_WIDTH ** -0.5),
        'ffn_conv_b': nrm(ks[17], (DEPTH, D_FF), 0.02),
        'w_ffn_down': nrm(ks[18], (DEPTH, D_FF, D_MODEL), D_FF ** -0.5),
        'norm_final_g': 1.0 + nrm(ks[19], (D_MODEL,), 0.1),
    }


def reference(x_prompt, x_sample, cache_k_win, cache_v_win, state_conv_a, state_conv_ffn,
              norm_mix_g, w_in, conv_a_w, conv_a_b, ln_a_g, ln_a_b, w_out,
              norm_ffn_g, w_ffn_gate, w_ffn_up, ffn_conv_w, ffn_conv_b, w_ffn_down, norm_final_g):
    B, S, _ = x_prompt.shape
    win_p = min(MAX_WINDOW, S)
    xp, xs = x_prompt, x_sample
    kp, vp, cap, cfp = [], [], [], []
    ksn, vsn, cas, cfs = [], [], [], []
    for l in range(DEPTH):
        params = (norm_mix_g[l], w_in[l], conv_a_w[l], conv_a_b[l], ln_a_g[l], ln_a_b[l], w_out[l],
                  norm_ffn_g[l], w_ffn_gate[l], w_ffn_up[l], ffn_conv_w[l], ffn_conv_b[l], w_ffn_down[l])
        zeros_a = jnp.zeros((B, CONV_A_WIDTH - 1, D_CONV), xp.dtype)
        zeros_f = jnp.zeros((B, FFN_CONV_WIDTH - 1, D_FF), xp.dtype)
        xp, k, v, ca, cf = trunk_layer(xp, zeros_a, zeros_f, dilated_attention_prompt, *params)
        kp.append(k[:, S - win_p:])
        vp.append(v[:, S - win_p:])
        cap.append(ca)
        cfp.append(cf)
        k_buf, v_buf = cache_k_win[l], cache_v_win[l]

        def attend_sample(q, k_new, v_new, k_buf=k_buf, v_buf=v_buf):
            return dilated_attention_sample(q, k_new, v_new, k_buf, v_buf)

        xs, k, v, ca, cf = trunk_layer(xs, state_conv_a[l], state_conv_ffn[l], attend_sample, *params)
        ksn.append(k)
        vsn.append(v)
        cas.append(ca)
        cfs.append(cf)
    y_prompt = rms_norm(xp, norm_final_g)
    y_sample = rms_norm(xs, norm_final_g)
    return (y_prompt, y_sample, jnp.stack(kp), jnp.stack(vp), jnp.stack(cap), jnp.stack(cfp),
            jnp.stack(ksn), jnp.stack(vsn), jnp.stack(cas), jnp.stack(cfs))
```

```python
from contextlib import ExitStack
import numpy as np
import concourse.bass as bass
import concourse.mybir as mybir
from concourse.bass_utils import run_bass_kernel_spmd

F32 = mybir.dt.float32
BF16 = mybir.dt.bfloat16
AF = mybir.ActivationFunctionType
ALU = mybir.AluOpType

NCORES = 8
D = 1024
DCONV = 256
DATT = 768
NH = 12
DIN = 2816
DFF = 2816
NFF = 22
TT = 512
NHALO = 4
NPRE = 1
NMAIN = 8
NT = NHALO + NPRE + NMAIN
EPS = 1e-6
SCALE = 0.125
ENGS = ("pe", "act", "dve", "pool", "sp")


_POS = {"matmul": ("out", "lhsT", "rhs"), "transpose": ("out", "in_", "identity"), "memset": ("ap", "constant")}


def _mk(name, *args, **kw):
    for n, a in zip(_POS.get(name, ()), args):
        kw[n] = a
    if name == "matmul" and kw.get("tile_position", 0) is None:
        kw.pop("tile_position")
    return (name, kw)


class Res:
    __slots__ = ("name", "w", "r", "chan")

    def __init__(self, name):
        self.name = name
        self.w = None
        self.r = {}
        self.chan = None


class Planner:
    def __init__(self, nc, stack):
        self.nc = nc
        self.stack = stack
        self.sems = {}
        self.plan = {e: [] for e in ENGS}
        self.cnt = {e: 0 for e in ENGS}
        self.seen = {e: {} for e in ENGS}
        self.chans = []
        for e in ENGS:
            self.sems[e] = stack.enter_context(nc.semaphore(name="s_" + e))
        self.nuniq = 0

    def _need(self, reads, writes):
        need = {}

        def add(k, v):
            if need.get(k, 0) < v:
                need[k] = v

        for r in reads:
            if r.w is not None:
                add(*r.w)
        for w in writes:
            if w.w is not None:
                add(*w.w)
            for k, v in w.r.items():
                add(k, v)
        return need

    def _waits(self, ename, reads, writes):
        need = self._need(reads, writes)
        seen = self.seen[ename]
        for k, v in need.items():
            if k == ename and ename == "pe":
                continue
            if seen.get(k, 0) >= v:
                continue
            self.plan[ename].append(("w", self.sems[k], v))
            seen[k] = v

    def _mark(self, t, reads, writes):
        for r in reads:
            if r.r.get(t[0], 0) < t[1]:
                r.r[t[0]] = t[1]
        for w in writes:
            w.w = t
            w.r = {}

    def op(self, ename, fn, reads=(), writes=(), inc=True):
        self._waits(ename, reads, writes)
        if inc:
            self.cnt[ename] += 1
            t = (ename, self.cnt[ename])
            self.plan[ename].append(("i", fn, self.sems[ename], 1))
        else:
            t = (ename, self.cnt[ename] + 1)
            self.plan[ename].append(("i", fn, None, 0))
        self._mark(t, reads, writes)
        return t

    def dma(self, out, in_, reads, writes, chan, q="sp", **kw):
        self._waits(q, reads, writes)
        if chan.chan is None:
            key = ("d", self.nuniq)
            self.nuniq += 1
            self.sems[key] = self.stack.enter_context(self.nc.semaphore(name="d_%d" % key[1]))
            chan.chan = [key, 0]
            self.chans.append(chan)
        chan.chan[1] += 16
        self.plan[q].append(("i", ("dma_start", dict(out=out, in_=in_, **kw)), self.sems[chan.chan[0]], 16))
        t = (chan.chan[0], chan.chan[1])
        self._mark(t, reads, writes)
        return t

    def barrier(self):
        for e in ENGS:
            seen = self.seen[e]
            for f in ENGS:
                if f == "sp" or (f == e and e == "pe"):
                    continue
                v = self.cnt[f]
                if v > 0 and seen.get(f, 0) < v:
                    self.plan[e].append(("w", self.sems[f], v))
                    seen[f] = v
            for ch in self.chans:
                k, v = ch.chan
                if v > 0 and seen.get(k, 0) < v:
                    self.plan[e].append(("w", self.sems[k], v))
                    seen[k] = v

    def check_deadlock(self, plan):
        semv = getattr(self, "_semv", {})
        pos = {e: 0 for e in ENGS}
        rev = {id(v): k for k, v in self.sems.items()}
        progress = True
        while progress:
            progress = False
            for e in ENGS:
                items = plan[e]
                while pos[e] < len(items):
                    it = items[pos[e]]
                    if it[0] == "w":
                        k = rev[id(it[1])]
                        if semv.get(k, 0) >= it[2]:
                            pos[e] += 1
                            progress = True
                        else:
                            break
                    else:
                        if it[2] is not None:
                            k = rev[id(it[2])]
                            semv[k] = semv.get(k, 0) + it[3]
                        pos[e] += 1
                        progress = True
        self._semv = semv
        for e in ENGS:
            if pos[e] < len(plan[e]):
                it = plan[e][pos[e]]
                raise RuntimeError("DEADLOCK: engine %s stuck at item %d/%d waiting %s >= %s (have %s); next=%s" % (
                    e, pos[e], len(plan[e]), rev[id(it[1])], it[2], semv.get(rev[id(it[1])], 0),
                    [x[1][0] if x[0] == "i" else "w" for x in plan[e][pos[e]:pos[e] + 4]]))

    def flush(self):
        plan = self.plan
        self.plan = {e: [] for e in ENGS}
        self.check_deadlock(plan)

        def mk(ename):
            def body(eng):
                for item in plan[ename]:
                    if item[0] == "w":
                        eng.wait_ge(item[1], item[2])
                    else:
                        inst = getattr(eng, item[1][0])(**item[1][1])
                        if item[2] is not None:
                            inst.then_inc(item[2], item[3])
            return body

        with self.nc.Block() as block:
            block.tensor(mk("pe"))
            block.scalar(mk("act"))
            block.vector(mk("dve"))
            block.gpsimd(mk("pool"))
            block.sync(mk("sp"))


def build_program():
    nc = bass.Bass("TRN2", target_bir_lowering=False)

    def din(name, shape):
        return nc.dram_tensor(name, list(shape), F32, kind="ExternalInput").ap()

    def dout(name, shape):
        return nc.dram_tensor(name, list(shape), F32, kind="ExternalOutput").ap()

    xp = din("xp", [NT * TT, D])
    xs = din("xs", [128, D])
    ck = din("ck", [16, 2048, DATT])
    cv = din("cv", [16, 2048, DATT])
    sca = din("sca", [16, 30, DCONV])
    scf = din("scf", [16, 2, DFF])
    win = din("win", [D, DIN])
    wout = din("wout", [D, D])
    wg = din("wg", [D, DFF])
    wu = din("wu", [D, DFF])
    wd = din("wd", [DFF, D])
    gmix = din("gmix", [128, 8])
    gffn = din("gffn", [128, 8])
    caw = din("caw", [128, 2 * 31])
    cab = din("cab", [128, 2])
    lng = din("lng", [128, 2])
    lnb = din("lnb", [128, 2])
    fcw = din("fcw", [128, NFF * 3])
    fcb = din("fcb", [128, NFF])
    gfin = din("gfin", [128, D])
    m2d = din("m2", [128, 1024])
    maskrd = din("maskr", [128, 4 * 32])
    lw32d = din("lw32", [128, 32])
    lw3d = din("lw3", [128, 32])
    identd = din("ident", [128, 128])
    flagsd = din("flags", [128, 8])
    smaskd = din("smask", [128, 16 * 96])

    y = dout("y", [NMAIN * TT, D])
    ys = dout("ys", [128, D])
    kwin = dout("kwin", [2048, DATT])
    vwin = dout("vwin", [2048, DATT])
    cap = dout("cap", [30, DCONV])
    cfp = dout("cfp", [2, DFF])
    ksn = dout("ksn", [128, DATT])
    vsn = dout("vsn", [128, DATT])
    cas = dout("cas", [16, 30, DCONV])
    cfs = dout("cfs", [16, 2, DFF])
    xmid = nc.dram_tensor("xmid", [34 * 128, D], F32, kind="Internal").ap()

    top = ExitStack()
    P = Planner(nc, top)
    out_tickets = []

    def sb(stack, name, shape, dt):
        return stack.enter_context(nc.sbuf_tensor(name, list(shape), dt))

    def ps(stack, name, shape, dt):
        return stack.enter_context(nc.psum_tensor(name, list(shape), dt))

    xmid_res = [Res("xmid%d" % i) for i in range(34)]

    with ExitStack() as SA:
        win_sb = sb(SA, "win_sb", [128, 8, DIN], BF16)
        wouta = sb(SA, "wouta", [128, 2, D], BF16)
        wouto = sb(SA, "wouto", [128, 6, D], BF16)
        mbA = sb(SA, "mbA", [128, 512], BF16)
        mbB = sb(SA, "mbB", [128, 512], BF16)
        mbR = sb(SA, "mbR", [128, 4, 32], BF16)
        mbL32 = sb(SA, "mbL32", [128, 32], BF16)
        mbL3 = sb(SA, "mbL3", [128, 32], BF16)
        identb = sb(SA, "identb", [128, 128], BF16)
        onesf = sb(SA, "onesf", [128, 128], F32)
        ones256 = sb(SA, "ones256", [128, 128], F32)
        flagb = sb(SA, "flagb", [128, 8], BF16)
        gmix_sb = sb(SA, "gmix_sb", [128, 8], F32)
        caw_sb = sb(SA, "caw_sb", [128, 2, 31], F32)
        cab_sb = sb(SA, "cab_sb", [128, 2], F32)
        lng_sb = sb(SA, "lng_sb", [128, 2], F32)
        lnb_sb = sb(SA, "lnb_sb", [128, 2], F32)
        R_w = Res("weights1")
        R_c = Res("consts1")

        with ExitStack() as S0:
            stg = [sb(S0, "stg%d" % i, [128, DIN], F32) for i in range(2)]
            R_stg = [Res("stg0"), Res("stg1")]
            cst = sb(S0, "cst", [128, 1024], F32)
            cst2 = sb(S0, "cst2", [128, 1024], F32)
            R_cst2 = Res("cst2")
            R_cst = Res("cst")
            for dst, src, n in ((gmix_sb, gmix, 8), (cab_sb, cab, 2), (lng_sb, lng, 2), (lnb_sb, lnb, 2)):
                r = Res("v")
                P.dma(dst[:, :], src[:, :], [], [r, R_c], r)
            r = Res("v")
            P.dma(caw_sb[:, :, :].rearrange("p a b -> p (a b)"), caw[:, :], [], [r, R_c], r)
            offs = 0
            def load_bias(dst_ap, src_ap, c0, n):
                P.dma(cst[:, c0:c0 + n], src_ap, [], [R_cst], R_cst)
                P.op("dve", _mk("tensor_scalar", out=dst_ap, in0=cst[:, c0:c0 + n], scalar1=30000.0, scalar2=-30000.0,
                                op0=ALU.mult, op1=ALU.add), [R_cst], [R_c])
            load_bias(mbA[:, :], m2d[:, 0:512], 0, 512)
            load_bias(mbB[:, :], m2d[:, 512:1024], 512, 512)
            load_bias(mbR[:, :, :].rearrange("p a b -> p (a b)"), maskrd[:, :], 0, 128)
            load_bias(mbL32[:, :], lw32d[:, :], 128, 32)
            load_bias(mbL3[:, :], lw3d[:, :], 160, 32)
            P.dma(cst[:, 416:544], identd[:, :], [], [R_cst], R_cst)
            P.op("dve", _mk("tensor_copy", out=identb[:, :], in_=cst[:, 416:544]), [R_cst], [R_c])
            P.dma(cst[:, 544:552], flagsd[:, :], [], [R_cst], R_cst)
            P.op("dve", _mk("tensor_copy", out=flagb[:, :], in_=cst[:, 544:552]), [R_cst], [R_c])
            P.op("pool", _mk("memset", onesf[:, :], 1.0), [], [R_c])
            P.op("pool", _mk("memset", ones256[:, :], 1.0 / 256.0), [], [R_c])
            R_woq = Res("woq")
            P.dma(wouta[:, :, :], wout[0:256, :].rearrange("(a p) d -> p a d", p=128), [], [], R_woq, q="pool")
            for c in range(0, 6, 2):
                P.dma(wouto[:, c:c + 2, :], wout[256 + c * 128:256 + (c + 2) * 128, :].rearrange("(a p) d -> p a d", p=128),
                      [], [], R_woq, q="pool")
            for c in range(8):
                s = c % 2
                P.dma(stg[s][:, :], win[c * 128:(c + 1) * 128, :], [], [R_stg[s]], R_stg[s], q=("pool" if s == 1 else "sp"))
                if c % 2 == 0:
                    P.op("act", _mk("activation", out=win_sb[:, c, :], in_=stg[s][:, :], func=AF.Copy,
                                                                 scale=gmix_sb[:, c:c + 1]), [R_stg[s], R_c], [])
                else:
                    P.op("dve", _mk("tensor_scalar", out=win_sb[:, c, :], in0=stg[s][:, :],
                                                                    scalar1=gmix_sb[:, c:c + 1], scalar2=None,
                                                                    op0=ALU.mult), [R_stg[s], R_c], [])
            P.barrier()
            P.flush()

        with ExitStack() as S1:
            xt = [sb(S1, "xt%d" % i, [128, D], F32) for i in range(2)]
            R_xt = [Res("xt0"), Res("xt1")]
            hb = sb(S1, "hb", [128, D], BF16)
            R_hb = Res("hb")
            junk = hb
            R_junk = R_hb
            ssq = sb(S1, "ssq", [128, 8], F32)
            R_ssq = Res("ssq")
            hT = sb(S1, "hT", [128, 8, TT], BF16)
            R_hT = Res("hT")
            QT = sb(S1, "QT", [128, 6, TT], BF16)
            R_QT = Res("QT")
            KT = [sb(S1, "KT%d" % i, [128, 6, TT], BF16) for i in range(2)]
            R_KT = [Res("KT0"), Res("KT1")]
            kt_cur = {"c": 0}
            K16r = sb(S1, "K16r", [128, 6, 16, 128], BF16)
            R_K16r = Res("K16r")
            KS = sb(S1, "KS", [128, 16, 64], BF16)
            R_KS = Res("KS")
            VS = sb(S1, "VS", [128, 16, 64], BF16)
            R_VS = Res("VS")
            VTall = sb(S1, "VTall", [128, 2, 6, TT], BF16)
            VT = [VTall[:, 0], VTall[:, 1]]
            R_VT = [Res("VT0"), Res("VT1")]
            V16r = sb(S1, "V16r", [128, 6, 16 * 2 * 65], BF16)
            R_V16r = [Res("V16r%d" % i) for i in range(6)]
            V16c = sb(S1, "V16c", [128, 16, 2, 65], BF16)
            R_V16c = Res("V16c")
            va1 = sb(S1, "va1", [128, 5, 2, 65], BF16)
            R_va1 = Res("va1")
            va4 = sb(S1, "va4", [128, 8, 2, 65], BF16)
            R_va4 = Res("va4")
            pt = [sb(S1, "pt%d" % i, [128, 512], BF16) for i in range(3)]
            R_pt = [Res("pt%d" % i) for i in range(3)]
            glu = sb(S1, "glu", [128, 2, 30 + TT], F32)
            R_glu = Res("glu")
            cacc = sb(S1, "cacc", [128, 2, TT], F32)
            R_cacc = Res("cacc")
            lnA = sb(S1, "lnA", [128, TT], F32)
            R_lnA = Res("lnA")
            lnB = sb(S1, "lnB", [128, TT], F32)
            R_lnB = Res("lnB")
            aT = sb(S1, "aT", [128, 2, TT], BF16)
            R_aT = Res("aT")
            OT = sb(S1, "OT", [128, 6, TT], BF16)
            R_OT = [Res("OT%d" % i) for i in range(6)]
            mbRc = sb(S1, "mbRc", [128, 512], BF16)
            mbLc = sb(S1, "mbLc", [128, 512], BF16)
            R_mbc = Res("mbc")
            otmp = sb(S1, "otmp", [64, TT], BF16)
            R_otmp = Res("otmp")
            numsb = sb(S1, "numsb", [64, TT], F32)
            R_numsb = Res("numsb")
            rden = sb(S1, "rden", [65, TT], F32)
            R_rden = Res("rden")
            kvo = lnA
            R_kvo = R_lnA

            pj = [ps(S1, "pj%d" % i, [128, 512], F32) for i in range(2)]
            R_pj = [Res("pj0"), Res("pj1")]
            sc = [ps(S1, "sc%d" % i, [128, 512], F32) for i in range(3)]
            R_sc = [Res("sc0"), Res("sc1"), Res("sc2")]
            ac = [ps(S1, "ac%d" % i, [128, 512], F32) for i in range(2)]
            R_ac = [Res("ac0"), Res("ac1")]
            tr = [ps(S1, "tr%d" % i, [128, 1024], BF16) for i in range(1)]
            R_tr = [Res("tr0")]

            st = {"pj": 0, "sc": 0, "ac": 0, "tr": 0, "pt": 0, "xt": 0}

            def nxt(k, n):
                v = st[k]
                st[k] = (v + 1) % n
                return v

            P.op("pool", _mk("memset", K16r[:, :, :, :].rearrange("p a b c -> p (a b c)"), 0.0), [], [R_K16r])
            for q_ in range(2):
                P.op("pool", _mk("memset", KT[q_][:, :, :].rearrange("p a b -> p (a b)"), 0.0), [], [R_KT[q_]])
            P.op("pool", _mk("memset", KS[:, :, :].rearrange("p a b -> p (a b)"), 0.0), [], [R_KS])
            P.op("pool", _mk("memset", VS[:, :, :].rearrange("p a b -> p (a b)"), 0.0), [], [R_VS])
            P.op("pool", _mk("memset", V16r[:, :, :].rearrange("p a b -> p (a b)"), 0.0), [], R_V16r)
            P.op("pool", _mk("memset", glu[:, :, :].rearrange("p a b -> p (a b)"), 0.0), [], [R_glu])
            P.op("pool", _mk("memset", VTall[:, :, :, :].rearrange("p a b c -> p (a b c)"), 0.0), [], R_VT)
            P.op("pool", _mk("memset", V16c[:, :, :, :].rearrange("p a b c -> p (a b c)"), 0.0), [], [R_V16c])
            P.op("pool", _mk("memset", va1[:, :, :, :].rearrange("p a b c -> p (a b c)"), 0.0), [], [R_va1])
            P.op("pool", _mk("memset", va4[:, :, :, :].rearrange("p a b c -> p (a b c)"), 0.0), [], [R_va4])

            def norm_transpose(src_rows_ap, j, ntok_cols=TT, hT_=None, R_hT_=None):
                hT_ = hT if hT_ is None else hT_
                R_hT_ = R_hT if R_hT_ is None else R_hT_
                b = nxt("xt", 2)
                P.dma(xt[b][:, :], src_rows_ap, [], [R_xt[b]], R_xt[b])
                P.op("act", _mk("activation", out=junk[:, :], in_=xt[b][:, :], func=AF.Square,
                                                   accum_out=ssq[:, 0:1]), [R_xt[b]], [R_junk, R_ssq])
                P.op("dve", _mk("tensor_scalar", out=ssq[:, 1:2], in0=ssq[:, 0:1], scalar1=1.0 / D, scalar2=EPS,
                                                      op0=ALU.mult, op1=ALU.add), [R_ssq], [R_ssq])
                P.op("act", _mk("activation", out=ssq[:, 3:4], in_=ssq[:, 1:2], func=AF.Ln), [R_ssq], [R_ssq])
                P.op("act", _mk("activation", out=ssq[:, 2:3], in_=ssq[:, 3:4], func=AF.Exp, scale=-0.5), [R_ssq], [R_ssq])
                P.op("act", _mk("activation", out=hb[:, :], in_=xt[b][:, :], func=AF.Copy, scale=ssq[:, 2:3]),
                     [R_xt[b], R_ssq], [R_hb])
                t = nxt("tr", 1)
                for c in range(8):
                    P.op("pe", _mk("transpose", out=tr[t][:, c * 128:(c + 1) * 128],
                                                          in_=hb[:, c * 128:(c + 1) * 128], identity=identb[:, :]),
                         [R_hb, R_c], [R_tr[t]], inc=(c == 7))
                P.op("dve", _mk("tensor_copy", out=hT_[:, :, 128 * j:128 * j + 128],
                                                    in_=tr[t][:, :].rearrange("p (c k) -> p c k", c=8)),
                     [R_tr[t]], [R_hT_])
                return b

            def proj_fm(fchunk, ncols=TT):
                pb = nxt("pj", 2)
                for c in range(8):
                    P.op("pe", _mk("matmul", pj[pb][:, 0:ncols], win_sb[:, c, fchunk * 128:(fchunk + 1) * 128],
                                                       hT[:, c, 0:ncols], start=(c == 0), stop=(c == 7)),
                         [R_w, R_hT], [R_pj[pb]], inc=(c == 7))
                return pb

            def evac(pb, dst_ap, R_dst, which):
                if which == "act":
                    P.op("act", _mk("activation", out=dst_ap, in_=pj[pb][:, 0:dst_ap.shape[-1]], func=AF.Copy),
                         [R_pj[pb]], [R_dst])
                else:
                    P.op("dve", _mk("tensor_copy", out=dst_ap, in_=pj[pb][:, 0:dst_ap.shape[-1]]),
                         [R_pj[pb]], [R_dst])

            def kv_project(par):
                for hp in range(6):
                    pb = proj_fm(10 + hp)
                    evac(pb, KT[kt_cur["c"]][:, hp, :], R_KT[kt_cur["c"]], "act" if hp % 2 == 0 else "dve")
                for hp in range(6):
                    pb = proj_fm(16 + hp)
                    evac(pb, VT[par][:, hp, :], R_VT[par], "dve" if hp % 2 == 0 else "act")

            def v16_build(par, hp, j):
                js = slice(64, 128) if j == 3 else slice(32 * j, 32 * j + 32)
                if j == 3:
                    P.op("pool", _mk("tensor_copy", out=VS[:, :, 0:32],
                                     in_=VT[1 - par][:, hp, :].rearrange("p (u r) -> p r u", r=16)), [R_VT[1 - par]], [R_VS])
                    P.op("pool", _mk("tensor_copy", out=VS[:, :, 32:64],
                                     in_=VT[par][:, hp, :].rearrange("p (u r) -> p r u", r=16)), [R_VT[par]], [R_VS])
                for half in range(2):
                    t = nxt("tr", 1)
                    for rr in range(8):
                        r = half * 8 + rr
                        if j == 3:
                            src_ = VS[:, r, :]
                            rds = [R_VS, R_c]
                        else:
                            src_ = VT[par][:, hp, r::16]
                            rds = [R_VT[par], R_c]
                        P.op("pe", _mk("transpose", out=tr[t][js, rr * 128:(rr + 1) * 128], in_=src_, identity=identb[:, :]),
                             rds, [R_tr[t]], inc=(rr == 7))
                    P.op("dve", _mk("tensor_copy",
                        out=V16c[js, half * 8:half * 8 + 8, :, 0:64],
                        in_=tr[t][js, :].rearrange("p (r h d) -> p r h d", r=8, h=2)),
                        [R_tr[t]], [R_V16c])

            def ring_insert_v(hp, j):
                js = slice(64, 128) if j == 3 else slice(32 * j, 32 * j + 32)
                P.op("act", _mk("activation", out=V16r[js, hp, :],
                                in_=V16c[js, :, :, :].rearrange("p a b c -> p (a b c)"), func=AF.Copy),
                     [R_V16c], [R_V16r[hp]])

            def set_aug(i):
                fl = slice(0, 2)
                on = slice(4, 6)
                src = flagb[:, fl] if i < 0 else flagb[:, on]
                P.op("pool", _mk("tensor_copy", out=V16c[:, :, :, 64:65],
                                                     in_=src.unsqueeze(1).unsqueeze(3).broadcast_to([128, 16, 2, 1])),
                     [R_c], [R_V16c])
                if i >= -1:
                    s_prev = flagb[:, fl] if i <= 0 else flagb[:, on]
                    s_cur = flagb[:, fl] if i < 0 else flagb[:, on]
                    P.op("pool", _mk("tensor_copy", out=va1[:, 0:1, :, 64:65],
                                                         in_=s_prev.unsqueeze(1).unsqueeze(3)), [R_c], [R_va1])
                    P.op("pool", _mk("tensor_copy", out=va1[:, 1:5, :, 64:65],
                                                         in_=s_cur.unsqueeze(1).unsqueeze(3).broadcast_to([128, 4, 2, 1])),
                         [R_c], [R_va1])
                    P.op("pool", _mk("tensor_copy", out=va4[:, 0:4, :, 64:65],
                                                         in_=s_prev.unsqueeze(1).unsqueeze(3).broadcast_to([128, 4, 2, 1])),
                         [R_c], [R_va4])
                    P.op("pool", _mk("tensor_copy", out=va4[:, 4:8, :, 64:65],
                                                         in_=s_cur.unsqueeze(1).unsqueeze(3).broadcast_to([128, 4, 2, 1])),
                         [R_c], [R_va4])

            def exp_mask(sb_, ncol0, ncol1, mask_ap, prow=slice(0, 128)):
                pi = nxt("pt", 3)
                P.op("act", _mk("activation", out=pt[pi][prow, ncol0:ncol1], in_=sc[sb_][prow, ncol0:ncol1],
                                                   func=AF.Exp, scale=SCALE), [R_sc[sb_]], [R_pt[pi]])
                pv_ = pt[pi][prow, ncol0:ncol1]
                if mask_ap.ndim == 3:
                    pv_ = pv_.rearrange("p (a b) -> p a b", a=mask_ap.shape[1])
                P.op("pool", _mk("tensor_tensor", out=pv_, in0=pv_, in1=mask_ap, op=ALU.mult), [R_pt[pi], R_c], [R_pt[pi]])
                return pi

            def attention(i, par, hooks=None):
                hooks = hooks or {}
                j = i % 4
                ppar = 1 - par
                js = slice(64, 128) if j == 3 else slice(32 * j, 32 * j + 32)
                nk = 64 if j == 3 else 32
                DEPTH = 2
                steps = []
                P.op("pool", _mk("tensor_copy", out=mbRc[:, :].rearrange("p (a b) -> p a b", a=16),
                                 in_=mbR[:, j, :].unsqueeze(1).broadcast_to([128, 16, 32])), [R_c], [R_mbc])
                mb__ = mbL3 if j == 3 else mbL32
                P.op("pool", _mk("tensor_copy", out=mbLc[:, :].rearrange("p (a b) -> p a b", a=16),
                                 in_=mb__[:, :].unsqueeze(1).broadcast_to([128, 16, 32])), [R_c], [R_mbc])

                def build_v(hp):
                    t = nxt("tr", 1)
                    for bi in range(5):
                        src = VT[ppar][:, hp, 384:512] if bi == 0 else VT[par][:, hp, 128 * (bi - 1):128 * bi]
                        rr_ = [R_VT[ppar]] if bi == 0 else [R_VT[par]]
                        P.op("pe", _mk("transpose", out=tr[t][:, bi * 128:(bi + 1) * 128], in_=src, identity=identb[:, :]),
                             rr_ + [R_c], [R_tr[t]], inc=(bi == 4))
                    P.op("act", _mk("activation", out=va1[:, :, :, 0:64],
                                    in_=tr[t][:, 0:640].rearrange("p (b h d) -> p b h d", b=5, h=2), func=AF.Copy),
                         [R_tr[t]], [R_va1])
                    t = nxt("tr", 1)
                    for bi in range(8):
                        r = bi % 4
                        src = VT[ppar][:, hp, r::4] if bi < 4 else VT[par][:, hp, r::4]
                        rr_ = [R_VT[ppar]] if bi < 4 else [R_VT[par]]
                        P.op("pe", _mk("transpose", out=tr[t][:, bi * 128:(bi + 1) * 128], in_=src, identity=identb[:, :]),
                             rr_ + [R_c], [R_tr[t]], inc=(bi == 7))
                    P.op("act", _mk("activation", out=va4[:, :, :, 0:64],
                                    in_=tr[t][:, :].rearrange("p (b h d) -> p b h d", b=8, h=2), func=AF.Copy),
                         [R_tr[t]], [R_va4])
                    v16_build(par, hp, j)

                def add_head(hp, hh):
                    hs = slice(64 * hh, 64 * hh + 64)
                    a_box = [None]
                    started = [False]

                    def pv(lhsT, rhs, out_ap, reads, last=False, tp=None):
                        stt = not started[0]
                        started[0] = True
                        a = a_box[0]
                        P.op("pe", _mk("matmul", out_ap, lhsT, rhs, start=stt, stop=True, skip_group_check=True,
                                       tile_position=tp), reads, [R_ac[a]], inc=last)

                    def maskmm(s, bias_ap, rows=slice(0, 128)):
                        tp = None if rows.start == 0 and rows.stop == 128 else (rows.start, rows.start)
                        P.op("pe", _mk("matmul", sc[s][rows, 0:512], identb[rows, rows], bias_ap, start=True, stop=False,
                                       skip_group_check=True, tile_position=tp), [R_c, R_mbc], [R_sc[s]], inc=False)

                    def qkmm(s, cols, kap, qap, reads, last, rows=slice(0, 128), tp=None):
                        P.op("pe", _mk("matmul", sc[s][rows, cols], kap, qap, start=False, stop=True, skip_group_check=True,
                                       tile_position=tp), reads, [R_sc[s]], inc=last)

                    def expo(s, pi, rows=slice(0, 128)):
                        P.op("act", _mk("activation", out=pt[pi][rows, :], in_=sc[s][rows, 0:512], func=AF.Exp, scale=SCALE),
                             [R_sc[s]], [R_pt[pi]])

                    def qk1(s):
                        maskmm(s, mbA[:, :])
                        qkmm(s, slice(0, 128), KT[1 - kt_cur["c"]][hs, hp, 384:512], QT[hs, hp, 0:128], [R_KT[1 - kt_cur["c"]], R_QT], False)
                        qkmm(s, slice(128, 384), KT[kt_cur["c"]][hs, hp, 0:128], QT[hs, hp, 0:256], [R_KT[kt_cur["c"]], R_QT], False)
                        qkmm(s, slice(384, 512), KT[kt_cur["c"]][hs, hp, 384:512], QT[hs, hp, 384:512], [R_KT[kt_cur["c"]], R_QT], True)

                    def rest1(s, pi):
                        a_box[0] = nxt("ac", 2)
                        a = a_box[0]
                        expo(s, pi)
                        rd = [R_va1, R_pt[pi]]
                        pv(va1[:, 0, hh, :], pt[pi][:, 0:128], ac[a][0:65, 0:128], rd)
                        pv(va1[:, 1, hh, :], pt[pi][:, 128:256], ac[a][0:65, 0:128], rd)
                        pv(va1[:, 1, hh, :], pt[pi][:, 256:384], ac[a][0:65, 128:256], rd)
                        pv(va1[:, 4, hh, :], pt[pi][:, 384:512], ac[a][0:65, 384:512], rd, last=True)

                    def qk2(s):
                        maskmm(s, mbB[:, :])
                        qkmm(s, slice(0, 256), KT[kt_cur["c"]][hs, hp, 128:256], QT[hs, hp, 128:384], [R_KT[kt_cur["c"]], R_QT], False)
                        qkmm(s, slice(256, 512), KT[kt_cur["c"]][hs, hp, 256:384], QT[hs, hp, 256:512], [R_KT[kt_cur["c"]], R_QT], True)

                    def rest2(s, pi):
                        a = a_box[0]
                        expo(s, pi)
                        rd = [R_va1, R_pt[pi]]
                        pv(va1[:, 2, hh, :], pt[pi][:, 0:128], ac[a][0:65, 128:256], rd)
                        pv(va1[:, 2, hh, :], pt[pi][:, 128:256], ac[a][0:65, 256:384], rd)
                        pv(va1[:, 3, hh, :], pt[pi][:, 256:384], ac[a][0:65, 256:384], rd)
                        pv(va1[:, 3, hh, :], pt[pi][:, 384:512], ac[a][0:65, 384:512], rd, last=True)

                    def mk4(r0):
                        def qk(s):
                            maskmm(s, mbB[:, :])
                            for q_, r in enumerate((r0, r0 + 1)):
                                qkmm(s, slice(256 * q_, 256 * q_ + 128), KT[kt_cur["c"]][hs, hp, r::4], QT[hs, hp, r::4], [R_KT[kt_cur["c"]], R_QT], False)
                                qkmm(s, slice(256 * q_ + 128, 256 * q_ + 256), KT[1 - kt_cur["c"]][hs, hp, r::4], QT[hs, hp, r::4],
                                     [R_KT[1 - kt_cur["c"]], R_QT], q_ == 1)

                        def rest(s, pi):
                            a = a_box[0]
                            expo(s, pi)
                            rd = [R_va4, R_pt[pi]]
                            for q_, r in enumerate((r0, r0 + 1)):
                                pv(va4[:, 4 + r, hh, :], pt[pi][:, 256 * q_:256 * q_ + 128], ac[a][0:65, r::4], rd)
                                pv(va4[:, r, hh, :], pt[pi][:, 256 * q_ + 128:256 * q_ + 256], ac[a][0:65, r::4], rd, last=(q_ == 1))
                        return qk, rest

                    def qk5(s):
                        maskmm(s, mbRc[:, :])
                        for r in range(16):
                            qkmm(s, slice(32 * r, 32 * r + 32), K16r[hs, hp, r, :], QT[hs, hp, r::16], [R_K16r, R_QT], r == 15)

                    def rest5(s, pi):
                        a = a_box[0]
                        expo(s, pi)
                        v16 = V16r[:, hp, :].rearrange("p (r h d) -> p r h d", r=16, h=2)
                        for r in range(16):
                            pv(v16[:, r, hh, :], pt[pi][:, 32 * r:32 * r + 32], ac[a][0:65, r::16], [R_V16r[hp], R_pt[pi]],
                               last=(r == 15))

                    def qk6(s):
                        if j == 3 and hh == 0:
                            P.op("pool", _mk("tensor_copy", out=KS[:, :, 32:64],
                                             in_=KT[kt_cur["c"]][:, hp, :].rearrange("p (u r) -> p r u", r=16)), [R_KT[kt_cur["c"]]], [R_KS])
                        P.op("pe", _mk("matmul", sc[s][js, 0:512], identb[hs, 64 * hh:64 * hh + nk], mbLc[hs, :], start=True,
                                       stop=False, skip_group_check=True, tile_position=(64 * hh, js.start)),
                             [R_c, R_mbc], [R_sc[s]], inc=False)
                        for r in range(16):
                            kap_ = KS[hs, r, :] if j == 3 else KT[kt_cur["c"]][hs, hp, r::16]
                            qkmm(s, slice(32 * r, 32 * r + 32), kap_, QT[hs, hp, r::16], [R_KT[kt_cur["c"]], R_KS, R_QT], r == 15,
                                 rows=js, tp=(64 * hh, js.start))

                    def rest6(s, pi):
                        a = a_box[0]
                        expo(s, pi, rows=js)
                        for r in range(16):
                            pv(V16c[js, r, hh, :], pt[pi][js, 32 * r:32 * r + 32], ac[a][0:65, r::16], [R_V16c, R_pt[pi]],
                               last=(r == 15), tp=(js.start, 0))

                    def norm_a():
                        a = a_box[0]
                        P.op("act", _mk("activation", out=numsb[:, :], in_=ac[a][0:64, :], func=AF.Copy), [R_ac[a]], [R_numsb])
                        P.op("dve", _mk("tensor_scalar", out=rden[64:65, :], in0=ac[a][64:65, :], scalar1=1e-18,
                                        scalar2=None, op0=ALU.add), [R_ac[a]], [R_rden])
                        P.op("act", _mk("activation", out=rden[64:65, :], in_=rden[64:65, :], func=AF.Ln), [R_rden], [R_rden])
                        P.op("act", _mk("activation", out=rden[64:65, :], in_=rden[64:65, :], func=AF.Exp, scale=-1.0),
                             [R_rden], [R_rden])
                        if hh == 1:
                            ring_insert_v(hp, j)

                    def norm_b():
                        pb = nxt("pj", 2)
                        P.op("pe", _mk("matmul", pj[pb][0:64, :], onesf[64:65, 0:64], rden[64:65, :], start=True, stop=True),
                             [R_rden, R_c], [R_pj[pb]])
                        if hh == 0:
                            P.op("dve", _mk("tensor_tensor", out=OT[0:64, hp, :], in0=numsb[:, :], in1=pj[pb][0:64, :],
                                            op=ALU.mult), [R_numsb, R_pj[pb]], [R_OT[hp]])
                        else:
                            P.op("dve", _mk("tensor_tensor", out=otmp[:, :], in0=numsb[:, :], in1=pj[pb][0:64, :],
                                            op=ALU.mult), [R_numsb, R_pj[pb]], [R_otmp])
                            P.dma(OT[64:128, hp, :], otmp[:, :], [R_otmp], [R_OT[hp]], R_otmp)

                    q3, r3 = mk4(0)
                    q4, r4 = mk4(2)
                    pre = (lambda: build_v(hp)) if hh == 0 else None
                    steps.append(dict(qk=qk1, rest=rest1, pre=pre, post=None, post2=None))
                    steps.append(dict(qk=qk2, rest=rest2, pre=None, post=None, post2=None))
                    steps.append(dict(qk=q3, rest=r3, pre=None, post=None, post2=None))
                    steps.append(dict(qk=q4, rest=r4, pre=None, post=None, post2=None))
                    steps.append(dict(qk=qk5, rest=rest5, pre=None, post=None, post2=None))
                    steps.append(dict(qk=qk6, rest=rest6, pre=None, post=norm_a, post2=norm_b))

                for hp in range(6):
                    for hh in range(2):
                        add_head(hp, hh)
                N = len(steps)
                banks = [None] * N
                deferred = {}
                for n in range(N + DEPTH + 4):
                    if n < N:
                        banks[n] = nxt("sc", 3)
                        steps[n]["qk"](banks[n])
                    m = n - DEPTH
                    if 0 <= m < N:
                        stp = steps[m]
                        if stp["pre"] is not None:
                            stp["pre"]()
                        pi = nxt("pt", 3)
                        stp["rest"](banks[m], pi)
                        if stp["post"] is not None:
                            stp["post"]()
                            deferred[m + 3] = stp["post2"]
                    if m in deferred:
                        deferred.pop(m)()
                    for h_ in hooks.pop(n, []):
                        h_()
                assert not deferred and not hooks

            def conv_a(i):
                for f in range(2):
                    pbg = proj_fm(2 + f)
                    P.op("act", _mk("activation", out=lnB[:, :], in_=pj[pbg][:, :], func=AF.Sigmoid),
                         [R_pj[pbg]], [R_lnB])
                    pbv = proj_fm(f)
                    P.op("dve", _mk("tensor_tensor", out=glu[:, f, 30:30 + TT], in0=pj[pbv][:, :], in1=lnB[:, :],
                                                          op=ALU.mult), [R_pj[pbv], R_lnB], [R_glu])

            def conv_taps(f, k0, k1):
                for k in range(k0, k1):
                    if k == 0:
                        P.op("dve", _mk("tensor_scalar", out=cacc[:, f, :], in0=glu[:, f, 0:TT], scalar1=caw_sb[:, f, 0:1],
                                        scalar2=cab_sb[:, f:f + 1], op0=ALU.mult, op1=ALU.add), [R_glu, R_c], [R_cacc])
                    else:
                        P.op("dve", _mk("scalar_tensor_tensor", out=cacc[:, f, :], in0=glu[:, f, k:k + TT],
                                        scalar=caw_sb[:, f, k:k + 1], in1=cacc[:, f, :], op0=ALU.mult, op1=ALU.add),
                             [R_glu, R_c, R_cacc], [R_cacc])

            def conv_hist():
                P.op("pool", _mk("tensor_copy", out=glu[:, :, 0:30], in_=glu[:, :, TT:TT + 30]), [R_glu], [R_glu])

            def conv_b(i):
                pm = nxt("pj", 2)
                for f in range(2):
                    P.op("pe", _mk("matmul", pj[pm][:, :], ones256[:, :], cacc[:, f, :], start=(f == 0), stop=(f == 1)),
                         [R_cacc, R_c], [R_pj[pm]], inc=(f == 1))
                P.op("act", _mk("activation", out=lnA[:, :], in_=cacc[:, 0, :], func=AF.Square), [R_cacc], [R_lnA])
                P.op("act", _mk("activation", out=lnB[:, :], in_=cacc[:, 1, :], func=AF.Square), [R_cacc], [R_lnB])
                pq = nxt("pj", 2)
                P.op("pe", _mk("matmul", pj[pq][:, :], ones256[:, :], lnA[:, :], start=True, stop=False),
                     [R_lnA, R_c], [R_pj[pq]], inc=False)
                P.op("pe", _mk("matmul", pj[pq][:, :], ones256[:, :], lnB[:, :], start=False, stop=True),
                     [R_lnB, R_c], [R_pj[pq]])
                P.op("act", _mk("activation", out=lnA[:, :], in_=pj[pm][:, :], func=AF.Square), [R_pj[pm]], [R_lnA])
                P.op("dve", _mk("tensor_tensor", out=lnA[:, :], in0=pj[pq][:, :], in1=lnA[:, :], op=ALU.subtract),
                     [R_pj[pq], R_lnA], [R_lnA])
                P.op("dve", _mk("tensor_scalar", out=lnA[:, :], in0=lnA[:, :], scalar1=EPS, scalar2=None, op0=ALU.add),
                     [R_lnA], [R_lnA])
                P.op("act", _mk("activation", out=lnA[:, :], in_=lnA[:, :], func=AF.Ln), [R_lnA], [R_lnA])
                P.op("act", _mk("activation", out=lnA[:, :], in_=lnA[:, :], func=AF.Exp, scale=-0.5), [R_lnA], [R_lnA])
                for f in range(2):
                    P.op("dve", _mk("tensor_tensor", out=lnB[:, :], in0=cacc[:, f, :], in1=pj[pm][:, :], op=ALU.subtract),
                         [R_cacc, R_pj[pm]], [R_lnB])
                    P.op("dve", _mk("tensor_tensor", out=lnB[:, :], in0=lnB[:, :], in1=lnA[:, :], op=ALU.mult),
                         [R_lnB, R_lnA], [R_lnB])
                    P.op("act", _mk("activation", out=aT[:, f, :], in_=lnB[:, :], func=AF.Silu,
                                                       scale=lng_sb[:, f:f + 1], bias=lnb_sb[:, f:f + 1]),
                         [R_lnB, R_c], [R_aT])

            def out_proj(i):
                blocks = [3] if i == -1 else [0, 1, 2, 3]
                for jb in blocks:
                    b = nxt("xt", 2)
                    row0 = (i + NHALO + NPRE) * TT + 128 * jb
                    P.dma(xt[b][:, :], xp[row0:row0 + 128, :], [], [R_xt[b]], R_xt[b])
                    for nh in range(2):
                        pb = nxt("pj", 2)
                        for f in range(2):
                            P.op("pe", _mk("matmul", pj[pb][:, :], aT[:, f, 128 * jb:128 * jb + 128],
                                                               wouta[:, f, 512 * nh:512 * nh + 512], start=(f == 0), stop=False),
                                 [R_aT, R_w], [R_pj[pb]], inc=False)
                        for hp in range(6):
                            P.op("pe", _mk("matmul", pj[pb][:, :], OT[:, hp, 128 * jb:128 * jb + 128],
                                                                 wouto[:, hp, 512 * nh:512 * nh + 512], start=False,
                                                                 stop=(hp == 5)),
                                 [R_OT[hp], R_w], [R_pj[pb]], inc=(hp == 5))
                        P.op("dve", _mk("tensor_tensor", out=xt[b][:, 512 * nh:512 * nh + 512],
                                                              in0=xt[b][:, 512 * nh:512 * nh + 512], in1=pj[pb][:, :],
                                                              op=ALU.add), [R_xt[b], R_pj[pb]], [R_xt[b]])
                    blk = 0 if i == -1 else 1 + 4 * i + jb
                    P.dma(xmid[blk * 128:(blk + 1) * 128, :], xt[b][:, :], [R_xt[b]], [xmid_res[blk]], R_xt[b])

            def kv_outputs(i):
                for jb in range(4):
                    row0 = (i - 4) * TT + 128 * jb
                    for g in range(3):
                        pb = nxt("pj", 2)
                        for c in range(8):
                            P.op("pe", _mk("matmul", pj[pb][:, :], hT[:, c, 128 * jb:128 * jb + 128],
                                                               win_sb[:, c, 1280 + 512 * g:1280 + 512 * g + 512],
                                                               start=(c == 0), stop=(c == 7)),
                                 [R_hT, R_w], [R_pj[pb]], inc=(c == 7))
                        P.op("act", _mk("activation", out=kvo[:, :], in_=pj[pb][:, :], func=AF.Copy), [R_pj[pb]], [R_kvo])
                        if g == 0:
                            out_tickets.append(P.dma(kwin[row0:row0 + 128, 0:512], kvo[:, :], [R_kvo], [], R_kvo))
                        elif g == 1:
                            out_tickets.append(P.dma(kwin[row0:row0 + 128, 512:768], kvo[:, 0:256], [R_kvo], [], R_kvo))
                            out_tickets.append(P.dma(vwin[row0:row0 + 128, 0:256], kvo[:, 256:512], [R_kvo], [], R_kvo))
                        else:
                            out_tickets.append(P.dma(vwin[row0:row0 + 128, 256:768], kvo[:, :], [R_kvo], [], R_kvo))

            def conv_a_output():
                pb = nxt("pj", 2)
                for c in range(8):
                    P.op("pe", _mk("matmul", pj[pb][:, :], hT[:, c, 384:512], win_sb[:, c, 0:512],
                                                       start=(c == 0), stop=(c == 7)), [R_hT, R_w], [R_pj[pb]], inc=(c == 7))
                P.op("act", _mk("activation", out=kvo[:, 256:512], in_=pj[pb][:, 256:512], func=AF.Sigmoid),
                     [R_pj[pb]], [R_kvo])
                P.op("dve", _mk("tensor_tensor", out=kvo[:, 0:256], in0=pj[pb][:, 0:256], in1=kvo[:, 256:512], op=ALU.mult),
                     [R_pj[pb], R_kvo], [R_kvo])
                out_tickets.append(P.dma(cap[:, :], kvo[98:128, 0:256], [R_kvo], [], R_kvo))

            def prep_block(ti_, jb):
                norm_transpose(xp[ti_ * TT + 128 * jb:ti_ * TT + 128 * jb + 128, :], jb)

            for jb in range(4):
                prep_block(0, jb)
            for ti in range(NT):
                i = ti - (NHALO + NPRE)
                par = (ti + 1) % 2
                kt_cur["c"] = par
                j = i % 4
                if i in (-(NHALO + NPRE), -1, 0, 1):
                    set_aug(i)
                if i < -1:
                    kv_project(par)
                    for jb in range(4):
                        prep_block(ti + 1, jb)
                    for hp in range(6):
                        v16_build(par, hp, j)
                        ring_insert_v(hp, j)
                else:
                    conv_a(i)
                    for hp in range(6):
                        pb = proj_fm(4 + hp)
                        evac(pb, QT[:, hp, :], R_QT, "act" if hp % 2 == 0 else "dve")
                    kv_project(par)
                    if i >= 4:
                        kv_outputs(i)
                    if i == NMAIN - 1:
                        conv_a_output()
                    hooks = {}
                    hk = 1
                    for f in range(2):
                        for k0 in range(0, 31, 8):
                            hooks.setdefault(hk, []).append(lambda f=f, k0=k0: conv_taps(f, k0, min(31, k0 + 8)))
                            hk += 2
                    hooks.setdefault(hk, []).append(conv_hist)
                    hooks.setdefault(hk + 1, []).append(lambda i=i: conv_b(i))
                    if ti + 1 < NT:
                        for jb in range(4):
                            hooks.setdefault(24 + 12 * jb, []).append(lambda ti=ti, jb=jb: prep_block(ti + 1, jb))
                    attention(i, par, hooks)
                    out_proj(i)
                for hp in range(6):
                    P.op("pool", _mk("tensor_copy", out=K16r[:, hp, :, 32 * j:32 * j + 32],
                                     in_=KT[kt_cur["c"]][:, hp, :].rearrange("p (u r) -> p r u", r=16)), [R_KT[kt_cur["c"]]], [R_K16r])
            P.barrier()
            P.flush()

        with ExitStack() as SS:
            xts = sb(SS, "xts", [128, D], F32)
            R_xts = Res("xts")
            hbs = sb(SS, "hbs", [128, D], BF16)
            R_hbs = Res("hbs")
            sqs = sb(SS, "sqs", [128, 8], F32)
            R_sqs = Res("sqs")
            hTs = sb(SS, "hTs", [128, 8, 128], BF16)
            R_hTs = Res("hTs")
            tmS = sb(SS, "tmS", [128, DIN], F32)
            R_tmS = Res("tmS")
            VN = sb(SS, "VN", [128, DATT], BF16)
            R_VN = Res("VN")
            gluS = sb(SS, "gluS", [128, 512], F32)
            R_gluS = Res("gluS")
            gluSF = sb(SS, "gluSF", [128, 2, 16, 38], F32)
            R_gluSF = Res("gluSF")
            scaS = sb(SS, "scaS", [128, 4, DCONV], F32)
            R_scaS = Res("scaS")
            caccS = sb(SS, "caccS", [128, 2, 128], F32)
            R_caccS = Res("caccS")
            csqS = sb(SS, "csqS", [128, 2, 128], F32)
            R_csqS = Res("csqS")
            lnAs = sb(SS, "lnAs", [128, 128], F32)
            R_lnAs = Res("lnAs")
            lnBs = sb(SS, "lnBs", [128, 128], F32)
            R_lnBs = Res("lnBs")
            aTs = sb(SS, "aTs", [128, 2, 128], BF16)
            R_aTs = Res("aTs")
            Qz = sb(SS, "Qz", [128, 6, 16, 2, 8], BF16)
            R_Qz = Res("Qz")
            KNT = sb(SS, "KNT", [128, 6, 128], BF16)
            R_KNT = Res("KNT")
            KA = [sb(SS, "KA%d" % i, [128, 8, DATT], BF16) for i in range(2)]
            KB = [sb(SS, "KB%d" % i, [128, 4, DATT], BF16) for i in range(2)]
            VA = [sb(SS, "VA%d" % i, [128, 8, DATT], BF16) for i in range(2)]
            VB = [sb(SS, "VB%d" % i, [128, 4, DATT], BF16) for i in range(2)]
            R_KA = [Res("KA0"), Res("KA1")]
            R_KB = [Res("KB0"), Res("KB1")]
            R_VA = [Res("VA0"), Res("VA1")]
            R_VB = [Res("VB0"), Res("VB1")]
            KTs = [sb(SS, "KTs%d" % i, [128, 12, 128], BF16) for i in range(2)]
            R_KTs = [Res("KTs0"), Res("KTs1")]
            pts = [sb(SS, "pts%d" % i, [128, 96], BF16) for i in range(2)]
            R_pts = [Res("pts0"), Res("pts1")]
            smf = sb(SS, "smf", [128, 16 * 96], F32)
            R_smf = Res("smf")
            smb = sb(SS, "smb", [128, 16, 96], BF16)
            onesb = sb(SS, "onesb", [128, 128], BF16)
            identf = sb(SS, "identf", [128, 128], F32)
            R_cs = Res("consts_s")
            OTs = sb(SS, "OTs", [128, 6, 128], BF16)
            R_OTs = Res("OTs")
            tmpo = sb(SS, "tmpo", [128, 96], F32)
            R_tmpo = Res("tmpo")
            rds = sb(SS, "rds", [128, 96], F32)
            R_rds = Res("rds")

            pjs = [ps(SS, "pjs%d" % i, [128, 512], F32) for i in range(2)]
            R_pjs = [Res("pjs0"), Res("pjs1")]
            trs = [ps(SS, "trs%d" % i, [128, 1024], BF16) for i in range(2)]
            R_trs = [Res("trs0"), Res("trs1")]
            scs = [ps(SS, "scs%d" % i, [128, 512], F32) for i in range(2)]
            R_scs = [Res("scs0"), Res("scs1")]
            nums = ps(SS, "nums", [128, 512], F32)
            R_nums = Res("nums")
            dens = ps(SS, "dens", [128, 512], F32)
            R_dens = Res("dens")
            sts = {"pj": 0, "tr": 0, "sc": 0, "pt": 0, "kt": 0}

            def nxs(k, n=2):
                v = sts[k]
                sts[k] = (v + 1) % n
                return v

            P.dma(smf[:, :], smaskd[:, :], [], [R_smf], R_smf)
            P.op("dve", _mk("tensor_copy", out=smb[:, :, :].rearrange("p a b -> p (a b)"), in_=smf[:, :]), [R_smf], [R_cs])
            P.op("pool", _mk("memset", onesb[:, :], 1.0), [], [R_cs])
            P.dma(identf[:, :], identd[:, :], [], [R_cs], R_cs)
            P.op("pool", _mk("memset", Qz[:, :, :, :, :].rearrange("p a b c d -> p (a b c d)"), 0.0), [], [R_Qz])

            def load_cache(s):
                b = s % 2
                ka = ck[s].rearrange("(g q) f -> g q f", q=16)
                va = cv[s].rearrange("(g q) f -> g q f", q=16)
                P.dma(KA[b][:, :, :], ka[:, 0:8, :], [], [R_KA[b]], R_KA[b], q="pool")
                P.dma(KB[b][:, :, :], ck[s, 1536:2048, :].rearrange("(m r) f -> m r f", r=4), [], [R_KB[b]], R_KB[b], q="pool")
                P.dma(VA[b][:, :, :], va[:, 0:8, :], [], [R_VA[b]], R_VA[b], q="pool")
                P.dma(VB[b][:, :, :], cv[s, 1536:2048, :].rearrange("(m r) f -> m r f", r=4), [], [R_VB[b]], R_VB[b], q="pool")

            load_cache(0)
            load_cache(1)

            P.dma(xts[:, :], xs[:, :], [], [R_xts], R_xts)
            P.op("act", _mk("activation", out=hbs[:, :], in_=xts[:, :], func=AF.Square, accum_out=sqs[:, 0:1]),
                 [R_xts], [R_hbs, R_sqs])
            P.op("dve", _mk("tensor_scalar", out=sqs[:, 1:2], in0=sqs[:, 0:1], scalar1=1.0 / D, scalar2=EPS,
                            op0=ALU.mult, op1=ALU.add), [R_sqs], [R_sqs])
            P.op("act", _mk("activation", out=sqs[:, 3:4], in_=sqs[:, 1:2], func=AF.Ln), [R_sqs], [R_sqs])
            P.op("act", _mk("activation", out=sqs[:, 2:3], in_=sqs[:, 3:4], func=AF.Exp, scale=-0.5), [R_sqs], [R_sqs])
            P.op("act", _mk("activation", out=hbs[:, :], in_=xts[:, :], func=AF.Copy, scale=sqs[:, 2:3]),
                 [R_xts, R_sqs], [R_hbs])
            t = nxs("tr")
            for c in range(8):
                P.op("pe", _mk("transpose", out=trs[t][:, c * 128:(c + 1) * 128], in_=hbs[:, c * 128:(c + 1) * 128],
                               identity=identb[:, :]), [R_hbs, R_c], [R_trs[t]], inc=(c == 7))
            P.op("dve", _mk("tensor_copy", out=hTs[:, :, :], in_=trs[t][:, :].rearrange("p (c k) -> p c k", c=8)),
                 [R_trs[t]], [R_hTs])
            for g6 in range(6):
                w0 = g6 * 512
                wn = min(512, DIN - w0)
                pb = nxs("pj")
                for c in range(8):
                    P.op("pe", _mk("matmul", pjs[pb][:, 0:wn], hTs[:, c, :], win_sb[:, c, w0:w0 + wn], start=(c == 0),
                                   stop=(c == 7)), [R_hTs, R_w], [R_pjs[pb]], inc=(c == 7))
                if g6 % 2 == 0:
                    P.op("act", _mk("activation", out=tmS[:, w0:w0 + wn], in_=pjs[pb][:, 0:wn], func=AF.Copy),
                         [R_pjs[pb]], [R_tmS])
                else:
                    P.op("dve", _mk("tensor_copy", out=tmS[:, w0:w0 + wn], in_=pjs[pb][:, 0:wn]), [R_pjs[pb]], [R_tmS])
            out_tickets.append(P.dma(ksn[:, :], tmS[:, 1280:2048], [R_tmS], [], R_tmS))
            out_tickets.append(P.dma(vsn[:, :], tmS[:, 2048:2816], [R_tmS], [], R_tmS))
            P.op("dve", _mk("tensor_copy", out=VN[:, :], in_=tmS[:, 2048:2816]), [R_tmS], [R_VN])
            P.op("act", _mk("activation", out=gluS[:, 256:512], in_=tmS[:, 256:512], func=AF.Sigmoid), [R_tmS], [R_gluS])
            P.op("dve", _mk("tensor_tensor", out=gluS[:, 0:256], in0=tmS[:, 0:256], in1=gluS[:, 256:512], op=ALU.mult),
                 [R_tmS, R_gluS], [R_gluS])
            out_tickets.append(P.dma(cas[:, 0:22, :], sca[:, 8:30, :], [], [], R_gluS))
            for s in range(16):
                out_tickets.append(P.dma(cas[s, 22:30, :], gluS[8 * s:8 * s + 8, 0:256], [R_gluS], [], R_gluS))
            P.dma(scaS[0:120, :, :], sca.rearrange("(a s) t c -> (s t) a c", a=4), [], [R_scaS], R_scaS)
            for a4 in range(4):
                for f in range(2):
                    pb = nxs("pj")
                    P.op("pe", _mk("transpose", out=pjs[pb][:, 0:120], in_=scaS[0:120, a4, f * 128:(f + 1) * 128],
                                   identity=identf[0:120, 0:120]), [R_scaS, R_cs], [R_pjs[pb]])
                    P.op("dve", _mk("tensor_copy", out=gluSF[:, f, 4 * a4:4 * a4 + 4, 0:30],
                                    in_=pjs[pb][:, 0:120].rearrange("p (s t) -> p s t", s=4)), [R_pjs[pb]], [R_gluSF])
            for f in range(2):
                pb = nxs("pj")
                P.op("pe", _mk("transpose", out=pjs[pb][:, 0:128], in_=gluS[:, f * 128:(f + 1) * 128], identity=identf[:, :]),
                     [R_gluS, R_cs], [R_pjs[pb]])
                P.op("dve", _mk("tensor_copy", out=gluSF[:, f, :, 30:38],
                                in_=pjs[pb][:, 0:128].rearrange("p (s t) -> p s t", s=16)), [R_pjs[pb]], [R_gluSF])
            for f in range(2):
                cv_ = caccS[:, f, :].rearrange("p (s t) -> p s t", s=16)
                P.op("dve", _mk("tensor_scalar", out=cv_, in0=gluSF[:, f, :, 0:8], scalar1=caw_sb[:, f, 0:1],
                                scalar2=cab_sb[:, f:f + 1], op0=ALU.mult, op1=ALU.add), [R_gluSF, R_c], [R_caccS])
                for k in range(1, 31):
                    P.op("dve", _mk("scalar_tensor_tensor", out=cv_, in0=gluSF[:, f, :, k:k + 8], scalar=caw_sb[:, f, k:k + 1],
                                    in1=cv_, op0=ALU.mult, op1=ALU.add), [R_gluSF, R_c, R_caccS], [R_caccS])
            pm = nxs("pj")
            for f in range(2):
                P.op("pe", _mk("matmul", pjs[pm][:, 0:128], ones256[:, :], caccS[:, f, :], start=(f == 0), stop=(f == 1)),
                     [R_caccS, R_c], [R_pjs[pm]], inc=(f == 1))
            P.op("act", _mk("activation", out=csqS[:, :, :].rearrange("p a b -> p (a b)"),
                            in_=caccS[:, :, :].rearrange("p a b -> p (a b)"), func=AF.Square), [R_caccS], [R_csqS])
            pq = nxs("pj")
            for f in range(2):
                P.op("pe", _mk("matmul", pjs[pq][:, 0:128], ones256[:, :], csqS[:, f, :], start=(f == 0), stop=(f == 1)),
                     [R_csqS, R_c], [R_pjs[pq]], inc=(f == 1))
            P.op("act", _mk("activation", out=lnAs[:, :], in_=pjs[pm][:, 0:128], func=AF.Square), [R_pjs[pm]], [R_lnAs])
            P.op("dve", _mk("tensor_tensor", out=lnAs[:, :], in0=pjs[pq][:, 0:128], in1=lnAs[:, :], op=ALU.subtract),
                 [R_pjs[pq], R_lnAs], [R_lnAs])
            P.op("dve", _mk("tensor_scalar", out=lnAs[:, :], in0=lnAs[:, :], scalar1=EPS, scalar2=None, op0=ALU.add),
                 [R_lnAs], [R_lnAs])
            P.op("act", _mk("activation", out=lnAs[:, :], in_=lnAs[:, :], func=AF.Ln), [R_lnAs], [R_lnAs])
            P.op("act", _mk("activation", out=lnAs[:, :], in_=lnAs[:, :], func=AF.Exp, scale=-0.5), [R_lnAs], [R_lnAs])
            for f in range(2):
                P.op("dve", _mk("tensor_tensor", out=lnBs[:, :], in0=caccS[:, f, :], in1=pjs[pm][:, 0:128], op=ALU.subtract),
                     [R_caccS, R_pjs[pm]], [R_lnBs])
                P.op("dve", _mk("tensor_tensor", out=lnBs[:, :], in0=lnBs[:, :], in1=lnAs[:, :], op=ALU.mult),
                     [R_lnBs, R_lnAs], [R_lnBs])
                P.op("act", _mk("activation", out=aTs[:, f, :], in_=lnBs[:, :], func=AF.Silu, scale=lng_sb[:, f:f + 1],
                                bias=lnb_sb[:, f:f + 1]), [R_lnBs, R_c], [R_aTs])
            for hp in range(6):
                pb = nxs("pj")
                for c in range(8):
                    P.op("pe", _mk("matmul", pjs[pb][:, 0:128], win_sb[:, c, (4 + hp) * 128:(5 + hp) * 128], hTs[:, c, :],
                                   start=(c == 0), stop=(c == 7)), [R_w, R_hTs], [R_pjs[pb]], inc=(c == 7))
                P.op("act", _mk("activation", out=Qz[0:64, hp, :, 0, :],
                                in_=pjs[pb][0:64, 0:128].rearrange("p (s t) -> p s t", s=16), func=AF.Copy), [R_pjs[pb]], [R_Qz])
                P.op("dve", _mk("tensor_copy", out=Qz[64:128, hp, :, 1, :],
                                in_=pjs[pb][64:128, 0:128].rearrange("p (s t) -> p s t", s=16)), [R_pjs[pb]], [R_Qz])
            for hp in range(6):
                pb = nxs("pj")
                for c in range(8):
                    P.op("pe", _mk("matmul", pjs[pb][:, 0:128], win_sb[:, c, (10 + hp) * 128:(11 + hp) * 128], hTs[:, c, :],
                                   start=(c == 0), stop=(c == 7)), [R_w, R_hTs], [R_pjs[pb]], inc=(c == 7))
                P.op("act", _mk("activation", out=KNT[:, hp, :], in_=pjs[pb][:, 0:128], func=AF.Copy), [R_pjs[pb]], [R_KNT])

            for s in range(16):
                b = s % 2
                first = [True]
                for hp in range(6):
                    fs = slice(128 * hp, 128 * hp + 128)
                    kt = nxs("kt")
                    t0 = nxs("tr")
                    for r in range(4):
                        P.op("pe", _mk("transpose", out=trs[t0][:, r * 128:(r + 1) * 128], in_=KB[b][:, r, fs],
                                       identity=identb[:, :]), [R_KB[b], R_c], [R_trs[t0]], inc=False)
                    for tq in range(4):
                        P.op("pe", _mk("transpose", out=trs[t0][:, (4 + tq) * 128:(5 + tq) * 128], in_=KA[b][:, tq, fs],
                                       identity=identb[:, :]), [R_KA[b], R_c], [R_trs[t0]], inc=(tq == 3))
                    P.op("dve", _mk("tensor_copy", out=KTs[kt][:, 0:8, :].rearrange("p a b -> p (a b)"), in_=trs[t0][:, :]),
                         [R_trs[t0]], [R_KTs[kt]])
                    t1 = nxs("tr")
                    for tq in range(4, 8):
                        P.op("pe", _mk("transpose", out=trs[t1][:, (tq - 4) * 128:(tq - 3) * 128], in_=KA[b][:, tq, fs],
                                       identity=identb[:, :]), [R_KA[b], R_c], [R_trs[t1]], inc=(tq == 7))
                    P.op("act", _mk("activation", out=KTs[kt][:, 8:12, :].rearrange("p a b -> p (a b)"), in_=trs[t1][:, 0:512],
                                    func=AF.Copy), [R_trs[t1]], [R_KTs[kt]])
                    sc_ = nxs("sc")
                    qall = Qz[:, hp, s, :, :].rearrange("p a b -> p (a b)")
                    for r in range(4):
                        P.op("pe", _mk("matmul", scs[sc_][:, 16 * r:16 * r + 16], KTs[kt][:, r, :], qall, start=True, stop=True),
                             [R_KTs[kt], R_Qz], [R_scs[sc_]], inc=False)
                    for tq in range(8):
                        P.op("pe", _mk("matmul", scs[sc_][:, 64 + 2 * tq:66 + 2 * tq], KTs[kt][:, 4 + tq, :], Qz[:, hp, s, :, tq],
                                       start=True, stop=True), [R_KTs[kt], R_Qz], [R_scs[sc_]], inc=False)
                    P.op("pe", _mk("matmul", scs[sc_][:, 80:96], KNT[:, hp, :], qall, start=True, stop=True),
                         [R_KNT, R_Qz], [R_scs[sc_]])
                    pi = nxs("pt")
                    P.op("act", _mk("activation", out=pts[pi][:, :], in_=scs[sc_][:, 0:96], func=AF.Exp, scale=SCALE),
                         [R_scs[sc_]], [R_pts[pi]])
                    P.op("dve", _mk("tensor_tensor", out=pts[pi][:, :], in0=pts[pi][:, :], in1=smb[:, s, :], op=ALU.mult),
                         [R_pts[pi], R_cs], [R_pts[pi]])
                    ncol = nums[:, 16 * hp:16 * hp + 16]
                    dcol = dens[:, 16 * hp:16 * hp + 16]
                    for r in range(4):
                        st_ = first[0]
                        first[0] = False
                        P.op("pe", _mk("matmul", ncol, VB[b][:, r, fs], pts[pi][:, 16 * r:16 * r + 16], start=st_, stop=True,
                                       skip_group_check=True), [R_VB[b], R_pts[pi]], [R_nums], inc=False)
                        P.op("pe", _mk("matmul", dcol, onesb[:, :], pts[pi][:, 16 * r:16 * r + 16], start=st_, stop=True,
                                       skip_group_check=True), [R_cs, R_pts[pi]], [R_dens], inc=False)
                    for tq in range(8):
                        ncol2 = nums[:, 16 * hp + tq:16 * hp + 16:8]
                        dcol2 = dens[:, 16 * hp + tq:16 * hp + 16:8]
                        P.op("pe", _mk("matmul", ncol2, VA[b][:, tq, fs], pts[pi][:, 64 + 2 * tq:66 + 2 * tq], start=False,
                                       stop=True, skip_group_check=True), [R_VA[b], R_pts[pi]], [R_nums], inc=False)
                        P.op("pe", _mk("matmul", dcol2, onesb[:, :], pts[pi][:, 64 + 2 * tq:66 + 2 * tq], start=False,
                                       stop=True, skip_group_check=True), [R_cs, R_pts[pi]], [R_dens], inc=False)
                    P.op("pe", _mk("matmul", ncol, VN[:, fs], pts[pi][:, 80:96], start=False, stop=True, skip_group_check=True),
                         [R_VN, R_pts[pi]], [R_nums], inc=False)
                    P.op("pe", _mk("matmul", dcol, onesb[:, :], pts[pi][:, 80:96], start=False, stop=True, skip_group_check=True),
                         [R_cs, R_pts[pi]], [R_dens, R_nums])
                P.op("dve", _mk("reciprocal", out=rds[:, :], in_=dens[:, 0:96]), [R_dens], [R_rds])
                P.op("dve", _mk("tensor_tensor", out=tmpo[:, :], in0=nums[:, 0:96], in1=rds[:, :], op=ALU.mult),
                     [R_nums, R_rds], [R_tmpo])
                tv = tmpo[:, :].rearrange("p (a h t) -> p a h t", a=6, h=2)
                P.op("act", _mk("activation", out=OTs[0:64, :, 8 * s:8 * s + 8], in_=tv[0:64, :, 0, :], func=AF.Copy),
                     [R_tmpo], [R_OTs])
                P.op("act", _mk("activation", out=OTs[64:128, :, 8 * s:8 * s + 8], in_=tv[64:128, :, 1, :], func=AF.Copy),
                     [R_tmpo], [R_OTs])
                if s + 2 < 16:
                    load_cache(s + 2)
            for nh in range(2):
                pb = nxs("pj")
                for f in range(2):
                    P.op("pe", _mk("matmul", pjs[pb][:, :], aTs[:, f, :], wouta[:, f, 512 * nh:512 * nh + 512], start=(f == 0),
                                   stop=False), [R_aTs, R_w], [R_pjs[pb]], inc=False)
                for hp in range(6):
                    P.op("pe", _mk("matmul", pjs[pb][:, :], OTs[:, hp, :], wouto[:, hp, 512 * nh:512 * nh + 512], start=False,
                                   stop=(hp == 5)), [R_OTs, R_w], [R_pjs[pb]], inc=(hp == 5))
                P.op("dve", _mk("tensor_tensor", out=xts[:, 512 * nh:512 * nh + 512], in0=xts[:, 512 * nh:512 * nh + 512],
                                in1=pjs[pb][:, :], op=ALU.add), [R_xts, R_pjs[pb]], [R_xts])
            P.dma(xmid[33 * 128:34 * 128, :], xts[:, :], [R_xts], [xmid_res[33]], R_xts)
            P.barrier()
            P.flush()

    with ExitStack() as SB:
        wg_sb = sb(SB, "wg_sb", [128, 8, DFF], BF16)
        wu_sb = sb(SB, "wu_sb", [128, 8, DFF], BF16)
        wd_sb = sb(SB, "wd_sb", [128, NFF, D], BF16)
        R_w2 = Res("weights2")
        R_c2 = Res("consts2")
        gffn_sb = sb(SB, "gffn_sb", [128, 8], F32)
        fcw_sb = sb(SB, "fcw_sb", [128, NFF, 3], F32)
        fcb_sb = sb(SB, "fcb_sb", [128, NFF], F32)
        gfin_sb = sb(SB, "gfin_sb", [128, D], F32)
        identb2 = sb(SB, "identb2", [128, 128], BF16)
        identf2 = sb(SB, "identf2", [128, 128], F32)
        flag2 = sb(SB, "flag2", [128, 8], F32)
        hist = sb(SB, "hist", [128, NFF, 2], F32)
        negh = sb(SB, "negh", [128, 8], F32)
        R_hist = Res("hist")

        with ExitStack() as SB0:
            stg2 = [sb(SB0, "stg2_%d" % i, [128, DFF], F32) for i in range(2)]
            R_stg2 = [Res("stg2_0"), Res("stg2_1")]
            for dst, src in ((gffn_sb, gffn), (fcb_sb, fcb), (gfin_sb, gfin), (flag2, flagsd), (identf2, identd)):
                r = Res("v")
                P.dma(dst[:, :], src[:, :], [], [r, R_c2], r)
            r = Res("v")
            P.dma(fcw_sb[:, :, :].rearrange("p a b -> p (a b)"), fcw[:, :], [], [r, R_c2], r)
            P.op("dve", _mk("tensor_copy", out=identb2[:, :], in_=identf2[:, :]), [R_c2], [R_c2])
            R_wdq = Res("wdq")
            for ff in range(0, NFF, 2):
                P.dma(wd_sb[:, ff:ff + 2, :], wd[ff * 128:(ff + 2) * 128, :].rearrange("(a p) d -> p a d", p=128),
                      [], [], R_wdq, q="pool")
            k = 0
            for (wsrc, wdst) in ((wg, wg_sb), (wu, wu_sb)):
                for c in range(8):
                    s = k % 2
                    k += 1
                    P.dma(stg2[s][:, :], wsrc[c * 128:(c + 1) * 128, :], [], [R_stg2[s]], R_stg2[s],
                          q=("pool" if s == 1 else "sp"))
                    if k % 2 == 0:
                        P.op("act", _mk("activation", out=wdst[:, c, :], in_=stg2[s][:, :], func=AF.Copy,
                                        scale=gffn_sb[:, c:c + 1]), [R_stg2[s], R_c2], [])
                    else:
                        P.op("dve", _mk("tensor_scalar", out=wdst[:, c, :], in0=stg2[s][:, :], scalar1=gffn_sb[:, c:c + 1],
                                        scalar2=None, op0=ALU.mult), [R_stg2[s], R_c2], [])
            P.op("pool", _mk("memset", hist[:, :, :].rearrange("p a b -> p (a b)"), 0.0), [], [R_hist])
            P.op("pool", _mk("memset", negh[:, :], -0.5), [], [R_c2])
            P.barrier()
            P.flush()

        def phase2_body(S, tag, sample):
            NXM = 2 if sample else 6
            xm = [sb(S, "xm%s%d" % (tag, i), [128, D], F32) for i in range(NXM)]
            R_xm = [Res("xm%d" % i) for i in range(NXM)]
            NHB = 1 if sample else 2
            hb2s = [sb(S, "hb2%s%d" % (tag, i), [128, D], BF16) for i in range(NHB)]
            R_hb2s = [Res("hb2_%d" % i) for i in range(NHB)]
            ss2 = sb(S, "ss2" + tag, [128, 8], F32)
            R_ss2 = Res("ss2")
            R_ss2f = Res("ss2f")
            NH2 = 1 if sample else 2
            h2T = [sb(S, "h2T%s%d" % (tag, i), [128, 8, 256], BF16) for i in range(NH2)]
            R_h2T = [Res("h2T%d" % i) for i in range(NH2)]
            NEB = 2 if sample else 4
            gbufs = [sb(S, "gbuf%s%d" % (tag, i), [128, 2 + 256], F32) for i in range(NEB)]
            R_gbufs = [Res("gbuf%d" % i) for i in range(NEB)]
            R_ghs = [Res("gh%d" % i) for i in range(NEB)]
            gcvs = [sb(S, "gcv%s%d" % (tag, i), [128, 256], F32) for i in range(NEB)]
            R_gcvs = [Res("gcv%d" % i) for i in range(NEB)]
            upsbs = [sb(S, "upsb%s%d" % (tag, i), [128, 256], F32) for i in range(NEB)]
            R_upsbs = [Res("upsb%d" % i) for i in range(NEB)]
            uT = [sb(S, "uT%s%d" % (tag, i), [128, NFF, 256], BF16) for i in range(NH2)]
            R_uT = [Res("uT%d" % i) for i in range(NH2)]
            if sample:
                scfS = sb(S, "scfS", [32, DFF], F32)
                R_scfS = Res("scfS")
                cfoS = sb(S, "cfoS", [128, DFF], F32)
                R_cfoS = Res("cfoS")
                shist = sb(S, "shist", [128, NFF, 32], F32)
                R_shist = Res("shist")
            NPG, NPU = 2, 4
            pg = [ps(S, "pg%s%d" % (tag, i), [128, 512], F32) for i in range(NPG)]
            R_pg = [Res("pg%d" % i) for i in range(NPG)]
            pu = [ps(S, "pu%s%d" % (tag, i), [128, 512], F32) for i in range(NPU)]
            R_pu = [Res("pu%d" % i) for i in range(NPU)]
            pd = [ps(S, "pd%s%d" % (tag, i), [128, 512], F32) for i in range(1)]
            R_pd = [Res("pd0")]
            tr2 = [ps(S, "tr2%s%d" % (tag, i), [128, 1024], BF16) for i in range(1)]
            R_tr2 = [Res("tr2_0")]
            st2 = {"pg": 0, "pu": 0, "pd": 0, "tr": 0, "xm": 0}

            def nxt2(k_, n=2):
                v = st2[k_]
                st2[k_] = (v + 1) % n
                return v

            def norm_part(blk, jcol):
                b = nxt2("xm", NXM)
                hb2, R_hb2 = hb2s[jcol % NHB], R_hb2s[jcol % NHB]
                P.dma(xm[b][:, :], xmid[blk * 128:(blk + 1) * 128, :], [xmid_res[blk]], [R_xm[b]], R_xm[b])
                P.op("act", _mk("activation", out=hb2[:, :], in_=xm[b][:, :], func=AF.Square, accum_out=ss2[:, 0:1]),
                     [R_xm[b]], [R_hb2, R_ss2])
                P.op("dve", _mk("tensor_scalar", out=ss2[:, 1:2], in0=ss2[:, 0:1], scalar1=1.0 / D, scalar2=EPS,
                                op0=ALU.mult, op1=ALU.add), [R_ss2], [R_ss2])
                P.op("pool", _mk("tensor_tensor", out=ss2[:, 2:3], in0=ss2[:, 1:2], in1=negh[:, 0:1], op=ALU.pow),
                     [R_ss2, R_c2], [R_ss2])
                P.op("act", _mk("activation", out=hb2[:, :], in_=xm[b][:, :], func=AF.Copy, scale=ss2[:, 2:3]),
                     [R_xm[b], R_ss2], [R_hb2])
                return b

            def trans_part(jcol, hsel):
                hb2, R_hb2 = hb2s[jcol % NHB], R_hb2s[jcol % NHB]
                t = nxt2("tr", 1)
                for c in range(8):
                    P.op("pe", _mk("transpose", out=tr2[t][:, c * 128:(c + 1) * 128], in_=hb2[:, c * 128:(c + 1) * 128],
                                   identity=identb2[:, :]), [R_hb2, R_c2], [R_tr2[t]], inc=(c == 7))
                P.op("dve", _mk("tensor_copy", out=h2T[hsel][:, :, 128 * jcol:128 * jcol + 128],
                                in_=tr2[t][:, :].rearrange("p (c k) -> p c k", c=8)), [R_tr2[t]], [R_h2T[hsel]])

            def load_norm_T(blk, jcol, hsel):
                b = norm_part(blk, jcol)
                trans_part(jcol, hsel)
                return b

            def gate_rows_only(ncols, hsel):
                for ff in range(NFF):
                    g = nxt2("pg", NPG)
                    for c in range(8):
                        P.op("pe", _mk("matmul", pg[g][:, 0:ncols], wg_sb[:, c, ff * 128:(ff + 1) * 128], h2T[hsel][:, c, 0:ncols],
                                       start=(c == 0), stop=(c == 7)), [R_w2, R_h2T[hsel]], [R_pg[g]], inc=(c == 7))
                    P.op("dve", _mk("tensor_scalar", out=hist[:, ff, :], in0=pg[g][:, ncols - 2:ncols], scalar1=flag2[:, 0:1],
                                    scalar2=None, op0=ALU.mult), [R_pg[g], R_c2], [R_hist])

            def gate_up(ncols, hsel, hooks=None):
                hooks = hooks or {}
                tail = [None]
                for ff in range(NFF):
                    g = nxt2("pg", NPG)
                    u = nxt2("pu", NPU)
                    gbuf, R_gbuf = gbufs[ff % NEB], R_gbufs[ff % NEB]
                    gcv, R_gcv = gcvs[ff % NEB], R_gcvs[ff % NEB]
                    upsb, R_upsb = upsbs[ff % NEB], R_upsbs[ff % NEB]
                    for c in range(8):
                        P.op("pe", _mk("matmul", pg[g][:, 0:ncols], wg_sb[:, c, ff * 128:(ff + 1) * 128], h2T[hsel][:, c, 0:ncols],
                                       start=(c == 0), stop=(c == 7)), [R_w2, R_h2T[hsel]], [R_pg[g]], inc=(c == 7))
                    for c in range(8):
                        P.op("pe", _mk("matmul", pu[u][:, 0:ncols], wu_sb[:, c, ff * 128:(ff + 1) * 128], h2T[hsel][:, c, 0:ncols],
                                       start=(c == 0), stop=(c == 7)), [R_w2, R_h2T[hsel]], [R_pu[u]], inc=(c == 7))
                    R_gh = R_ghs[ff % NEB]
                    if not sample:
                        P.op("dve", _mk("tensor_copy", out=gbuf[:, 0:2], in_=hist[:, ff, :]), [R_hist], [R_gh])
                        P.op("act", _mk("activation", out=gbuf[:, 2:2 + ncols], in_=pg[g][:, 0:ncols], func=AF.Copy),
                             [R_pg[g]], [R_gbuf])
                        P.op("dve", _mk("tensor_copy", out=hist[:, ff, :], in_=gbuf[:, ncols:ncols + 2]), [R_gbuf], [R_hist])
                        srcs = [gbuf[:, k_:k_ + ncols] for k_ in range(3)]
                        gout = gcv[:, 0:ncols]
                    else:
                        gv = gbuf[:, 0:160].rearrange("p (s t) -> p s t", s=16)
                        P.op("pool", _mk("tensor_copy", out=gv[:, :, 0:2], in_=shist[:, ff, :].rearrange("p (s t) -> p s t", s=16)),
                             [R_shist], [R_gbuf])
                        P.op("act", _mk("activation", out=gv[:, :, 2:10], in_=pg[g][:, 0:128].rearrange("p (s t) -> p s t", s=16),
                                        func=AF.Copy), [R_pg[g]], [R_gbuf])
                        srcs = [gv[:, :, k_:k_ + 8] for k_ in range(3)]
                        gout = gcv[:, 0:128].rearrange("p (s t) -> p s t", s=16)
                    P.op("dve", _mk("tensor_scalar", out=gout, in0=srcs[0], scalar1=fcw_sb[:, ff, 0:1], scalar2=fcb_sb[:, ff:ff + 1],
                                    op0=ALU.mult, op1=ALU.add), [R_gbuf, R_gh, R_c2], [R_gcv])
                    for k_ in (1, 2):
                        P.op("dve", _mk("scalar_tensor_tensor", out=gout, in0=srcs[k_], scalar=fcw_sb[:, ff, k_:k_ + 1], in1=gout,
                                        op0=ALU.mult, op1=ALU.add), [R_gbuf, R_gh, R_c2, R_gcv], [R_gcv])
                    if tail[0] is not None:
                        tail[0]()

                    def mk_tail(ff=ff, gcv=gcv, R_gcv=R_gcv, u=u):
                        def t_():
                            P.op("act", _mk("activation", out=gcv[:, 0:ncols], in_=gcv[:, 0:ncols], func=AF.Silu), [R_gcv], [R_gcv])
                            P.op("dve", _mk("tensor_tensor", out=uT[hsel][:, ff, 0:ncols], in0=gcv[:, 0:ncols],
                                            in1=pu[u][:, 0:ncols], op=ALU.mult), [R_gcv, R_pu[u]], [R_uT[hsel]])
                        return t_
                    tail[0] = mk_tail()
                    for h_ in hooks.pop(ff, []):
                        h_()
                tail[0]()

            def down_group(jcol, b, nh, hsel):
                d = nxt2("pd", 1)
                for ff in range(NFF):
                    P.op("pe", _mk("matmul", pd[d][:, :], uT[hsel][:, ff, 128 * jcol:128 * jcol + 128],
                                   wd_sb[:, ff, 512 * nh:512 * nh + 512], start=(ff == 0), stop=(ff == NFF - 1)),
                         [R_uT[hsel], R_w2], [R_pd[d]], inc=(ff == NFF - 1))
                P.op("dve", _mk("tensor_tensor", out=xm[b][:, 512 * nh:512 * nh + 512], in0=xm[b][:, 512 * nh:512 * nh + 512],
                                in1=pd[d][:, :], op=ALU.add), [R_xm[b], R_pd[d]], [R_xm[b]])

            def final_part(b, out_ap, jsel=0):
                hb2, R_hb2 = hb2s[jsel % NHB], R_hb2s[jsel % NHB]
                P.op("act", _mk("activation", out=hb2[:, :], in_=xm[b][:, :], func=AF.Square, accum_out=ss2[:, 4:5]),
                     [R_xm[b]], [R_hb2, R_ss2f])
                P.op("dve", _mk("tensor_scalar", out=ss2[:, 5:6], in0=ss2[:, 4:5], scalar1=1.0 / D, scalar2=EPS,
                                op0=ALU.mult, op1=ALU.add), [R_ss2f], [R_ss2f])
                P.op("pool", _mk("tensor_tensor", out=ss2[:, 6:7], in0=ss2[:, 5:6], in1=negh[:, 0:1], op=ALU.pow),
                     [R_ss2f, R_c2], [R_ss2f])
                P.op("dve", _mk("scalar_tensor_tensor", out=xm[b][:, :], in0=xm[b][:, :], scalar=ss2[:, 6:7], in1=gfin_sb[:, :],
                                op0=ALU.mult, op1=ALU.mult), [R_xm[b], R_ss2f, R_c2], [R_xm[b]])
                out_tickets.append(P.dma(out_ap, xm[b][:, :], [R_xm[b]], [], R_xm[b]))

            def down_final(bufs, hsel, out_ap_fn):
                for jcol, b in enumerate(bufs):
                    for nh in range(2):
                        down_group(jcol, b, nh, hsel)
                    final_part(b, out_ap_fn(jcol), jcol)

            if not sample:
                load_norm_T(0, 0, 1)
                bufs_next = [load_norm_T(1, 0, 0), load_norm_T(2, 1, 0)]
                gate_rows_only(128, 1)
                prev = None
                for t2 in range(16):
                    hsel = t2 % 2
                    bufs_cur = bufs_next
                    box = {"b": [None, None]}
                    hooks = {}
                    if t2 + 1 < 16:
                        nsel = (t2 + 1) % 2
                        for jc in range(2):
                            hooks.setdefault(1 + 2 * jc, []).append(
                                lambda jc=jc, t2=t2, box=box: box["b"].__setitem__(jc, norm_part(3 + 2 * t2 + jc, jc)))
                            hooks.setdefault(5 + 2 * jc, []).append(lambda jc=jc, nsel=nsel: trans_part(jc, nsel))
                    if prev is not None:
                        pbufs, phsel, pout = prev
                        for jc in range(2):
                            for nh in range(2):
                                hooks.setdefault(7 + 6 * jc + 3 * nh, []).append(
                                    lambda jc=jc, nh=nh, pbufs=pbufs, phsel=phsel: down_group(jc, pbufs[jc], nh, phsel))
                            hooks.setdefault(13 + 6 * jc, []).append(
                                lambda jc=jc, pbufs=pbufs, pout=pout: final_part(pbufs[jc], pout(jc), jc))
                    gate_up(256, hsel, hooks=hooks)
                    assert not hooks
                    prev = (bufs_cur, hsel, (lambda jcol, t2=t2: y[(2 * t2 + jcol) * 128:(2 * t2 + jcol + 1) * 128, :]))
                    bufs_next = box["b"]
                bfree = [i_ for i_ in range(NXM) if i_ not in prev[0]][0]
                cfo, R_cfo = xm[bfree], R_xm[bfree]
                for g6 in range(6):
                    w0 = g6 * 512
                    wn = min(512, DFF - w0)
                    g = nxt2("pg", NPG)
                    for c in range(8):
                        P.op("pe", _mk("matmul", pg[g][:, 0:wn], h2T[1][:, c, 128:256], wg_sb[:, c, w0:w0 + wn],
                                       start=(c == 0), stop=(c == 7)), [R_h2T[1], R_w2], [R_pg[g]], inc=(c == 7))
                    P.op("act", _mk("activation", out=cfo[:, 0:wn], in_=pg[g][:, 0:wn], func=AF.Copy), [R_pg[g]], [R_cfo])
                    out_tickets.append(P.dma(cfp[:, w0:w0 + wn], cfo[126:128, 0:wn], [R_cfo], [], R_cfo))
                down_final(*prev)
            else:
                P.dma(scfS[0:32, :], scf.rearrange("s t f -> (s t) f"), [], [R_scfS], R_scfS)
                for ff in range(NFF):
                    g = 0 if ff < 16 else 1
                    col = (ff % 16) * 32
                    P.op("pe", _mk("transpose", out=pg[g][:, col:col + 32], in_=scfS[0:32, ff * 128:(ff + 1) * 128],
                                   identity=identf2[0:32, 0:32]), [R_scfS, R_c2], [R_pg[g]], inc=(ff == 15 or ff == NFF - 1))
                P.op("dve", _mk("tensor_copy", out=shist[:, 0:16, :].rearrange("p a b -> p (a b)"), in_=pg[0][:, 0:512]),
                     [R_pg[0]], [R_shist])
                P.op("dve", _mk("tensor_copy", out=shist[:, 16:NFF, :].rearrange("p a b -> p (a b)"), in_=pg[1][:, 0:192]),
                     [R_pg[1]], [R_shist])
                st2["pg"] = 0
                bufs = [load_norm_T(33, 0, 0)]
                gate_up(128, 0)
                down_final(bufs, 0, lambda jcol: ys[:, :])
                for g6 in range(6):
                    w0 = g6 * 512
                    wn = min(512, DFF - w0)
                    g = nxt2("pg", NPG)
                    for c in range(8):
                        P.op("pe", _mk("matmul", pg[g][:, 0:wn], h2T[0][:, c, 0:128], wg_sb[:, c, w0:w0 + wn],
                                       start=(c == 0), stop=(c == 7)), [R_h2T[0], R_w2], [R_pg[g]], inc=(c == 7))
                    P.op("act", _mk("activation", out=cfoS[:, w0:w0 + wn], in_=pg[g][:, 0:wn], func=AF.Copy), [R_pg[g]], [R_cfoS])
                for s in range(16):
                    out_tickets.append(P.dma(cfs[s, :, :], cfoS[8 * s + 6:8 * s + 8, :], [R_cfoS], [], R_cfoS))

        with ExitStack() as SB1:
            phase2_body(SB1, "p", False)
            P.barrier()
            P.flush()
        with ExitStack() as SB2:
            phase2_body(SB2, "s", True)
            need = {}
            for k_, v_ in out_tickets:
                if need.get(k_, 0) < v_:
                    need[k_] = v_
            for k_, v_ in need.items():
                P.plan["sp"].append(("w", P.sems[k_], v_))
            P.barrier()
            P.flush()
    top.close()
    return nc


def _consts():
    k = np.arange(128)[:, None]
    q = np.arange(128)[None, :]
    Lw = (k <= q).astype(np.float32)
    U = (k >= q).astype(np.float32)
    m2 = np.concatenate([U, Lw, U, Lw, Lw, U, Lw, U], axis=1)
    maskr = np.zeros((128, 4, 32), np.float32)
    for j in range(4):
        kn = (np.arange(128) - 32 * j) % 128
        maskr[:, j, :] = (kn[:, None] >= np.arange(32)[None, :]).astype(np.float32)
    L32 = Lw[0:32, 0:32]
    lw32 = np.ones((128, 32), np.float32)
    lw3 = np.zeros((128, 32), np.float32)
    for base in (0, 64):
        lw32[base:base + 32] = L32
        lw3[base + 32:base + 64] = L32
    ident = np.eye(128, dtype=np.float32)
    sm = np.zeros((128, 16, 96), np.float32)
    m_ = np.arange(128)
    for r in range(4):
        for t in range(8):
            d4 = ((t % 4) == r) * np.where(t >= 4, m_ >= 1, True)
            d1 = (4 * m_ + r >= 384 + t)
            for hh in range(2):
                sm[:, :, 16 * r + 8 * hh + t] = (d4.astype(np.float32) + d1.astype(np.float32))[:, None]
    sm[:, :, 64:80] = 1.0
    for s in range(16):
        for u in range(8):
            for t in range(8):
                mult = float(u <= t) + float(u <= t and (t - u) % 4 == 0) + float(u == t)
                for hh in range(2):
                    sm[8 * s + u, s, 80 + 8 * hh + t] = mult
    return m2, maskr.reshape(128, 128), lw32, lw3, ident, sm.reshape(128, 16 * 96)


_NC_CACHE = {}


def kernel(x_prompt, x_sample, cache_k_win, cache_v_win, state_conv_a, state_conv_ffn,
           norm_mix_g, w_in, conv_a_w, conv_a_b, ln_a_g, ln_a_b, w_out,
           norm_ffn_g, w_ffn_gate, w_ffn_up, ffn_conv_w, ffn_conv_b, w_ffn_down, norm_final_g):
    f = np.float32
    x_prompt = np.asarray(x_prompt, f)
    x_sample = np.asarray(x_sample, f)
    ckw = np.asarray(cache_k_win, f)[0].reshape(128, 2048, DATT)
    cvw = np.asarray(cache_v_win, f)[0].reshape(128, 2048, DATT)
    sca_ = np.asarray(state_conv_a, f)[0]
    scf_ = np.asarray(state_conv_ffn, f)[0]
    m2, maskr, lw32, lw3, ident, smask = _consts()

    def pc(v, n):
        return np.ascontiguousarray(np.asarray(v, f).reshape(n, 128).T)

    common = {
        "win": np.ascontiguousarray(np.asarray(w_in, f)[0]),
        "wout": np.ascontiguousarray(np.asarray(w_out, f)[0]),
        "wg": np.ascontiguousarray(np.asarray(w_ffn_gate, f)[0]),
        "wu": np.ascontiguousarray(np.asarray(w_ffn_up, f)[0]),
        "wd": np.ascontiguousarray(np.asarray(w_ffn_down, f)[0]),
        "gmix": pc(np.asarray(norm_mix_g)[0], 8),
        "gffn": pc(np.asarray(norm_ffn_g)[0], 8),
        "caw": np.ascontiguousarray(np.asarray(conv_a_w, f)[0].T.reshape(2, 128, 31).transpose(1, 0, 2).reshape(128, 62)),
        "cab": pc(np.asarray(conv_a_b)[0], 2),
        "lng": pc(np.asarray(ln_a_g)[0], 2),
        "lnb": pc(np.asarray(ln_a_b)[0], 2),
        "fcw": np.ascontiguousarray(np.asarray(ffn_conv_w, f)[0].T.reshape(NFF, 128, 3).transpose(1, 0, 2).reshape(128, NFF * 3)),
        "fcb": pc(np.asarray(ffn_conv_b)[0], NFF),
        "gfin": np.ascontiguousarray(np.broadcast_to(np.asarray(norm_final_g, f)[None, :], (128, D))),
        "m2": m2, "maskr": maskr, "lw32": lw32, "lw3": lw3, "ident": ident,
        "smask": smask,
    }
    in_maps = []
    for c in range(NCORES):
        b, half = c // 2, c % 2
        xpc = np.zeros((NT * TT, D), f)
        pre = (NHALO + NPRE) * TT
        if half == 0:
            xpc[pre:] = x_prompt[b, 0:4096]
        else:
            xpc[:] = x_prompt[b, 4096 - pre:8192]
        flags = np.zeros((128, 8), f)
        flags[:, 0:4] = float(half)
        flags[:, 4:8] = 1.0
        m = dict(common)
        m.update({
            "xp": xpc,
            "xs": np.ascontiguousarray(x_sample[16 * c:16 * c + 16].reshape(128, D)),
            "ck": np.ascontiguousarray(ckw[16 * c:16 * c + 16]),
            "cv": np.ascontiguousarray(cvw[16 * c:16 * c + 16]),
            "sca": np.ascontiguousarray(sca_[16 * c:16 * c + 16]),
            "scf": np.ascontiguousarray(scf_[16 * c:16 * c + 16]),
            "flags": flags,
        })
        in_maps.append(m)
    if "nc" not in _NC_CACHE:
        _NC_CACHE["nc"] = build_program()
    nc = _NC_CACHE["nc"]
    res = run_bass_kernel_spmd(nc, in_maps, core_ids=list(range(NCORES)))
    R = res.results
    y_prompt = np.stack([np.concatenate([R[2 * b]["y"], R[2 * b + 1]["y"]], axis=0) for b in range(4)])
    y_sample = np.concatenate([R[c]["ys"] for c in range(NCORES)], axis=0).reshape(128, 8, D)
    kp = np.stack([R[2 * b + 1]["kwin"].reshape(2048, NH, 64) for b in range(4)])[None]
    vp = np.stack([R[2 * b + 1]["vwin"].reshape(2048, NH, 64) for b in range(4)])[None]
    capo = np.stack([R[2 * b + 1]["cap"] for b in range(4)])[None]
    cfpo = np.stack([R[2 * b + 1]["cfp"] for b in range(4)])[None]
    ksno = np.concatenate([R[c]["ksn"] for c in range(NCORES)], axis=0).reshape(1, 128, 8, NH, 64)
    vsno = np.concatenate([R[c]["vsn"] for c in range(NCORES)], axis=0).reshape(1, 128, 8, NH, 64)
    caso = np.concatenate([R[c]["cas"] for c in range(NCORES)], axis=0)[None]
    cfso = np.concatenate([R[c]["cfs"] for c in range(NCORES)], axis=0)[None]
    return tuple(np.asarray(a, f) for a in (y_prompt, y_sample, kp, vp, capo, cfpo, ksno, vsno, caso, cfso))
```

```python
from contextlib import ExitStack
import numpy as np
import concourse.bass as bass
import concourse.mybir as mybir
from concourse.bass_utils import run_bass_kernel_spmd

F32 = mybir.dt.float32
BF16 = mybir.dt.bfloat16
AF = mybir.ActivationFunctionType
ALU = mybir.AluOpType

NCORES = 8
D = 1024
DCONV = 256
DATT = 768
NH = 12
DIN = 2816
DFF = 2816
NFF = 22
TT = 512
NHALO = 4
NPRE = 1
NMAIN = 8
NT = NHALO + NPRE + NMAIN
EPS = 1e-6
SCALE = 0.125
ENGS = ("pe", "act", "dve", "pool", "sp")


_POS = {"matmul": ("out", "lhsT", "rhs"), "transpose": ("out", "in_", "identity"), "memset": ("ap", "constant")}


def _mk(name, *args, **kw):
    for n, a in zip(_POS.get(name, ()), args):
        kw[n] = a
    if name == "matmul" and kw.get("tile_position", 0) is None:
        kw.pop("tile_position")
    return (name, kw)


class Res:
    __slots__ = ("name", "w", "r", "chan")

    def __init__(self, name):
        self.name = name
        self.w = None
        self.r = {}
        self.chan = None


class Planner:
    def __init__(self, nc, stack):
        self.nc = nc
        self.stack = stack
        self.sems = {}
        self.plan = {e: [] for e in ENGS}
        self.cnt = {e: 0 for e in ENGS}
        self.seen = {e: {} for e in ENGS}
        self.chans = []
        for e in ENGS:
            self.sems[e] = stack.enter_context(nc.semaphore(name="s_" + e))
        self.nuniq = 0

    def _need(self, reads, writes):
        need = {}

        def add(k, v):
            if need.get(k, 0) < v:
                need[k] = v

        for r in reads:
            if r.w is not None:
                add(*r.w)
        for w in writes:
            if w.w is not None:
                add(*w.w)
            for k, v in w.r.items():
                add(k, v)
        return need

    def _waits(self, ename, reads, writes):
        need = self._need(reads, writes)
        seen = self.seen[ename]
        for k, v in need.items():
            if k == ename and ename == "pe":
                continue
            if seen.get(k, 0) >= v:
                continue
            self.plan[ename].append(("w", self.sems[k], v))
            seen[k] = v

    def _mark(self, t, reads, writes):
        for r in reads:
            if r.r.get(t[0], 0) < t[1]:
                r.r[t[0]] = t[1]
        for w in writes:
            w.w = t
            w.r = {}

    def op(self, ename, fn, reads=(), writes=(), inc=True):
        self._waits(ename, reads, writes)
        if inc:
            self.cnt[ename] += 1
            t = (ename, self.cnt[ename])
            self.plan[ename].append(("i", fn, self.sems[ename], 1))
        else:
            t = (ename, self.cnt[ename] + 1)
            self.plan[ename].append(("i", fn, None, 0))
        self._mark(t, reads, writes)
        return t

    def dma(self, out, in_, reads, writes, chan, q="sp", **kw):
        self._waits(q, reads, writes)
        if chan.chan is None:
            key = ("d", self.nuniq)
            self.nuniq += 1
            self.sems[key] = self.stack.enter_context(self.nc.semaphore(name="d_%d" % key[1]))
            chan.chan = [key, 0]
            self.chans.append(chan)
        chan.chan[1] += 16
        self.plan[q].append(("i", ("dma_start", dict(out=out, in_=in_, **kw)), self.sems[chan.chan[0]], 16))
        t = (chan.chan[0], chan.chan[1])
        self._mark(t, reads, writes)
        return t

    def barrier(self):
        for e in ENGS:
            seen = self.seen[e]
            for f in ENGS:
                if f == "sp" or (f == e and e == "pe"):
                    continue
                v = self.cnt[f]
                if v > 0 and seen.get(f, 0) < v:
                    self.plan[e].append(("w", self.sems[f], v))
                    seen[f] = v
            for ch in self.chans:
                k, v = ch.chan
                if v > 0 and seen.get(k, 0) < v:
                    self.plan[e].append(("w", self.sems[k], v))
                    seen[k] = v

    def check_deadlock(self, plan):
        semv = getattr(self, "_semv", {})
        pos = {e: 0 for e in ENGS}
        rev = {id(v): k for k, v in self.sems.items()}
        progress = True
        while progress:
            progress = False
            for e in ENGS:
                items = plan[e]
                while pos[e] < len(items):
                    it = items[pos[e]]
                    if it[0] == "w":
                        k = rev[id(it[1])]
                        if semv.get(k, 0) >= it[2]:
                            pos[e] += 1
                            progress = True
                        else:
                            break
                    else:
                        if it[2] is not None:
                            k = rev[id(it[2])]
                            semv[k] = semv.get(k, 0) + it[3]
                        pos[e] += 1
                        progress = True
        self._semv = semv
        for e in ENGS:
            if pos[e] < len(plan[e]):
                it = plan[e][pos[e]]
                raise RuntimeError("DEADLOCK: engine %s stuck at item %d/%d waiting %s >= %s (have %s); next=%s" % (
                    e, pos[e], len(plan[e]), rev[id(it[1])], it[2], semv.get(rev[id(it[1])], 0),
                    [x[1][0] if x[0] == "i" else "w" for x in plan[e][pos[e]:pos[e] + 4]]))

    def flush(self):
        plan = self.plan
        self.plan = {e: [] for e in ENGS}
        self.check_deadlock(plan)

        def mk(ename):
            def body(eng):
                for item in plan[ename]:
                    if item[0] == "w":
                        eng.wait_ge(item[1], item[2])
                    else:
                        inst = getattr(eng, item[1][0])(**item[1][1])
                        if item[2] is not None:
                            inst.then_inc(item[2], item[3])
            return body

        with self.nc.Block() as block:
            block.tensor(mk("pe"))
            block.scalar(mk("act"))
            block.vector(mk("dve"))
            block.gpsimd(mk("pool"))
            block.sync(mk("sp"))


def build_program():
    nc = bass.Bass("TRN2", target_bir_lowering=False)

    def din(name, shape):
        return nc.dram_tensor(name, list(shape), F32, kind="ExternalInput").ap()

    def dout(name, shape):
        return nc.dram_tensor(name, list(shape), F32, kind="ExternalOutput").ap()

    xp = din("xp", [NT * TT, D])
    xs = din("xs", [128, D])
    ck = din("ck", [16, 2048, DATT])
    cv = din("cv", [16, 2048, DATT])
    sca = din("sca", [16, 30, DCONV])
    scf = din("scf", [16, 2, DFF])
    win = din("win", [D, DIN])
    wout = din("wout", [D, D])
    wg = din("wg", [D, DFF])
    wu = din("wu", [D, DFF])
    wd = din("wd", [DFF, D])
    gmix = din("gmix", [128, 8])
    gffn = din("gffn", [128, 8])
    caw = din("caw", [128, 2 * 31])
    cab = din("cab", [128, 2])
    lng = din("lng", [128, 2])
    lnb = din("lnb", [128, 2])
    fcw = din("fcw", [128, NFF * 3])
    fcb = din("fcb", [128, NFF])
    gfin = din("gfin", [128, D])
    m2d = din("m2", [128, 1024])
    maskrd = din("maskr", [128, 4 * 32])
    lw32d = din("lw32", [128, 32])
    lw3d = din("lw3", [128, 32])
    identd = din("ident", [128, 128])
    flagsd = din("flags", [128, 8])
    smaskd = din("smask", [128, 16 * 96])

    y = dout("y", [NMAIN * TT, D])
    ys = dout("ys", [128, D])
    kwin = dout("kwin", [2048, DATT])
    vwin = dout("vwin", [2048, DATT])
    cap = dout("cap", [30, DCONV])
    cfp = dout("cfp", [2, DFF])
    ksn = dout("ksn", [128, DATT])
    vsn = dout("vsn", [128, DATT])
    cas = dout("cas", [16, 30, DCONV])
    cfs = dout("cfs", [16, 2, DFF])
    xmid = nc.dram_tensor("xmid", [34 * 128, D], F32, kind="Internal").ap()

    top = ExitStack()
    P = Planner(nc, top)
    out_tickets = []

    def sb(stack, name, shape, dt):
        return stack.enter_context(nc.sbuf_tensor(name, list(shape), dt))

    def ps(stack, name, shape, dt):
        return stack.enter_context(nc.psum_tensor(name, list(shape), dt))

    xmid_res = [Res("xmid%d" % i) for i in range(34)]

    with ExitStack() as SA:
        win_sb = sb(SA, "win_sb", [128, 8, DIN], BF16)
        wouta = sb(SA, "wouta", [128, 2, D], BF16)
        wouto = sb(SA, "wouto", [128, 6, D], BF16)
        mbA = sb(SA, "mbA", [128, 512], BF16)
        mbB = sb(SA, "mbB", [128, 512], BF16)
        mbR = sb(SA, "mbR", [128, 4, 32], BF16)
        mbL32 = sb(SA, "mbL32", [128, 32], BF16)
        mbL3 = sb(SA, "mbL3", [128, 32], BF16)
        identb = sb(SA, "identb", [128, 128], BF16)
        onesf = sb(SA, "onesf", [128, 128], F32)
        ones256 = sb(SA, "ones256", [128, 128], F32)
        flagb = sb(SA, "flagb", [128, 8], BF16)
        gmix_sb = sb(SA, "gmix_sb", [128, 8], F32)
        caw_sb = sb(SA, "caw_sb", [128, 2, 31], F32)
        cab_sb = sb(SA, "cab_sb", [128, 2], F32)
        lng_sb = sb(SA, "lng_sb", [128, 2], F32)
        lnb_sb = sb(SA, "lnb_sb", [128, 2], F32)
        R_w = Res("weights1")
        R_c = Res("consts1")

        with ExitStack() as S0:
            stg = [sb(S0, "stg%d" % i, [128, DIN], F32) for i in range(2)]
            R_stg = [Res("stg0"), Res("stg1")]
            cst = sb(S0, "cst", [128, 1024], F32)
            cst2 = sb(S0, "cst2", [128, 1024], F32)
            R_cst2 = Res("cst2")
            R_cst = Res("cst")
            for dst, src, n in ((gmix_sb, gmix, 8), (cab_sb, cab, 2), (lng_sb, lng, 2), (lnb_sb, lnb, 2)):
                r = Res("v")
                P.dma(dst[:, :], src[:, :], [], [r, R_c], r)
            r = Res("v")
            P.dma(caw_sb[:, :, :].rearrange("p a b -> p (a b)"), caw[:, :], [], [r, R_c], r)
            offs = 0
            def load_bias(dst_ap, src_ap, c0, n):
                P.dma(cst[:, c0:c0 + n], src_ap, [], [R_cst], R_cst)
                P.op("dve", _mk("tensor_scalar", out=dst_ap, in0=cst[:, c0:c0 + n], scalar1=30000.0, scalar2=-30000.0,
                                op0=ALU.mult, op1=ALU.add), [R_cst], [R_c])
            load_bias(mbA[:, :], m2d[:, 0:512], 0, 512)
            load_bias(mbB[:, :], m2d[:, 512:1024], 512, 512)
            load_bias(mbR[:, :, :].rearrange("p a b -> p (a b)"), maskrd[:, :], 0, 128)
            load_bias(mbL32[:, :], lw32d[:, :], 128, 32)
            load_bias(mbL3[:, :], lw3d[:, :], 160, 32)
            P.dma(cst[:, 416:544], identd[:, :], [], [R_cst], R_cst)
            P.op("dve", _mk("tensor_copy", out=identb[:, :], in_=cst[:, 416:544]), [R_cst], [R_c])
            P.dma(cst[:, 544:552], flagsd[:, :], [], [R_cst], R_cst)
            P.op("dve", _mk("tensor_copy", out=flagb[:, :], in_=cst[:, 544:552]), [R_cst], [R_c])
            P.op("pool", _mk("memset", onesf[:, :], 1.0), [], [R_c])
            P.op("pool", _mk("memset", ones256[:, :], 1.0 / 256.0), [], [R_c])
            R_woq = Res("woq")
            P.dma(wouta[:, :, :], wout[0:256, :].rearrange("(a p) d -> p a d", p=128), [], [], R_woq, q="pool")
            for c in range(0, 6, 2):
                P.dma(wouto[:, c:c + 2, :], wout[256 + c * 128:256 + (c + 2) * 128, :].rearrange("(a p) d -> p a d", p=128),
                      [], [], R_woq, q="pool")
            for c in range(8):
                s = c % 2
                P.dma(stg[s][:, :], win[c * 128:(c + 1) * 128, :], [], [R_stg[s]], R_stg[s])
                if c % 2 == 0:
                    P.op("act", _mk("activation", out=win_sb[:, c, :], in_=stg[s][:, :], func=AF.Copy,
                                                                 scale=gmix_sb[:, c:c + 1]), [R_stg[s], R_c], [])
                else:
                    P.op("dve", _mk("tensor_scalar", out=win_sb[:, c, :], in0=stg[s][:, :],
                                                                    scalar1=gmix_sb[:, c:c + 1], scalar2=None,
                                                                    op0=ALU.mult), [R_stg[s], R_c], [])
            P.barrier()
            P.flush()

        with ExitStack() as S1:
            xt = [sb(S1, "xt%d" % i, [128, D], F32) for i in range(2)]
            R_xt = [Res("xt0"), Res("xt1")]
            hb = sb(S1, "hb", [128, D], BF16)
            R_hb = Res("hb")
            junk = hb
            R_junk = R_hb
            ssq = sb(S1, "ssq", [128, 8], F32)
            R_ssq = Res("ssq")
            hT = sb(S1, "hT", [128, 8, TT], BF16)
            R_hT = Res("hT")
            QT = sb(S1, "QT", [128, 6, TT], BF16)
            R_QT = Res("QT")
            KT = [sb(S1, "KT%d" % i, [128, 6, TT], BF16) for i in range(2)]
            R_KT = [Res("KT0"), Res("KT1")]
            kt_cur = {"c": 0}
            K16r = sb(S1, "K16r", [128, 6, 16, 128], BF16)
            R_K16r = Res("K16r")
            KS = sb(S1, "KS", [128, 16, 64], BF16)
            R_KS = Res("KS")
            VS = sb(S1, "VS", [128, 16, 64], BF16)
            R_VS = Res("VS")
            VTall = sb(S1, "VTall", [128, 2, 6, TT], BF16)
            VT = [VTall[:, 0], VTall[:, 1]]
            R_VT = [Res("VT0"), Res("VT1")]
            V16r = sb(S1, "V16r", [128, 6, 16 * 2 * 65], BF16)
            R_V16r = [Res("V16r%d" % i) for i in range(6)]
            V16c = sb(S1, "V16c", [128, 16, 2, 65], BF16)
            R_V16c = Res("V16c")
            va1 = sb(S1, "va1", [128, 5, 2, 65], BF16)
            R_va1 = Res("va1")
            va4 = sb(S1, "va4", [128, 8, 2, 65], BF16)
            R_va4 = Res("va4")
            pt = [sb(S1, "pt%d" % i, [128, 512], BF16) for i in range(3)]
            R_pt = [Res("pt%d" % i) for i in range(3)]
            glu = sb(S1, "glu", [128, 2, 30 + TT], F32)
            R_glu = Res("glu")
            cacc = sb(S1, "cacc", [128, 2, TT], F32)
            R_cacc = Res("cacc")
            lnA = sb(S1, "lnA", [128, TT], F32)
            R_lnA = Res("lnA")
            lnB = sb(S1, "lnB", [128, TT], F32)
            R_lnB = Res("lnB")
            aT = sb(S1, "aT", [128, 2, TT], BF16)
            R_aT = Res("aT")
            OT = sb(S1, "OT", [128, 6, TT], BF16)
            R_OT = [Res("OT%d" % i) for i in range(6)]
            mbRc = sb(S1, "mbRc", [128, 512], BF16)
            mbLc = sb(S1, "mbLc", [128, 512], BF16)
            R_mbc = Res("mbc")
            otmp = sb(S1, "otmp", [64, TT], BF16)
            R_otmp = Res("otmp")
            numsb = sb(S1, "numsb", [64, TT], F32)
            R_numsb = Res("numsb")
            rden = sb(S1, "rden", [65, TT], F32)
            R_rden = Res("rden")
            kvo = lnA
            R_kvo = R_lnA

            pj = [ps(S1, "pj%d" % i, [128, 512], F32) for i in range(2)]
            R_pj = [Res("pj0"), Res("pj1")]
            sc = [ps(S1, "sc%d" % i, [128, 512], F32) for i in range(3)]
            R_sc = [Res("sc0"), Res("sc1"), Res("sc2")]
            ac = [ps(S1, "ac%d" % i, [128, 512], F32) for i in range(2)]
            R_ac = [Res("ac0"), Res("ac1")]
            tr = [ps(S1, "tr%d" % i, [128, 1024], BF16) for i in range(1)]
            R_tr = [Res("tr0")]

            st = {"pj": 0, "sc": 0, "ac": 0, "tr": 0, "pt": 0, "xt": 0}

            def nxt(k, n):
                v = st[k]
                st[k] = (v + 1) % n
                return v

            P.op("pool", _mk("memset", K16r[:, :, :, :].rearrange("p a b c -> p (a b c)"), 0.0), [], [R_K16r])
            for q_ in range(2):
                P.op("pool", _mk("memset", KT[q_][:, :, :].rearrange("p a b -> p (a b)"), 0.0), [], [R_KT[q_]])
            P.op("pool", _mk("memset", KS[:, :, :].rearrange("p a b -> p (a b)"), 0.0), [], [R_KS])
            P.op("pool", _mk("memset", VS[:, :, :].rearrange("p a b -> p (a b)"), 0.0), [], [R_VS])
            P.op("pool", _mk("memset", V16r[:, :, :].rearrange("p a b -> p (a b)"), 0.0), [], R_V16r)
            P.op("pool", _mk("memset", glu[:, :, :].rearrange("p a b -> p (a b)"), 0.0), [], [R_glu])
            P.op("pool", _mk("memset", VTall[:, :, :, :].rearrange("p a b c -> p (a b c)"), 0.0), [], R_VT)
            P.op("pool", _mk("memset", V16c[:, :, :, :].rearrange("p a b c -> p (a b c)"), 0.0), [], [R_V16c])
            P.op("pool", _mk("memset", va1[:, :, :, :].rearrange("p a b c -> p (a b c)"), 0.0), [], [R_va1])
            P.op("pool", _mk("memset", va4[:, :, :, :].rearrange("p a b c -> p (a b c)"), 0.0), [], [R_va4])

            def norm_transpose(src_rows_ap, j, ntok_cols=TT, hT_=None, R_hT_=None):
                hT_ = hT if hT_ is None else hT_
                R_hT_ = R_hT if R_hT_ is None else R_hT_
                b = nxt("xt", 2)
                P.dma(xt[b][:, :], src_rows_ap, [], [R_xt[b]], R_xt[b])
                P.op("act", _mk("activation", out=junk[:, :], in_=xt[b][:, :], func=AF.Square,
                                                   accum_out=ssq[:, 0:1]), [R_xt[b]], [R_junk, R_ssq])
                P.op("dve", _mk("tensor_scalar", out=ssq[:, 1:2], in0=ssq[:, 0:1], scalar1=1.0 / D, scalar2=EPS,
                                                      op0=ALU.mult, op1=ALU.add), [R_ssq], [R_ssq])
                P.op("act", _mk("activation", out=ssq[:, 3:4], in_=ssq[:, 1:2], func=AF.Ln), [R_ssq], [R_ssq])
                P.op("act", _mk("activation", out=ssq[:, 2:3], in_=ssq[:, 3:4], func=AF.Exp, scale=-0.5), [R_ssq], [R_ssq])
                P.op("act", _mk("activation", out=hb[:, :], in_=xt[b][:, :], func=AF.Copy, scale=ssq[:, 2:3]),
                     [R_xt[b], R_ssq], [R_hb])
                t = nxt("tr", 1)
                for c in range(8):
                    P.op("pe", _mk("transpose", out=tr[t][:, c * 128:(c + 1) * 128],
                                                          in_=hb[:, c * 128:(c + 1) * 128], identity=identb[:, :]),
                         [R_hb, R_c], [R_tr[t]], inc=(c == 7))
                P.op("dve", _mk("tensor_copy", out=hT_[:, :, 128 * j:128 * j + 128],
                                                    in_=tr[t][:, :].rearrange("p (c k) -> p c k", c=8)),
                     [R_tr[t]], [R_hT_])
                return b

            def proj_fm(fchunk, ncols=TT):
                pb = nxt("pj", 2)
                for c in range(8):
                    P.op("pe", _mk("matmul", pj[pb][:, 0:ncols], win_sb[:, c, fchunk * 128:(fchunk + 1) * 128],
                                                       hT[:, c, 0:ncols], start=(c == 0), stop=(c == 7)),
                         [R_w, R_hT], [R_pj[pb]], inc=(c == 7))
                return pb

            def evac(pb, dst_ap, R_dst, which):
                if which == "act":
                    P.op("act", _mk("activation", out=dst_ap, in_=pj[pb][:, 0:dst_ap.shape[-1]], func=AF.Copy),
                         [R_pj[pb]], [R_dst])
                else:
                    P.op("dve", _mk("tensor_copy", out=dst_ap, in_=pj[pb][:, 0:dst_ap.shape[-1]]),
                         [R_pj[pb]], [R_dst])

            def kv_project(par):
                for hp in range(6):
                    pb = proj_fm(10 + hp)
                    evac(pb, KT[kt_cur["c"]][:, hp, :], R_KT[kt_cur["c"]], "act" if hp % 2 == 0 else "dve")
                for hp in range(6):
                    pb = proj_fm(16 + hp)
                    evac(pb, VT[par][:, hp, :], R_VT[par], "dve" if hp % 2 == 0 else "act")

            def v16_build(par, hp, j):
                js = slice(64, 128) if j == 3 else slice(32 * j, 32 * j + 32)
                if j == 3:
                    P.op("pool", _mk("tensor_copy", out=VS[:, :, 0:32],
                                     in_=VT[1 - par][:, hp, :].rearrange("p (u r) -> p r u", r=16)), [R_VT[1 - par]], [R_VS])
                    P.op("pool", _mk("tensor_copy", out=VS[:, :, 32:64],
                                     in_=VT[par][:, hp, :].rearrange("p (u r) -> p r u", r=16)), [R_VT[par]], [R_VS])
                for half in range(2):
                    t = nxt("tr", 1)
                    for rr in range(8):
                        r = half * 8 + rr
                        if j == 3:
                            src_ = VS[:, r, :]
                            rds = [R_VS, R_c]
                        else:
                            src_ = VT[par][:, hp, r::16]
                            rds = [R_VT[par], R_c]
                        P.op("pe", _mk("transpose", out=tr[t][js, rr * 128:(rr + 1) * 128], in_=src_, identity=identb[:, :]),
                             rds, [R_tr[t]], inc=(rr == 7))
                    P.op("dve", _mk("tensor_copy",
                        out=V16c[js, half * 8:half * 8 + 8, :, 0:64],
                        in_=tr[t][js, :].rearrange("p (r h d) -> p r h d", r=8, h=2)),
                        [R_tr[t]], [R_V16c])

            def ring_insert_v(hp, j):
                js = slice(64, 128) if j == 3 else slice(32 * j, 32 * j + 32)
                P.op("act", _mk("activation", out=V16r[js, hp, :],
                                in_=V16c[js, :, :, :].rearrange("p a b c -> p (a b c)"), func=AF.Copy),
                     [R_V16c], [R_V16r[hp]])

            def set_aug(i):
                fl = slice(0, 2)
                on = slice(4, 6)
                src = flagb[:, fl] if i < 0 else flagb[:, on]
                P.op("pool", _mk("tensor_copy", out=V16c[:, :, :, 64:65],
                                                     in_=src.unsqueeze(1).unsqueeze(3).broadcast_to([128, 16, 2, 1])),
                     [R_c], [R_V16c])
                if i >= -1:
                    s_prev = flagb[:, fl] if i <= 0 else flagb[:, on]
                    s_cur = flagb[:, fl] if i < 0 else flagb[:, on]
                    P.op("pool", _mk("tensor_copy", out=va1[:, 0:1, :, 64:65],
                                                         in_=s_prev.unsqueeze(1).unsqueeze(3)), [R_c], [R_va1])
                    P.op("pool", _mk("tensor_copy", out=va1[:, 1:5, :, 64:65],
                                                         in_=s_cur.unsqueeze(1).unsqueeze(3).broadcast_to([128, 4, 2, 1])),
                         [R_c], [R_va1])
                    P.op("pool", _mk("tensor_copy", out=va4[:, 0:4, :, 64:65],
                                                         in_=s_prev.unsqueeze(1).unsqueeze(3).broadcast_to([128, 4, 2, 1])),
                         [R_c], [R_va4])
                    P.op("pool", _mk("tensor_copy", out=va4[:, 4:8, :, 64:65],
                                                         in_=s_cur.unsqueeze(1).unsqueeze(3).broadcast_to([128, 4, 2, 1])),
                         [R_c], [R_va4])

            def exp_mask(sb_, ncol0, ncol1, mask_ap, prow=slice(0, 128)):
                pi = nxt("pt", 3)
                P.op("act", _mk("activation", out=pt[pi][prow, ncol0:ncol1], in_=sc[sb_][prow, ncol0:ncol1],
                                                   func=AF.Exp, scale=SCALE), [R_sc[sb_]], [R_pt[pi]])
                pv_ = pt[pi][prow, ncol0:ncol1]
                if mask_ap.ndim == 3:
                    pv_ = pv_.rearrange("p (a b) -> p a b", a=mask_ap.shape[1])
                P.op("pool", _mk("tensor_tensor", out=pv_, in0=pv_, in1=mask_ap, op=ALU.mult), [R_pt[pi], R_c], [R_pt[pi]])
                return pi

            def attention(i, par, hooks=None):
                hooks = hooks or {}
                j = i % 4
                ppar = 1 - par
                js = slice(64, 128) if j == 3 else slice(32 * j, 32 * j + 32)
                nk = 64 if j == 3 else 32
                DEPTH = 2
                steps = []
                P.op("pool", _mk("tensor_copy", out=mbRc[:, :].rearrange("p (a b) -> p a b", a=16),
                                 in_=mbR[:, j, :].unsqueeze(1).broadcast_to([128, 16, 32])), [R_c], [R_mbc])
                mb__ = mbL3 if j == 3 else mbL32
                P.op("pool", _mk("tensor_copy", out=mbLc[:, :].rearrange("p (a b) -> p a b", a=16),
                                 in_=mb__[:, :].unsqueeze(1).broadcast_to([128, 16, 32])), [R_c], [R_mbc])

                def build_v(hp):
                    t = nxt("tr", 1)
                    for bi in range(5):
                        src = VT[ppar][:, hp, 384:512] if bi == 0 else VT[par][:, hp, 128 * (bi - 1):128 * bi]
                        rr_ = [R_VT[ppar]] if bi == 0 else [R_VT[par]]
                        P.op("pe", _mk("transpose", out=tr[t][:, bi * 128:(bi + 1) * 128], in_=src, identity=identb[:, :]),
                             rr_ + [R_c], [R_tr[t]], inc=(bi == 4))
                    P.op("act", _mk("activation", out=va1[:, :, :, 0:64],
                                    in_=tr[t][:, 0:640].rearrange("p (b h d) -> p b h d", b=5, h=2), func=AF.Copy),
                         [R_tr[t]], [R_va1])
                    t = nxt("tr", 1)
                    for bi in range(8):
                        r = bi % 4
                        src = VT[ppar][:, hp, r::4] if bi < 4 else VT[par][:, hp, r::4]
                        rr_ = [R_VT[ppar]] if bi < 4 else [R_VT[par]]
                        P.op("pe", _mk("transpose", out=tr[t][:, bi * 128:(bi + 1) * 128], in_=src, identity=identb[:, :]),
                             rr_ + [R_c], [R_tr[t]], inc=(bi == 7))
                    P.op("act", _mk("activation", out=va4[:, :, :, 0:64],
                                    in_=tr[t][:, :].rearrange("p (b h d) -> p b h d", b=8, h=2), func=AF.Copy),
                         [R_tr[t]], [R_va4])
                    v16_build(par, hp, j)

                def add_head(hp, hh):
                    hs = slice(64 * hh, 64 * hh + 64)
                    a_box = [None]
                    started = [False]

                    def pv(lhsT, rhs, out_ap, reads, last=False, tp=None):
                        stt = not started[0]
                        started[0] = True
                        a = a_box[0]
                        P.op("pe", _mk("matmul", out_ap, lhsT, rhs, start=stt, stop=True, skip_group_check=True,
                                       tile_position=tp), reads, [R_ac[a]], inc=last)

                    def maskmm(s, bias_ap, rows=slice(0, 128)):
                        tp = None if rows.start == 0 and rows.stop == 128 else (rows.start, rows.start)
                        P.op("pe", _mk("matmul", sc[s][rows, 0:512], identb[rows, rows], bias_ap, start=True, stop=False,
                                       skip_group_check=True, tile_position=tp), [R_c, R_mbc], [R_sc[s]], inc=False)

                    def qkmm(s, cols, kap, qap, reads, last, rows=slice(0, 128), tp=None):
                        P.op("pe", _mk("matmul", sc[s][rows, cols], kap, qap, start=False, stop=True, skip_group_check=True,
                                       tile_position=tp), reads, [R_sc[s]], inc=last)

                    def expo(s, pi, rows=slice(0, 128)):
                        P.op("act", _mk("activation", out=pt[pi][rows, :], in_=sc[s][rows, 0:512], func=AF.Exp, scale=SCALE),
                             [R_sc[s]], [R_pt[pi]])

                    def qk1(s):
                        maskmm(s, mbA[:, :])
                        qkmm(s, slice(0, 128), KT[1 - kt_cur["c"]][hs, hp, 384:512], QT[hs, hp, 0:128], [R_KT[1 - kt_cur["c"]], R_QT], False)
                        qkmm(s, slice(128, 384), KT[kt_cur["c"]][hs, hp, 0:128], QT[hs, hp, 0:256], [R_KT[kt_cur["c"]], R_QT], False)
                        qkmm(s, slice(384, 512), KT[kt_cur["c"]][hs, hp, 384:512], QT[hs, hp, 384:512], [R_KT[kt_cur["c"]], R_QT], True)

                    def rest1(s, pi):
                        a_box[0] = nxt("ac", 2)
                        a = a_box[0]
                        expo(s, pi)
                        rd = [R_va1, R_pt[pi]]
                        pv(va1[:, 0, hh, :], pt[pi][:, 0:128], ac[a][0:65, 0:128], rd)
                        pv(va1[:, 1, hh, :], pt[pi][:, 128:256], ac[a][0:65, 0:128], rd)
                        pv(va1[:, 1, hh, :], pt[pi][:, 256:384], ac[a][0:65, 128:256], rd)
                        pv(va1[:, 4, hh, :], pt[pi][:, 384:512], ac[a][0:65, 384:512], rd, last=True)

                    def qk2(s):
                        maskmm(s, mbB[:, :])
                        qkmm(s, slice(0, 256), KT[kt_cur["c"]][hs, hp, 128:256], QT[hs, hp, 128:384], [R_KT[kt_cur["c"]], R_QT], False)
                        qkmm(s, slice(256, 512), KT[kt_cur["c"]][hs, hp, 256:384], QT[hs, hp, 256:512], [R_KT[kt_cur["c"]], R_QT], True)

                    def rest2(s, pi):
                        a = a_box[0]
                        expo(s, pi)
                        rd = [R_va1, R_pt[pi]]
                        pv(va1[:, 2, hh, :], pt[pi][:, 0:128], ac[a][0:65, 128:256], rd)
                        pv(va1[:, 2, hh, :], pt[pi][:, 128:256], ac[a][0:65, 256:384], rd)
                        pv(va1[:, 3, hh, :], pt[pi][:, 256:384], ac[a][0:65, 256:384], rd)
                        pv(va1[:, 3, hh, :], pt[pi][:, 384:512], ac[a][0:65, 384:512], rd, last=True)

                    def mk4(r0):
                        def qk(s):
                            maskmm(s, mbB[:, :])
                            for q_, r in enumerate((r0, r0 + 1)):
                                qkmm(s, slice(256 * q_, 256 * q_ + 128), KT[kt_cur["c"]][hs, hp, r::4], QT[hs, hp, r::4], [R_KT[kt_cur["c"]], R_QT], False)
                                qkmm(s, slice(256 * q_ + 128, 256 * q_ + 256), KT[1 - kt_cur["c"]][hs, hp, r::4], QT[hs, hp, r::4],
                                     [R_KT[1 - kt_cur["c"]], R_QT], q_ == 1)

                        def rest(s, pi):
                            a = a_box[0]
                            expo(s, pi)
                            rd = [R_va4, R_pt[pi]]
                            for q_, r in enumerate((r0, r0 + 1)):
                                pv(va4[:, 4 + r, hh, :], pt[pi][:, 256 * q_:256 * q_ + 128], ac[a][0:65, r::4], rd)
                                pv(va4[:, r, hh, :], pt[pi][:, 256 * q_ + 128:256 * q_ + 256], ac[a][0:65, r::4], rd, last=(q_ == 1))
                        return qk, rest

                    def qk5(s):
                        maskmm(s, mbRc[:, :])
                        for r in range(16):
                            qkmm(s, slice(32 * r, 32 * r + 32), K16r[hs, hp, r, :], QT[hs, hp, r::16], [R_K16r, R_QT], r == 15)

                    def rest5(s, pi):
                        a = a_box[0]
                        expo(s, pi)
                        v16 = V16r[:, hp, :].rearrange("p (r h d) -> p r h d", r=16, h=2)
                        for r in range(16):
                            pv(v16[:, r, hh, :], pt[pi][:, 32 * r:32 * r + 32], ac[a][0:65, r::16], [R_V16r[hp], R_pt[pi]],
                               last=(r == 15))

                    def qk6(s):
                        if j == 3 and hh == 0:
                            P.op("pool", _mk("tensor_copy", out=KS[:, :, 32:64],
                                             in_=KT[kt_cur["c"]][:, hp, :].rearrange("p (u r) -> p r u", r=16)), [R_KT[kt_cur["c"]]], [R_KS])
                        P.op("pe", _mk("matmul", sc[s][js, 0:512], identb[hs, 64 * hh:64 * hh + nk], mbLc[hs, :], start=True,
                                       stop=False, skip_group_check=True, tile_position=(64 * hh, js.start)),
                             [R_c, R_mbc], [R_sc[s]], inc=False)
                        for r in range(16):
                            kap_ = KS[hs, r, :] if j == 3 else KT[kt_cur["c"]][hs, hp, r::16]
                            qkmm(s, slice(32 * r, 32 * r + 32), kap_, QT[hs, hp, r::16], [R_KT[kt_cur["c"]], R_KS, R_QT], r == 15,
                                 rows=js, tp=(64 * hh, js.start))

                    def rest6(s, pi):
                        a = a_box[0]
                        expo(s, pi, rows=js)
                        for r in range(16):
                            pv(V16c[js, r, hh, :], pt[pi][js, 32 * r:32 * r + 32], ac[a][0:65, r::16], [R_V16c, R_pt[pi]],
                               last=(r == 15), tp=(js.start, 0))

                    def norm_a():
                        a = a_box[0]
                        P.op("act", _mk("activation", out=numsb[:, :], in_=ac[a][0:64, :], func=AF.Copy), [R_ac[a]], [R_numsb])
                        P.op("dve", _mk("tensor_scalar", out=rden[64:65, :], in0=ac[a][64:65, :], scalar1=1e-18,
                                        scalar2=None, op0=ALU.add), [R_ac[a]], [R_rden])
                        P.op("act", _mk("activation", out=rden[64:65, :], in_=rden[64:65, :], func=AF.Ln), [R_rden], [R_rden])
                        P.op("act", _mk("activation", out=rden[64:65, :], in_=rden[64:65, :], func=AF.Exp, scale=-1.0),
                             [R_rden], [R_rden])
                        if hh == 1:
                            ring_insert_v(hp, j)

                    def norm_b():
                        pb = nxt("pj", 2)
                        P.op("pe", _mk("matmul", pj[pb][0:64, :], onesf[64:65, 0:64], rden[64:65, :], start=True, stop=True),
                             [R_rden, R_c], [R_pj[pb]])
                        if hh == 0:
                            P.op("dve", _mk("tensor_tensor", out=OT[0:64, hp, :], in0=numsb[:, :], in1=pj[pb][0:64, :],
                                            op=ALU.mult), [R_numsb, R_pj[pb]], [R_OT[hp]])
                        else:
                            P.op("dve", _mk("tensor_tensor", out=otmp[:, :], in0=numsb[:, :], in1=pj[pb][0:64, :],
                                            op=ALU.mult), [R_numsb, R_pj[pb]], [R_otmp])
                            P.dma(OT[64:128, hp, :], otmp[:, :], [R_otmp], [R_OT[hp]], R_otmp)

                    q3, r3 = mk4(0)
                    q4, r4 = mk4(2)
                    pre = (lambda: build_v(hp)) if hh == 0 else None
                    steps.append(dict(qk=qk1, rest=rest1, pre=pre, post=None, post2=None))
                    steps.append(dict(qk=qk2, rest=rest2, pre=None, post=None, post2=None))
                    steps.append(dict(qk=q3, rest=r3, pre=None, post=None, post2=None))
                    steps.append(dict(qk=q4, rest=r4, pre=None, post=None, post2=None))
                    steps.append(dict(qk=qk5, rest=rest5, pre=None, post=None, post2=None))
                    steps.append(dict(qk=qk6, rest=rest6, pre=None, post=norm_a, post2=norm_b))

                for hp in range(6):
                    for hh in range(2):
                        add_head(hp, hh)
                N = len(steps)
                banks = [None] * N
                deferred = {}
                for n in range(N + DEPTH + 4):
                    if n < N:
                        banks[n] = nxt("sc", 3)
                        steps[n]["qk"](banks[n])
                    m = n - DEPTH
                    if 0 <= m < N:
                        stp = steps[m]
                        if stp["pre"] is not None:
                            stp["pre"]()
                        pi = nxt("pt", 3)
                        stp["rest"](banks[m], pi)
                        if stp["post"] is not None:
                            stp["post"]()
                            deferred[m + 3] = stp["post2"]
                    if m in deferred:
                        deferred.pop(m)()
                    for h_ in hooks.pop(n, []):
                        h_()
                assert not deferred and not hooks

            def conv_a(i):
                for f in range(2):
                    pbg = proj_fm(2 + f)
                    P.op("act", _mk("activation", out=lnB[:, :], in_=pj[pbg][:, :], func=AF.Sigmoid),
                         [R_pj[pbg]], [R_lnB])
                    pbv = proj_fm(f)
                    P.op("dve", _mk("tensor_tensor", out=glu[:, f, 30:30 + TT], in0=pj[pbv][:, :], in1=lnB[:, :],
                                                          op=ALU.mult), [R_pj[pbv], R_lnB], [R_glu])

            def conv_taps(f, k0, k1):
                for k in range(k0, k1):
                    if k == 0:
                        P.op("dve", _mk("tensor_scalar", out=cacc[:, f, :], in0=glu[:, f, 0:TT], scalar1=caw_sb[:, f, 0:1],
                                        scalar2=cab_sb[:, f:f + 1], op0=ALU.mult, op1=ALU.add), [R_glu, R_c], [R_cacc])
                    else:
                        P.op("dve", _mk("scalar_tensor_tensor", out=cacc[:, f, :], in0=glu[:, f, k:k + TT],
                                        scalar=caw_sb[:, f, k:k + 1], in1=cacc[:, f, :], op0=ALU.mult, op1=ALU.add),
                             [R_glu, R_c, R_cacc], [R_cacc])

            def conv_hist():
                P.op("pool", _mk("tensor_copy", out=glu[:, :, 0:30], in_=glu[:, :, TT:TT + 30]), [R_glu], [R_glu])

            def conv_b(i):
                pm = nxt("pj", 2)
                for f in range(2):
                    P.op("pe", _mk("matmul", pj[pm][:, :], ones256[:, :], cacc[:, f, :], start=(f == 0), stop=(f == 1)),
                         [R_cacc, R_c], [R_pj[pm]], inc=(f == 1))
                P.op("act", _mk("activation", out=lnA[:, :], in_=cacc[:, 0, :], func=AF.Square), [R_cacc], [R_lnA])
                P.op("act", _mk("activation", out=lnB[:, :], in_=cacc[:, 1, :], func=AF.Square), [R_cacc], [R_lnB])
                pq = nxt("pj", 2)
                P.op("pe", _mk("matmul", pj[pq][:, :], ones256[:, :], lnA[:, :], start=True, stop=False),
                     [R_lnA, R_c], [R_pj[pq]], inc=False)
                P.op("pe", _mk("matmul", pj[pq][:, :], ones256[:, :], lnB[:, :], start=False, stop=True),
                     [R_lnB, R_c], [R_pj[pq]])
                P.op("act", _mk("activation", out=lnA[:, :], in_=pj[pm][:, :], func=AF.Square), [R_pj[pm]], [R_lnA])
                P.op("dve", _mk("tensor_tensor", out=lnA[:, :], in0=pj[pq][:, :], in1=lnA[:, :], op=ALU.subtract),
                     [R_pj[pq], R_lnA], [R_lnA])
                P.op("dve", _mk("tensor_scalar", out=lnA[:, :], in0=lnA[:, :], scalar1=EPS, scalar2=None, op0=ALU.add),
                     [R_lnA], [R_lnA])
                P.op("act", _mk("activation", out=lnA[:, :], in_=lnA[:, :], func=AF.Ln), [R_lnA], [R_lnA])
                P.op("act", _mk("activation", out=lnA[:, :], in_=lnA[:, :], func=AF.Exp, scale=-0.5), [R_lnA], [R_lnA])
                for f in range(2):
                    P.op("dve", _mk("tensor_tensor", out=lnB[:, :], in0=cacc[:, f, :], in1=pj[pm][:, :], op=ALU.subtract),
                         [R_cacc, R_pj[pm]], [R_lnB])
                    P.op("dve", _mk("tensor_tensor", out=lnB[:, :], in0=lnB[:, :], in1=lnA[:, :], op=ALU.mult),
                         [R_lnB, R_lnA], [R_lnB])
                    P.op("act", _mk("activation", out=aT[:, f, :], in_=lnB[:, :], func=AF.Silu,
                                                       scale=lng_sb[:, f:f + 1], bias=lnb_sb[:, f:f + 1]),
                         [R_lnB, R_c], [R_aT])

            def out_proj(i):
                blocks = [3] if i == -1 else [0, 1, 2, 3]
                for jb in blocks:
                    b = nxt("xt", 2)
                    row0 = (i + NHALO + NPRE) * TT + 128 * jb
                    P.dma(xt[b][:, :], xp[row0:row0 + 128, :], [], [R_xt[b]], R_xt[b])
                    for nh in range(2):
                        pb = nxt("pj", 2)
                        for f in range(2):
                            P.op("pe", _mk("matmul", pj[pb][:, :], aT[:, f, 128 * jb:128 * jb + 128],
                                                               wouta[:, f, 512 * nh:512 * nh + 512], start=(f == 0), stop=False),
                                 [R_aT, R_w], [R_pj[pb]], inc=False)
                        for hp in range(6):
                            P.op("pe", _mk("matmul", pj[pb][:, :], OT[:, hp, 128 * jb:128 * jb + 128],
                                                                 wouto[:, hp, 512 * nh:512 * nh + 512], start=False,
                                                                 stop=(hp == 5)),
                                 [R_OT[hp], R_w], [R_pj[pb]], inc=(hp == 5))
                        P.op("dve", _mk("tensor_tensor", out=xt[b][:, 512 * nh:512 * nh + 512],
                                                              in0=xt[b][:, 512 * nh:512 * nh + 512], in1=pj[pb][:, :],
                                                              op=ALU.add), [R_xt[b], R_pj[pb]], [R_xt[b]])
                    blk = 0 if i == -1 else 1 + 4 * i + jb
                    P.dma(xmid[blk * 128:(blk + 1) * 128, :], xt[b][:, :], [R_xt[b]], [xmid_res[blk]], R_xt[b])

            def kv_outputs(i):
                for jb in range(4):
                    row0 = (i - 4) * TT + 128 * jb
                    for g in range(3):
                        pb = nxt("pj", 2)
                        for c in range(8):
                            P.op("pe", _mk("matmul", pj[pb][:, :], hT[:, c, 128 * jb:128 * jb + 128],
                                                               win_sb[:, c, 1280 + 512 * g:1280 + 512 * g + 512],
                                                               start=(c == 0), stop=(c == 7)),
                                 [R_hT, R_w], [R_pj[pb]], inc=(c == 7))
                        P.op("act", _mk("activation", out=kvo[:, :], in_=pj[pb][:, :], func=AF.Copy), [R_pj[pb]], [R_kvo])
                        if g == 0:
                            out_tickets.append(P.dma(kwin[row0:row0 + 128, 0:512], kvo[:, :], [R_kvo], [], R_kvo))
                        elif g == 1:
                            out_tickets.append(P.dma(kwin[row0:row0 + 128, 512:768], kvo[:, 0:256], [R_kvo], [], R_kvo))
                            out_tickets.append(P.dma(vwin[row0:row0 + 128, 0:256], kvo[:, 256:512], [R_kvo], [], R_kvo))
                        else:
                            out_tickets.append(P.dma(vwin[row0:row0 + 128, 256:768], kvo[:, :], [R_kvo], [], R_kvo))

            def conv_a_output():
                pb = nxt("pj", 2)
                for c in range(8):
                    P.op("pe", _mk("matmul", pj[pb][:, :], hT[:, c, 384:512], win_sb[:, c, 0:512],
                                                       start=(c == 0), stop=(c == 7)), [R_hT, R_w], [R_pj[pb]], inc=(c == 7))
                P.op("act", _mk("activation", out=kvo[:, 256:512], in_=pj[pb][:, 256:512], func=AF.Sigmoid),
                     [R_pj[pb]], [R_kvo])
                P.op("dve", _mk("tensor_tensor", out=kvo[:, 0:256], in0=pj[pb][:, 0:256], in1=kvo[:, 256:512], op=ALU.mult),
                     [R_pj[pb], R_kvo], [R_kvo])
                out_tickets.append(P.dma(cap[:, :], kvo[98:128, 0:256], [R_kvo], [], R_kvo))

            def prep_block(ti_, jb):
                norm_transpose(xp[ti_ * TT + 128 * jb:ti_ * TT + 128 * jb + 128, :], jb)

            for jb in range(4):
                prep_block(0, jb)
            for ti in range(NT):
                i = ti - (NHALO + NPRE)
                par = (ti + 1) % 2
                kt_cur["c"] = par
                j = i % 4
                if i in (-(NHALO + NPRE), -1, 0, 1):
                    set_aug(i)
                if i < -1:
                    kv_project(par)
                    for jb in range(4):
                        prep_block(ti + 1, jb)
                    for hp in range(6):
                        v16_build(par, hp, j)
                        ring_insert_v(hp, j)
                else:
                    conv_a(i)
                    for hp in range(6):
                        pb = proj_fm(4 + hp)
                        evac(pb, QT[:, hp, :], R_QT, "act" if hp % 2 == 0 else "dve")
                    kv_project(par)
                    if i >= 4:
                        kv_outputs(i)
                    if i == NMAIN - 1:
                        conv_a_output()
                    hooks = {}
                    hk = 1
                    for f in range(2):
                        for k0 in range(0, 31, 8):
                            hooks.setdefault(hk, []).append(lambda f=f, k0=k0: conv_taps(f, k0, min(31, k0 + 8)))
                            hk += 2
                    hooks.setdefault(hk, []).append(conv_hist)
                    hooks.setdefault(hk + 1, []).append(lambda i=i: conv_b(i))
                    if ti + 1 < NT:
                        for jb in range(4):
                            hooks.setdefault(24 + 12 * jb, []).append(lambda ti=ti, jb=jb: prep_block(ti + 1, jb))
                    attention(i, par, hooks)
                    out_proj(i)
                for hp in range(6):
                    P.op("pool", _mk("tensor_copy", out=K16r[:, hp, :, 32 * j:32 * j + 32],
                                     in_=KT[kt_cur["c"]][:, hp, :].rearrange("p (u r) -> p r u", r=16)), [R_KT[kt_cur["c"]]], [R_K16r])
            P.barrier()
            P.flush()

        with ExitStack() as SS:
            xts = sb(SS, "xts", [128, D], F32)
            R_xts = Res("xts")
            hbs = sb(SS, "hbs", [128, D], BF16)
            R_hbs = Res("hbs")
            sqs = sb(SS, "sqs", [128, 8], F32)
            R_sqs = Res("sqs")
            hTs = sb(SS, "hTs", [128, 8, 128], BF16)
            R_hTs = Res("hTs")
            tmS = sb(SS, "tmS", [128, DIN], F32)
            R_tmS = Res("tmS")
            VN = sb(SS, "VN", [128, DATT], BF16)
            R_VN = Res("VN")
            gluS = sb(SS, "gluS", [128, 512], F32)
            R_gluS = Res("gluS")
            gluSF = sb(SS, "gluSF", [128, 2, 16, 38], F32)
            R_gluSF = Res("gluSF")
            scaS = sb(SS, "scaS", [128, 4, DCONV], F32)
            R_scaS = Res("scaS")
            caccS = sb(SS, "caccS", [128, 2, 128], F32)
            R_caccS = Res("caccS")
            csqS = sb(SS, "csqS", [128, 2, 128], F32)
            R_csqS = Res("csqS")
            lnAs = sb(SS, "lnAs", [128, 128], F32)
            R_lnAs = Res("lnAs")
            lnBs = sb(SS, "lnBs", [128, 128], F32)
            R_lnBs = Res("lnBs")
            aTs = sb(SS, "aTs", [128, 2, 128], BF16)
            R_aTs = Res("aTs")
            Qz = sb(SS, "Qz", [128, 6, 16, 2, 8], BF16)
            R_Qz = Res("Qz")
            KNT = sb(SS, "KNT", [128, 6, 128], BF16)
            R_KNT = Res("KNT")
            KA = [sb(SS, "KA%d" % i, [128, 8, DATT], BF16) for i in range(2)]
            KB = [sb(SS, "KB%d" % i, [128, 4, DATT], BF16) for i in range(2)]
            VA = [sb(SS, "VA%d" % i, [128, 8, DATT], BF16) for i in range(2)]
            VB = [sb(SS, "VB%d" % i, [128, 4, DATT], BF16) for i in range(2)]
            R_KA = [Res("KA0"), Res("KA1")]
            R_KB = [Res("KB0"), Res("KB1")]
            R_VA = [Res("VA0"), Res("VA1")]
            R_VB = [Res("VB0"), Res("VB1")]
            KTs = [sb(SS, "KTs%d" % i, [128, 12, 128], BF16) for i in range(2)]
            R_KTs = [Res("KTs0"), Res("KTs1")]
            pts = [sb(SS, "pts%d" % i, [128, 96], BF16) for i in range(2)]
            R_pts = [Res("pts0"), Res("pts1")]
            smf = sb(SS, "smf", [128, 16 * 96], F32)
            R_smf = Res("smf")
            smb = sb(SS, "smb", [128, 16, 96], BF16)
            onesb = sb(SS, "onesb", [128, 128], BF16)
            identf = sb(SS, "identf", [128, 128], F32)
            R_cs = Res("consts_s")
            OTs = sb(SS, "OTs", [128, 6, 128], BF16)
            R_OTs = Res("OTs")
            tmpo = sb(SS, "tmpo", [128, 96], F32)
            R_tmpo = Res("tmpo")
            rds = sb(SS, "rds", [128, 96], F32)
            R_rds = Res("rds")

            pjs = [ps(SS, "pjs%d" % i, [128, 512], F32) for i in range(2)]
            R_pjs = [Res("pjs0"), Res("pjs1")]
            trs = [ps(SS, "trs%d" % i, [128, 1024], BF16) for i in range(2)]
            R_trs = [Res("trs0"), Res("trs1")]
            scs = [ps(SS, "scs%d" % i, [128, 512], F32) for i in range(2)]
            R_scs = [Res("scs0"), Res("scs1")]
            nums = ps(SS, "nums", [128, 512], F32)
            R_nums = Res("nums")
            dens = ps(SS, "dens", [128, 512], F32)
            R_dens = Res("dens")
            sts = {"pj": 0, "tr": 0, "sc": 0, "pt": 0, "kt": 0}

            def nxs(k, n=2):
                v = sts[k]
                sts[k] = (v + 1) % n
                return v

            P.dma(smf[:, :], smaskd[:, :], [], [R_smf], R_smf)
            P.op("dve", _mk("tensor_copy", out=smb[:, :, :].rearrange("p a b -> p (a b)"), in_=smf[:, :]), [R_smf], [R_cs])
            P.op("pool", _mk("memset", onesb[:, :], 1.0), [], [R_cs])
            P.dma(identf[:, :], identd[:, :], [], [R_cs], R_cs)
            P.op("pool", _mk("memset", Qz[:, :, :, :, :].rearrange("p a b c d -> p (a b c d)"), 0.0), [], [R_Qz])

            def load_cache(s):
                b = s % 2
                ka = ck[s].rearrange("(g q) f -> g q f", q=16)
                va = cv[s].rearrange("(g q) f -> g q f", q=16)
                P.dma(KA[b][:, :, :], ka[:, 0:8, :], [], [R_KA[b]], R_KA[b], q="pool")
                P.dma(KB[b][:, :, :], ck[s, 1536:2048, :].rearrange("(m r) f -> m r f", r=4), [], [R_KB[b]], R_KB[b], q="pool")
                P.dma(VA[b][:, :, :], va[:, 0:8, :], [], [R_VA[b]], R_VA[b], q="pool")
                P.dma(VB[b][:, :, :], cv[s, 1536:2048, :].rearrange("(m r) f -> m r f", r=4), [], [R_VB[b]], R_VB[b], q="pool")

            load_cache(0)
            load_cache(1)

            P.dma(xts[:, :], xs[:, :], [], [R_xts], R_xts)
            P.op("act", _mk("activation", out=hbs[:, :], in_=xts[:, :], func=AF.Square, accum_out=sqs[:, 0:1]),
                 [R_xts], [R_hbs, R_sqs])
            P.op("dve", _mk("tensor_scalar", out=sqs[:, 1:2], in0=sqs[:, 0:1], scalar1=1.0 / D, scalar2=EPS,
                            op0=ALU.mult, op1=ALU.add), [R_sqs], [R_sqs])
            P.op("act", _mk("activation", out=sqs[:, 3:4], in_=sqs[:, 1:2], func=AF.Ln), [R_sqs], [R_sqs])
            P.op("act", _mk("activation", out=sqs[:, 2:3], in_=sqs[:, 3:4], func=AF.Exp, scale=-0.5), [R_sqs], [R_sqs])
            P.op("act", _mk("activation", out=hbs[:, :], in_=xts[:, :], func=AF.Copy, scale=sqs[:, 2:3]),
                 [R_xts, R_sqs], [R_hbs])
            t = nxs("tr")
            for c in range(8):
                P.op("pe", _mk("transpose", out=trs[t][:, c * 128:(c + 1) * 128], in_=hbs[:, c * 128:(c + 1) * 128],
                               identity=identb[:, :]), [R_hbs, R_c], [R_trs[t]], inc=(c == 7))
            P.op("dve", _mk("tensor_copy", out=hTs[:, :, :], in_=trs[t][:, :].rearrange("p (c k) -> p c k", c=8)),
                 [R_trs[t]], [R_hTs])
            for g6 in range(6):
                w0 = g6 * 512
                wn = min(512, DIN - w0)
                pb = nxs("pj")
                for c in range(8):
                    P.op("pe", _mk("matmul", pjs[pb][:, 0:wn], hTs[:, c, :], win_sb[:, c, w0:w0 + wn], start=(c == 0),
                                   stop=(c == 7)), [R_hTs, R_w], [R_pjs[pb]], inc=(c == 7))
                if g6 % 2 == 0:
                    P.op("act", _mk("activation", out=tmS[:, w0:w0 + wn], in_=pjs[pb][:, 0:wn], func=AF.Copy),
                         [R_pjs[pb]], [R_tmS])
                else:
                    P.op("dve", _mk("tensor_copy", out=tmS[:, w0:w0 + wn], in_=pjs[pb][:, 0:wn]), [R_pjs[pb]], [R_tmS])
            out_tickets.append(P.dma(ksn[:, :], tmS[:, 1280:2048], [R_tmS], [], R_tmS))
            out_tickets.append(P.dma(vsn[:, :], tmS[:, 2048:2816], [R_tmS], [], R_tmS))
            P.op("dve", _mk("tensor_copy", out=VN[:, :], in_=tmS[:, 2048:2816]), [R_tmS], [R_VN])
            P.op("act", _mk("activation", out=gluS[:, 256:512], in_=tmS[:, 256:512], func=AF.Sigmoid), [R_tmS], [R_gluS])
            P.op("dve", _mk("tensor_tensor", out=gluS[:, 0:256], in0=tmS[:, 0:256], in1=gluS[:, 256:512], op=ALU.mult),
                 [R_tmS, R_gluS], [R_gluS])
            out_tickets.append(P.dma(cas[:, 0:22, :], sca[:, 8:30, :], [], [], R_gluS))
            for s in range(16):
                out_tickets.append(P.dma(cas[s, 22:30, :], gluS[8 * s:8 * s + 8, 0:256], [R_gluS], [], R_gluS))
            P.dma(scaS[0:120, :, :], sca.rearrange("(a s) t c -> (s t) a c", a=4), [], [R_scaS], R_scaS)
            for a4 in range(4):
                for f in range(2):
                    pb = nxs("pj")
                    P.op("pe", _mk("transpose", out=pjs[pb][:, 0:120], in_=scaS[0:120, a4, f * 128:(f + 1) * 128],
                                   identity=identf[0:120, 0:120]), [R_scaS, R_cs], [R_pjs[pb]])
                    P.op("dve", _mk("tensor_copy", out=gluSF[:, f, 4 * a4:4 * a4 + 4, 0:30],
                                    in_=pjs[pb][:, 0:120].rearrange("p (s t) -> p s t", s=4)), [R_pjs[pb]], [R_gluSF])
            for f in range(2):
                pb = nxs("pj")
                P.op("pe", _mk("transpose", out=pjs[pb][:, 0:128], in_=gluS[:, f * 128:(f + 1) * 128], identity=identf[:, :]),
                     [R_gluS, R_cs], [R_pjs[pb]])
                P.op("dve", _mk("tensor_copy", out=gluSF[:, f, :, 30:38],
                                in_=pjs[pb][:, 0:128].rearrange("p (s t) -> p s t", s=16)), [R_pjs[pb]], [R_gluSF])
            for f in range(2):
                cv_ = caccS[:, f, :].rearrange("p (s t) -> p s t", s=16)
                P.op("dve", _mk("tensor_scalar", out=cv_, in0=gluSF[:, f, :, 0:8], scalar1=caw_sb[:, f, 0:1],
                                scalar2=cab_sb[:, f:f + 1], op0=ALU.mult, op1=ALU.add), [R_gluSF, R_c], [R_caccS])
                for k in range(1, 31):
                    P.op("dve", _mk("scalar_tensor_tensor", out=cv_, in0=gluSF[:, f, :, k:k + 8], scalar=caw_sb[:, f, k:k + 1],
                                    in1=cv_, op0=ALU.mult, op1=ALU.add), [R_gluSF, R_c, R_caccS], [R_caccS])
            pm = nxs("pj")
            for f in range(2):
                P.op("pe", _mk("matmul", pjs[pm][:, 0:128], ones256[:, :], caccS[:, f, :], start=(f == 0), stop=(f == 1)),
                     [R_caccS, R_c], [R_pjs[pm]], inc=(f == 1))
            P.op("act", _mk("activation", out=csqS[:, :, :].rearrange("p a b -> p (a b)"),
                            in_=caccS[:, :, :].rearrange("p a b -> p (a b)"), func=AF.Square), [R_caccS], [R_csqS])
            pq = nxs("pj")
            for f in range(2):
                P.op("pe", _mk("matmul", pjs[pq][:, 0:128], ones256[:, :], csqS[:, f, :], start=(f == 0), stop=(f == 1)),
                     [R_csqS, R_c], [R_pjs[pq]], inc=(f == 1))
            P.op("act", _mk("activation", out=lnAs[:, :], in_=pjs[pm][:, 0:128], func=AF.Square), [R_pjs[pm]], [R_lnAs])
            P.op("dve", _mk("tensor_tensor", out=lnAs[:, :], in0=pjs[pq][:, 0:128], in1=lnAs[:, :], op=ALU.subtract),
                 [R_pjs[pq], R_lnAs], [R_lnAs])
            P.op("dve", _mk("tensor_scalar", out=lnAs[:, :], in0=lnAs[:, :], scalar1=EPS, scalar2=None, op0=ALU.add),
                 [R_lnAs], [R_lnAs])
            P.op("act", _mk("activation", out=lnAs[:, :], in_=lnAs[:, :], func=AF.Ln), [R_lnAs], [R_lnAs])
            P.op("act", _mk("activation", out=lnAs[:, :], in_=lnAs[:, :], func=AF.Exp, scale=-0.5), [R_lnAs], [R_lnAs])
            for f in range(2):
                P.op("dve", _mk("tensor_tensor", out=lnBs[:, :], in0=caccS[:, f, :], in1=pjs[pm][:, 0:128], op=ALU.subtract),
                     [R_caccS, R_pjs[pm]], [R_lnBs])
                P.op("dve", _mk("tensor_tensor", out=lnBs[:, :], in0=lnBs[:, :], in1=lnAs[:, :], op=ALU.mult),
                     [R_lnBs, R_lnAs], [R_lnBs])
                P.op("act", _mk("activation", out=aTs[:, f, :], in_=lnBs[:, :], func=AF.Silu, scale=lng_sb[:, f:f + 1],
                                bias=lnb_sb[:, f:f + 1]), [R_lnBs, R_c], [R_aTs])
            for hp in range(6):
                pb = nxs("pj")
                for c in range(8):
                    P.op("pe", _mk("matmul", pjs[pb][:, 0:128], win_sb[:, c, (4 + hp) * 128:(5 + hp) * 128], hTs[:, c, :],
                                   start=(c == 0), stop=(c == 7)), [R_w, R_hTs], [R_pjs[pb]], inc=(c == 7))
                P.op("act", _mk("activation", out=Qz[0:64, hp, :, 0, :],
                                in_=pjs[pb][0:64, 0:128].rearrange("p (s t) -> p s t", s=16), func=AF.Copy), [R_pjs[pb]], [R_Qz])
                P.op("dve", _mk("tensor_copy", out=Qz[64:128, hp, :, 1, :],
                                in_=pjs[pb][64:128, 0:128].rearrange("p (s t) -> p s t", s=16)), [R_pjs[pb]], [R_Qz])
            for hp in range(6):
                pb = nxs("pj")
                for c in range(8):
                    P.op("pe", _mk("matmul", pjs[pb][:, 0:128], win_sb[:, c, (10 + hp) * 128:(11 + hp) * 128], hTs[:, c, :],
                                   start=(c == 0), stop=(c == 7)), [R_w, R_hTs], [R_pjs[pb]], inc=(c == 7))
                P.op("act", _mk("activation", out=KNT[:, hp, :], in_=pjs[pb][:, 0:128], func=AF.Copy), [R_pjs[pb]], [R_KNT])

            for s in range(16):
                b = s % 2
                first = [True]
                for hp in range(6):
                    fs = slice(128 * hp, 128 * hp + 128)
                    kt = nxs("kt")
                    t0 = nxs("tr")
                    for r in range(4):
                        P.op("pe", _mk("transpose", out=trs[t0][:, r * 128:(r + 1) * 128], in_=KB[b][:, r, fs],
                                       identity=identb[:, :]), [R_KB[b], R_c], [R_trs[t0]], inc=False)
                    for tq in range(4):
                        P.op("pe", _mk("transpose", out=trs[t0][:, (4 + tq) * 128:(5 + tq) * 128], in_=KA[b][:, tq, fs],
                                       identity=identb[:, :]), [R_KA[b], R_c], [R_trs[t0]], inc=(tq == 3))
                    P.op("dve", _mk("tensor_copy", out=KTs[kt][:, 0:8, :].rearrange("p a b -> p (a b)"), in_=trs[t0][:, :]),
                         [R_trs[t0]], [R_KTs[kt]])
                    t1 = nxs("tr")
                    for tq in range(4, 8):
                        P.op("pe", _mk("transpose", out=trs[t1][:, (tq - 4) * 128:(tq - 3) * 128], in_=KA[b][:, tq, fs],
                                       identity=identb[:, :]), [R_KA[b], R_c], [R_trs[t1]], inc=(tq == 7))
                    P.op("act", _mk("activation", out=KTs[kt][:, 8:12, :].rearrange("p a b -> p (a b)"), in_=trs[t1][:, 0:512],
                                    func=AF.Copy), [R_trs[t1]], [R_KTs[kt]])
                    sc_ = nxs("sc")
                    qall = Qz[:, hp, s, :, :].rearrange("p a b -> p (a b)")
                    for r in range(4):
                        P.op("pe", _mk("matmul", scs[sc_][:, 16 * r:16 * r + 16], KTs[kt][:, r, :], qall, start=True, stop=True),
                             [R_KTs[kt], R_Qz], [R_scs[sc_]], inc=False)
                    for tq in range(8):
                        P.op("pe", _mk("matmul", scs[sc_][:, 64 + 2 * tq:66 + 2 * tq], KTs[kt][:, 4 + tq, :], Qz[:, hp, s, :, tq],
                                       start=True, stop=True), [R_KTs[kt], R_Qz], [R_scs[sc_]], inc=False)
                    P.op("pe", _mk("matmul", scs[sc_][:, 80:96], KNT[:, hp, :], qall, start=True, stop=True),
                         [R_KNT, R_Qz], [R_scs[sc_]])
                    pi = nxs("pt")
                    P.op("act", _mk("activation", out=pts[pi][:, :], in_=scs[sc_][:, 0:96], func=AF.Exp, scale=SCALE),
                         [R_scs[sc_]], [R_pts[pi]])
                    P.op("dve", _mk("tensor_tensor", out=pts[pi][:, :], in0=pts[pi][:, :], in1=smb[:, s, :], op=ALU.mult),
                         [R_pts[pi], R_cs], [R_pts[pi]])
                    ncol = nums[:, 16 * hp:16 * hp + 16]
                    dcol = dens[:, 16 * hp:16 * hp + 16]
                    for r in range(4):
                        st_ = first[0]
                        first[0] = False
                        P.op("pe", _mk("matmul", ncol, VB[b][:, r, fs], pts[pi][:, 16 * r:16 * r + 16], start=st_, stop=True,
                                       skip_group_check=True), [R_VB[b], R_pts[pi]], [R_nums], inc=False)
                        P.op("pe", _mk("matmul", dcol, onesb[:, :], pts[pi][:, 16 * r:16 * r + 16], start=st_, stop=True,
                                       skip_group_check=True), [R_cs, R_pts[pi]], [R_dens], inc=False)
                    for tq in range(8):
                        ncol2 = nums[:, 16 * hp + tq:16 * hp + 16:8]
                        dcol2 = dens[:, 16 * hp + tq:16 * hp + 16:8]
                        P.op("pe", _mk("matmul", ncol2, VA[b][:, tq, fs], pts[pi][:, 64 + 2 * tq:66 + 2 * tq], start=False,
                                       stop=True, skip_group_check=True), [R_VA[b], R_pts[pi]], [R_nums], inc=False)
                        P.op("pe", _mk("matmul", dcol2, onesb[:, :], pts[pi][:, 64 + 2 * tq:66 + 2 * tq], start=False,
                                       stop=True, skip_group_check=True), [R_cs, R_pts[pi]], [R_dens], inc=False)
                    P.op("pe", _mk("matmul", ncol, VN[:, fs], pts[pi][:, 80:96], start=False, stop=True, skip_group_check=True),
                         [R_VN, R_pts[pi]], [R_nums], inc=False)
                    P.op("pe", _mk("matmul", dcol, onesb[:, :], pts[pi][:, 80:96], start=False, stop=True, skip_group_check=True),
                         [R_cs, R_pts[pi]], [R_dens, R_nums])
                P.op("dve", _mk("reciprocal", out=rds[:, :], in_=dens[:, 0:96]), [R_dens], [R_rds])
                P.op("dve", _mk("tensor_tensor", out=tmpo[:, :], in0=nums[:, 0:96], in1=rds[:, :], op=ALU.mult),
                     [R_nums, R_rds], [R_tmpo])
                tv = tmpo[:, :].rearrange("p (a h t) -> p a h t", a=6, h=2)
                P.op("act", _mk("activation", out=OTs[0:64, :, 8 * s:8 * s + 8], in_=tv[0:64, :, 0, :], func=AF.Copy),
                     [R_tmpo], [R_OTs])
                P.op("act", _mk("activation", out=OTs[64:128, :, 8 * s:8 * s + 8], in_=tv[64:128, :, 1, :], func=AF.Copy),
                     [R_tmpo], [R_OTs])
                if s + 2 < 16:
                    load_cache(s + 2)
            for nh in range(2):
                pb = nxs("pj")
                for f in range(2):
                    P.op("pe", _mk("matmul", pjs[pb][:, :], aTs[:, f, :], wouta[:, f, 512 * nh:512 * nh + 512], start=(f == 0),
                                   stop=False), [R_aTs, R_w], [R_pjs[pb]], inc=False)
                for hp in range(6):
                    P.op("pe", _mk("matmul", pjs[pb][:, :], OTs[:, hp, :], wouto[:, hp, 512 * nh:512 * nh + 512], start=False,
                                   stop=(hp == 5)), [R_OTs, R_w], [R_pjs[pb]], inc=(hp == 5))
                P.op("dve", _mk("tensor_tensor", out=xts[:, 512 * nh:512 * nh + 512], in0=xts[:, 512 * nh:512 * nh + 512],
                                in1=pjs[pb][:, :], op=ALU.add), [R_xts, R_pjs[pb]], [R_xts])
            P.dma(xmid[33 * 128:34 * 128, :], xts[:, :], [R_xts], [xmid_res[33]], R_xts)
            P.barrier()
            P.flush()

    with ExitStack() as SB:
        wg_sb = sb(SB, "wg_sb", [128, 8, DFF], BF16)
        wu_sb = sb(SB, "wu_sb", [128, 8, DFF], BF16)
        wd_sb = sb(SB, "wd_sb", [128, NFF, D], BF16)
        R_w2 = Res("weights2")
        R_c2 = Res("consts2")
        gffn_sb = sb(SB, "gffn_sb", [128, 8], F32)
        fcw_sb = sb(SB, "fcw_sb", [128, NFF, 3], F32)
        fcb_sb = sb(SB, "fcb_sb", [128, NFF], F32)
        gfin_sb = sb(SB, "gfin_sb", [128, D], F32)
        identb2 = sb(SB, "identb2", [128, 128], BF16)
        identf2 = sb(SB, "identf2", [128, 128], F32)
        flag2 = sb(SB, "flag2", [128, 8], F32)
        hist = sb(SB, "hist", [128, NFF, 2], F32)
        negh = sb(SB, "negh", [128, 8], F32)
        R_hist = Res("hist")

        with ExitStack() as SB0:
            stg2 = [sb(SB0, "stg2_%d" % i, [128, DFF], F32) for i in range(2)]
            R_stg2 = [Res("stg2_0"), Res("stg2_1")]
            for dst, src in ((gffn_sb, gffn), (fcb_sb, fcb), (gfin_sb, gfin), (flag2, flagsd), (identf2, identd)):
                r = Res("v")
                P.dma(dst[:, :], src[:, :], [], [r, R_c2], r)
            r = Res("v")
            P.dma(fcw_sb[:, :, :].rearrange("p a b -> p (a b)"), fcw[:, :], [], [r, R_c2], r)
            P.op("dve", _mk("tensor_copy", out=identb2[:, :], in_=identf2[:, :]), [R_c2], [R_c2])
            R_wdq = Res("wdq")
            for ff in range(0, NFF, 2):
                P.dma(wd_sb[:, ff:ff + 2, :], wd[ff * 128:(ff + 2) * 128, :].rearrange("(a p) d -> p a d", p=128),
                      [], [], R_wdq, q="pool")
            k = 0
            for (wsrc, wdst) in ((wg, wg_sb), (wu, wu_sb)):
                for c in range(8):
                    s = k % 2
                    k += 1
                    P.dma(stg2[s][:, :], wsrc[c * 128:(c + 1) * 128, :], [], [R_stg2[s]], R_stg2[s])
                    if k % 2 == 0:
                        P.op("act", _mk("activation", out=wdst[:, c, :], in_=stg2[s][:, :], func=AF.Copy,
                                        scale=gffn_sb[:, c:c + 1]), [R_stg2[s], R_c2], [])
                    else:
                        P.op("dve", _mk("tensor_scalar", out=wdst[:, c, :], in0=stg2[s][:, :], scalar1=gffn_sb[:, c:c + 1],
                                        scalar2=None, op0=ALU.mult), [R_stg2[s], R_c2], [])
            P.op("pool", _mk("memset", hist[:, :, :].rearrange("p a b -> p (a b)"), 0.0), [], [R_hist])
            P.op("pool", _mk("memset", negh[:, :], -0.5), [], [R_c2])
            P.barrier()
            P.flush()

        def phase2_body(S, tag, sample):
            NXM = 2 if sample else 6
            xm = [sb(S, "xm%s%d" % (tag, i), [128, D], F32) for i in range(NXM)]
            R_xm = [Res("xm%d" % i) for i in range(NXM)]
            NHB = 1 if sample else 2
            hb2s = [sb(S, "hb2%s%d" % (tag, i), [128, D], BF16) for i in range(NHB)]
            R_hb2s = [Res("hb2_%d" % i) for i in range(NHB)]
            ss2 = sb(S, "ss2" + tag, [128, 8], F32)
            R_ss2 = Res("ss2")
            R_ss2f = Res("ss2f")
            NH2 = 1 if sample else 2
            h2T = [sb(S, "h2T%s%d" % (tag, i), [128, 8, 256], BF16) for i in range(NH2)]
            R_h2T = [Res("h2T%d" % i) for i in range(NH2)]
            NEB = 2 if sample else 4
            gbufs = [sb(S, "gbuf%s%d" % (tag, i), [128, 2 + 256], F32) for i in range(NEB)]
            R_gbufs = [Res("gbuf%d" % i) for i in range(NEB)]
            R_ghs = [Res("gh%d" % i) for i in range(NEB)]
            gcvs = [sb(S, "gcv%s%d" % (tag, i), [128, 256], F32) for i in range(NEB)]
            R_gcvs = [Res("gcv%d" % i) for i in range(NEB)]
            upsbs = [sb(S, "upsb%s%d" % (tag, i), [128, 256], F32) for i in range(NEB)]
            R_upsbs = [Res("upsb%d" % i) for i in range(NEB)]
            uT = [sb(S, "uT%s%d" % (tag, i), [128, NFF, 256], BF16) for i in range(NH2)]
            R_uT = [Res("uT%d" % i) for i in range(NH2)]
            if sample:
                scfS = sb(S, "scfS", [32, DFF], F32)
                R_scfS = Res("scfS")
                cfoS = sb(S, "cfoS", [128, DFF], F32)
                R_cfoS = Res("cfoS")
                shist = sb(S, "shist", [128, NFF, 32], F32)
                R_shist = Res("shist")
            NPG, NPU = 2, 4
            pg = [ps(S, "pg%s%d" % (tag, i), [128, 512], F32) for i in range(NPG)]
            R_pg = [Res("pg%d" % i) for i in range(NPG)]
            pu = [ps(S, "pu%s%d" % (tag, i), [128, 512], F32) for i in range(NPU)]
            R_pu = [Res("pu%d" % i) for i in range(NPU)]
            pd = [ps(S, "pd%s%d" % (tag, i), [128, 512], F32) for i in range(1)]
            R_pd = [Res("pd0")]
            tr2 = [ps(S, "tr2%s%d" % (tag, i), [128, 1024], BF16) for i in range(1)]
            R_tr2 = [Res("tr2_0")]
            st2 = {"pg": 0, "pu": 0, "pd": 0, "tr": 0, "xm": 0}

            def nxt2(k_, n=2):
                v = st2[k_]
                st2[k_] = (v + 1) % n
                return v

            def norm_part(blk, jcol, defer=False):
                b = nxt2("xm", NXM)
                hb2, R_hb2 = hb2s[jcol % NHB], R_hb2s[jcol % NHB]
                P.dma(xm[b][:, :], xmid[blk * 128:(blk + 1) * 128, :], [xmid_res[blk]], [R_xm[b]], R_xm[b])
                P.op("act", _mk("activation", out=hb2[:, :], in_=xm[b][:, :], func=AF.Square, accum_out=ss2[:, 0:1]),
                     [R_xm[b]], [R_hb2, R_ss2])
                P.op("dve", _mk("tensor_scalar", out=ss2[:, 1:2], in0=ss2[:, 0:1], scalar1=1.0 / D, scalar2=EPS,
                                op0=ALU.mult, op1=ALU.add), [R_ss2], [R_ss2])
                P.op("pool", _mk("tensor_tensor", out=ss2[:, 2:3], in0=ss2[:, 1:2], in1=negh[:, 0:1], op=ALU.pow),
                     [R_ss2, R_c2], [R_ss2])
                if not defer:
                    norm_part2(b, jcol)
                return b

            def norm_part2(b, jcol):
                hb2, R_hb2 = hb2s[jcol % NHB], R_hb2s[jcol % NHB]
                P.op("act", _mk("activation", out=hb2[:, :], in_=xm[b][:, :], func=AF.Copy, scale=ss2[:, 2:3]),
                     [R_xm[b], R_ss2], [R_hb2])

            def trans_part(jcol, hsel):
                hb2, R_hb2 = hb2s[jcol % NHB], R_hb2s[jcol % NHB]
                t = nxt2("tr", 1)
                for c in range(8):
                    P.op("pe", _mk("transpose", out=tr2[t][:, c * 128:(c + 1) * 128], in_=hb2[:, c * 128:(c + 1) * 128],
                                   identity=identb2[:, :]), [R_hb2, R_c2], [R_tr2[t]], inc=(c == 7))
                P.op("dve", _mk("tensor_copy", out=h2T[hsel][:, :, 128 * jcol:128 * jcol + 128],
                                in_=tr2[t][:, :].rearrange("p (c k) -> p c k", c=8)), [R_tr2[t]], [R_h2T[hsel]])

            def load_norm_T(blk, jcol, hsel):
                b = norm_part(blk, jcol)
                trans_part(jcol, hsel)
                return b

            def gate_rows_only(ncols, hsel):
                for ff in range(NFF):
                    g = nxt2("pg", NPG)
                    for c in range(8):
                        P.op("pe", _mk("matmul", pg[g][:, 0:ncols], wg_sb[:, c, ff * 128:(ff + 1) * 128], h2T[hsel][:, c, 0:ncols],
                                       start=(c == 0), stop=(c == 7)), [R_w2, R_h2T[hsel]], [R_pg[g]], inc=(c == 7))
                    P.op("dve", _mk("tensor_scalar", out=hist[:, ff, :], in0=pg[g][:, ncols - 2:ncols], scalar1=flag2[:, 0:1],
                                    scalar2=None, op0=ALU.mult), [R_pg[g], R_c2], [R_hist])

            def gate_up(ncols, hsel, hooks=None):
                hooks = hooks or {}
                tail = [None]
                for ff in range(NFF):
                    g = nxt2("pg", NPG)
                    u = nxt2("pu", NPU)
                    gbuf, R_gbuf = gbufs[ff % NEB], R_gbufs[ff % NEB]
                    gcv, R_gcv = gcvs[ff % NEB], R_gcvs[ff % NEB]
                    upsb, R_upsb = upsbs[ff % NEB], R_upsbs[ff % NEB]
                    for c in range(8):
                        P.op("pe", _mk("matmul", pg[g][:, 0:ncols], wg_sb[:, c, ff * 128:(ff + 1) * 128], h2T[hsel][:, c, 0:ncols],
                                       start=(c == 0), stop=(c == 7)), [R_w2, R_h2T[hsel]], [R_pg[g]], inc=(c == 7))
                    for c in range(8):
                        P.op("pe", _mk("matmul", pu[u][:, 0:ncols], wu_sb[:, c, ff * 128:(ff + 1) * 128], h2T[hsel][:, c, 0:ncols],
                                       start=(c == 0), stop=(c == 7)), [R_w2, R_h2T[hsel]], [R_pu[u]], inc=(c == 7))
                    R_gh = R_ghs[ff % NEB]
                    if not sample:
                        P.op("dve", _mk("tensor_copy", out=gbuf[:, 0:2], in_=hist[:, ff, :]), [R_hist], [R_gh])
                        P.op("act", _mk("activation", out=gbuf[:, 2:2 + ncols], in_=pg[g][:, 0:ncols], func=AF.Copy),
                             [R_pg[g]], [R_gbuf])
                        P.op("dve", _mk("tensor_copy", out=hist[:, ff, :], in_=gbuf[:, ncols:ncols + 2]), [R_gbuf], [R_hist])
                        srcs = [gbuf[:, k_:k_ + ncols] for k_ in range(3)]
                        gout = gcv[:, 0:ncols]
                    else:
                        gv = gbuf[:, 0:160].rearrange("p (s t) -> p s t", s=16)
                        P.op("pool", _mk("tensor_copy", out=gv[:, :, 0:2], in_=shist[:, ff, :].rearrange("p (s t) -> p s t", s=16)),
                             [R_shist], [R_gbuf])
                        P.op("act", _mk("activation", out=gv[:, :, 2:10], in_=pg[g][:, 0:128].rearrange("p (s t) -> p s t", s=16),
                                        func=AF.Copy), [R_pg[g]], [R_gbuf])
                        srcs = [gv[:, :, k_:k_ + 8] for k_ in range(3)]
                        gout = gcv[:, 0:128].rearrange("p (s t) -> p s t", s=16)
                    P.op("dve", _mk("tensor_scalar", out=gout, in0=srcs[0], scalar1=fcw_sb[:, ff, 0:1], scalar2=fcb_sb[:, ff:ff + 1],
                                    op0=ALU.mult, op1=ALU.add), [R_gbuf, R_gh, R_c2], [R_gcv])
                    for k_ in (1, 2):
                        P.op("dve", _mk("scalar_tensor_tensor", out=gout, in0=srcs[k_], scalar=fcw_sb[:, ff, k_:k_ + 1], in1=gout,
                                        op0=ALU.mult, op1=ALU.add), [R_gbuf, R_gh, R_c2, R_gcv], [R_gcv])
                    if tail[0] is not None:
                        tail[0]()

                    def mk_tail(ff=ff, gcv=gcv, R_gcv=R_gcv, u=u):
                        def t_():
                            P.op("act", _mk("activation", out=gcv[:, 0:ncols], in_=gcv[:, 0:ncols], func=AF.Silu), [R_gcv], [R_gcv])
                            P.op("dve", _mk("tensor_tensor", out=uT[hsel][:, ff, 0:ncols], in0=gcv[:, 0:ncols],
                                            in1=pu[u][:, 0:ncols], op=ALU.mult), [R_gcv, R_pu[u]], [R_uT[hsel]])
                        return t_
                    tail[0] = mk_tail()
                    for h_ in hooks.pop(ff, []):
                        h_()
                tail[0]()

            def down_group(jcol, b, nh, hsel):
                d = nxt2("pd", 1)
                for ff in range(NFF):
                    P.op("pe", _mk("matmul", pd[d][:, :], uT[hsel][:, ff, 128 * jcol:128 * jcol + 128],
                                   wd_sb[:, ff, 512 * nh:512 * nh + 512], start=(ff == 0), stop=(ff == NFF - 1)),
                         [R_uT[hsel], R_w2], [R_pd[d]], inc=(ff == NFF - 1))
                P.op("dve", _mk("tensor_tensor", out=xm[b][:, 512 * nh:512 * nh + 512], in0=xm[b][:, 512 * nh:512 * nh + 512],
                                in1=pd[d][:, :], op=ALU.add), [R_xm[b], R_pd[d]], [R_xm[b]])

            def final_part(b, out_ap, jsel=0, defer=False):
                hb2, R_hb2 = hb2s[jsel % NHB], R_hb2s[jsel % NHB]
                P.op("act", _mk("activation", out=hb2[:, :], in_=xm[b][:, :], func=AF.Square, accum_out=ss2[:, 4:5]),
                     [R_xm[b]], [R_hb2, R_ss2f])
                P.op("dve", _mk("tensor_scalar", out=ss2[:, 5:6], in0=ss2[:, 4:5], scalar1=1.0 / D, scalar2=EPS,
                                op0=ALU.mult, op1=ALU.add), [R_ss2f], [R_ss2f])
                P.op("pool", _mk("tensor_tensor", out=ss2[:, 6:7], in0=ss2[:, 5:6], in1=negh[:, 0:1], op=ALU.pow),
                     [R_ss2f, R_c2], [R_ss2f])
                if not defer:
                    final_part2(b, out_ap)

            def final_part2(b, out_ap):
                P.op("dve", _mk("scalar_tensor_tensor", out=xm[b][:, :], in0=xm[b][:, :], scalar=ss2[:, 6:7], in1=gfin_sb[:, :],
                                op0=ALU.mult, op1=ALU.mult), [R_xm[b], R_ss2f, R_c2], [R_xm[b]])
                out_tickets.append(P.dma(out_ap, xm[b][:, :], [R_xm[b]], [], R_xm[b]))

            def down_final(bufs, hsel, out_ap_fn):
                for jcol, b in enumerate(bufs):
                    for nh in range(2):
                        down_group(jcol, b, nh, hsel)
                    final_part(b, out_ap_fn(jcol), jcol)

            if not sample:
                load_norm_T(0, 0, 1)
                bufs_next = [load_norm_T(1, 0, 0), load_norm_T(2, 1, 0)]
                gate_rows_only(128, 1)
                prev = None
                for t2 in range(16):
                    hsel = t2 % 2
                    bufs_cur = bufs_next
                    box = {"b": [None, None]}
                    hooks = {}
                    if t2 + 1 < 16:
                        nsel = (t2 + 1) % 2
                        for jc in range(2):
                            hooks.setdefault(1 + 2 * jc, []).append(
                                lambda jc=jc, t2=t2, box=box: box["b"].__setitem__(jc, norm_part(3 + 2 * t2 + jc, jc, defer=True)))
                            hooks.setdefault(2 + 2 * jc, []).append(lambda jc=jc, box=box: norm_part2(box["b"][jc], jc))
                            hooks.setdefault(5 + 2 * jc, []).append(lambda jc=jc, nsel=nsel: trans_part(jc, nsel))
                    if prev is not None:
                        pbufs, phsel, pout = prev
                        for jc in range(2):
                            for nh in range(2):
                                hooks.setdefault(7 + 6 * jc + 3 * nh, []).append(
                                    lambda jc=jc, nh=nh, pbufs=pbufs, phsel=phsel: down_group(jc, pbufs[jc], nh, phsel))
                            hooks.setdefault(13 + 6 * jc, []).append(
                                lambda jc=jc, pbufs=pbufs, pout=pout: final_part(pbufs[jc], pout(jc), jc, defer=True))
                            hooks.setdefault(14 + 6 * jc, []).append(
                                lambda jc=jc, pbufs=pbufs, pout=pout: final_part2(pbufs[jc], pout(jc)))
                    gate_up(256, hsel, hooks=hooks)
                    assert not hooks
                    prev = (bufs_cur, hsel, (lambda jcol, t2=t2: y[(2 * t2 + jcol) * 128:(2 * t2 + jcol + 1) * 128, :]))
                    bufs_next = box["b"]
                bfree = [i_ for i_ in range(NXM) if i_ not in prev[0]][0]
                cfo, R_cfo = xm[bfree], R_xm[bfree]
                for g6 in range(6):
                    w0 = g6 * 512
                    wn = min(512, DFF - w0)
                    g = nxt2("pg", NPG)
                    for c in range(8):
                        P.op("pe", _mk("matmul", pg[g][:, 0:wn], h2T[1][:, c, 128:256], wg_sb[:, c, w0:w0 + wn],
                                       start=(c == 0), stop=(c == 7)), [R_h2T[1], R_w2], [R_pg[g]], inc=(c == 7))
                    P.op("act", _mk("activation", out=cfo[:, 0:wn], in_=pg[g][:, 0:wn], func=AF.Copy), [R_pg[g]], [R_cfo])
                    out_tickets.append(P.dma(cfp[:, w0:w0 + wn], cfo[126:128, 0:wn], [R_cfo], [], R_cfo))
                down_final(*prev)
            else:
                P.dma(scfS[0:32, :], scf.rearrange("s t f -> (s t) f"), [], [R_scfS], R_scfS)
                for ff in range(NFF):
                    g = 0 if ff < 16 else 1
                    col = (ff % 16) * 32
                    P.op("pe", _mk("transpose", out=pg[g][:, col:col + 32], in_=scfS[0:32, ff * 128:(ff + 1) * 128],
                                   identity=identf2[0:32, 0:32]), [R_scfS, R_c2], [R_pg[g]], inc=(ff == 15 or ff == NFF - 1))
                P.op("dve", _mk("tensor_copy", out=shist[:, 0:16, :].rearrange("p a b -> p (a b)"), in_=pg[0][:, 0:512]),
                     [R_pg[0]], [R_shist])
                P.op("dve", _mk("tensor_copy", out=shist[:, 16:NFF, :].rearrange("p a b -> p (a b)"), in_=pg[1][:, 0:192]),
                     [R_pg[1]], [R_shist])
                st2["pg"] = 0
                bufs = [load_norm_T(33, 0, 0)]
                gate_up(128, 0)
                down_final(bufs, 0, lambda jcol: ys[:, :])
                for g6 in range(6):
                    w0 = g6 * 512
                    wn = min(512, DFF - w0)
                    g = nxt2("pg", NPG)
                    for c in range(8):
                        P.op("pe", _mk("matmul", pg[g][:, 0:wn], h2T[0][:, c, 0:128], wg_sb[:, c, w0:w0 + wn],
                                       start=(c == 0), stop=(c == 7)), [R_h2T[0], R_w2], [R_pg[g]], inc=(c == 7))
                    P.op("act", _mk("activation", out=cfoS[:, w0:w0 + wn], in_=pg[g][:, 0:wn], func=AF.Copy), [R_pg[g]], [R_cfoS])
                for s in range(16):
                    out_tickets.append(P.dma(cfs[s, :, :], cfoS[8 * s + 6:8 * s + 8, :], [R_cfoS], [], R_cfoS))

        with ExitStack() as SB1:
            phase2_body(SB1, "p", False)
            P.barrier()
            P.flush()
        with ExitStack() as SB2:
            phase2_body(SB2, "s", True)
            need = {}
            for k_, v_ in out_tickets:
                if need.get(k_, 0) < v_:
                    need[k_] = v_
            for k_, v_ in need.items():
                P.plan["sp"].append(("w", P.sems[k_], v_))
            P.barrier()
            P.flush()
    top.close()
    return nc


def _consts():
    k = np.arange(128)[:, None]
    q = np.arange(128)[None, :]
    Lw = (k <= q).astype(np.float32)
    U = (k >= q).astype(np.float32)
    m2 = np.concatenate([U, Lw, U, Lw, Lw, U, Lw, U], axis=1)
    maskr = np.zeros((128, 4, 32), np.float32)
    for j in range(4):
        kn = (np.arange(128) - 32 * j) % 128
        maskr[:, j, :] = (kn[:, None] >= np.arange(32)[None, :]).astype(np.float32)
    L32 = Lw[0:32, 0:32]
    lw32 = np.ones((128, 32), np.float32)
    lw3 = np.zeros((128, 32), np.float32)
    for base in (0, 64):
        lw32[base:base + 32] = L32
        lw3[base + 32:base + 64] = L32
    ident = np.eye(128, dtype=np.float32)
    sm = np.zeros((128, 16, 96), np.float32)
    m_ = np.arange(128)
    for r in range(4):
        for t in range(8):
            d4 = ((t % 4) == r) * np.where(t >= 4, m_ >= 1, True)
            d1 = (4 * m_ + r >= 384 + t)
            for hh in range(2):
                sm[:, :, 16 * r + 8 * hh + t] = (d4.astype(np.float32) + d1.astype(np.float32))[:, None]
    sm[:, :, 64:80] = 1.0
    for s in range(16):
        for u in range(8):
            for t in range(8):
                mult = float(u <= t) + float(u <= t and (t - u) % 4 == 0) + float(u == t)
                for hh in range(2):
                    sm[8 * s + u, s, 80 + 8 * hh + t] = mult
    return m2, maskr.reshape(128, 128), lw32, lw3, ident, sm.reshape(128, 16 * 96)


_NC_CACHE = {}


def kernel(x_prompt, x_sample, cache_k_win, cache_v_win, state_conv_a, state_conv_ffn,
           norm_mix_g, w_in, conv_a_w, conv_a_b, ln_a_g, ln_a_b, w_out,
           norm_ffn_g, w_ffn_gate, w_ffn_up, ffn_conv_w, ffn_conv_b, w_ffn_down, norm_final_g):
    f = np.float32
    x_prompt = np.asarray(x_prompt, f)
    x_sample = np.asarray(x_sample, f)
    ckw = np.asarray(cache_k_win, f)[0].reshape(128, 2048, DATT)
    cvw = np.asarray(cache_v_win, f)[0].reshape(128, 2048, DATT)
    sca_ = np.asarray(state_conv_a, f)[0]
    scf_ = np.asarray(state_conv_ffn, f)[0]
    m2, maskr, lw32, lw3, ident, smask = _consts()

    def pc(v, n):
        return np.ascontiguousarray(np.asarray(v, f).reshape(n, 128).T)

    common = {
        "win": np.ascontiguousarray(np.asarray(w_in, f)[0]),
        "wout": np.ascontiguousarray(np.asarray(w_out, f)[0]),
        "wg": np.ascontiguousarray(np.asarray(w_ffn_gate, f)[0]),
        "wu": np.ascontiguousarray(np.asarray(w_ffn_up, f)[0]),
        "wd": np.ascontiguousarray(np.asarray(w_ffn_down, f)[0]),
        "gmix": pc(np.asarray(norm_mix_g)[0], 8),
        "gffn": pc(np.asarray(norm_ffn_g)[0], 8),
        "caw": np.ascontiguousarray(np.asarray(conv_a_w, f)[0].T.reshape(2, 128, 31).transpose(1, 0, 2).reshape(128, 62)),
        "cab": pc(np.asarray(conv_a_b)[0], 2),
        "lng": pc(np.asarray(ln_a_g)[0], 2),
        "lnb": pc(np.asarray(ln_a_b)[0], 2),
        "fcw": np.ascontiguousarray(np.asarray(ffn_conv_w, f)[0].T.reshape(NFF, 128, 3).transpose(1, 0, 2).reshape(128, NFF * 3)),
        "fcb": pc(np.asarray(ffn_conv_b)[0], NFF),
        "gfin": np.ascontiguousarray(np.broadcast_to(np.asarray(norm_final_g, f)[None, :], (128, D))),
        "m2": m2, "maskr": maskr, "lw32": lw32, "lw3": lw3, "ident": ident,
        "smask": smask,
    }
    in_maps = []
    for c in range(NCORES):
        b, half = c // 2, c % 2
        xpc = np.zeros((NT * TT, D), f)
        pre = (NHALO + NPRE) * TT
        if half == 0:
            xpc[pre:] = x_prompt[b, 0:4096]
        else:
            xpc[:] = x_prompt[b, 4096 - pre:8192]
        flags = np.zeros((128, 8), f)
        flags[:, 0:4] = float(half)
        flags[:, 4:8] = 1.0
        m = dict(common)
        m.update({
            "xp": xpc,
            "xs": np.ascontiguousarray(x_sample[16 * c:16 * c + 16].reshape(128, D)),
            "ck": np.ascontiguousarray(ckw[16 * c:16 * c + 16]),
            "cv": np.ascontiguousarray(cvw[16 * c:16 * c + 16]),
            "sca": np.ascontiguousarray(sca_[16 * c:16 * c + 16]),
            "scf": np.ascontiguousarray(scf_[16 * c:16 * c + 16]),
            "flags": flags,
        })
        in_maps.append(m)
    if "nc" not in _NC_CACHE:
        _NC_CACHE["nc"] = build_program()
    nc = _NC_CACHE["nc"]
    res = run_bass_kernel_spmd(nc, in_maps, core_ids=list(range(NCORES)))
    R = res.results
    y_prompt = np.stack([np.concatenate([R[2 * b]["y"], R[2 * b + 1]["y"]], axis=0) for b in range(4)])
    y_sample = np.concatenate([R[c]["ys"] for c in range(NCORES)], axis=0).reshape(128, 8, D)
    kp = np.stack([R[2 * b + 1]["kwin"].reshape(2048, NH, 64) for b in range(4)])[None]
    vp = np.stack([R[2 * b + 1]["vwin"].reshape(2048, NH, 64) for b in range(4)])[None]
    capo = np.stack([R[2 * b + 1]["cap"] for b in range(4)])[None]
    cfpo = np.stack([R[2 * b + 1]["cfp"] for b in range(4)])[None]
    ksno = np.concatenate([R[c]["ksn"] for c in range(NCORES)], axis=0).reshape(1, 128, 8, NH, 64)
    vsno = np.concatenate([R[c]["vsn"] for c in range(NCORES)], axis=0).reshape(1, 128, 8, NH, 64)
    caso = np.concatenate([R[c]["cas"] for c in range(NCORES)], axis=0)[None]
    cfso = np.concatenate([R[c]["cfs"] for c in range(NCORES)], axis=0)[None]
    return tuple(np.asarray(a, f) for a in (y_prompt, y_sample, kp, vp, capo, cfpo, ksno, vsno, caso, cfso))
```

```python
from contextlib import ExitStack
import numpy as np
import concourse.bass as bass
import concourse.mybir as mybir
from concourse.bass_utils import run_bass_kernel_spmd

F32 = mybir.dt.float32
BF16 = mybir.dt.bfloat16
AF = mybir.ActivationFunctionType
ALU = mybir.AluOpType

NCORES = 8
D = 1024
DCONV = 256
DATT = 768
NH = 12
DIN = 2816
DFF = 2816
NFF = 22
TT = 512
NHALO = 4
NPRE = 1
NMAIN = 8
NT = NHALO + NPRE + NMAIN
EPS = 1e-6
SCALE = 0.125
ENGS = ("pe", "act", "dve", "pool", "sp")


_POS = {"matmul": ("out", "lhsT", "rhs"), "transpose": ("out", "in_", "identity"), "memset": ("ap", "constant")}


def _mk(name, *args, **kw):
    for n, a in zip(_POS.get(name, ()), args):
        kw[n] = a
    if name == "matmul" and kw.get("tile_position", 0) is None:
        kw.pop("tile_position")
    return (name, kw)


class Res:
    __slots__ = ("name", "w", "r", "chan")

    def __init__(self, name):
        self.name = name
        self.w = None
        self.r = {}
        self.chan = None


class Planner:
    def __init__(self, nc, stack):
        self.nc = nc
        self.stack = stack
        self.sems = {}
        self.plan = {e: [] for e in ENGS}
        self.cnt = {e: 0 for e in ENGS}
        self.seen = {e: {} for e in ENGS}
        self.chans = []
        for e in ENGS:
            self.sems[e] = stack.enter_context(nc.semaphore(name="s_" + e))
        self.nuniq = 0

    def _need(self, reads, writes):
        need = {}

        def add(k, v):
            if need.get(k, 0) < v:
                need[k] = v

        for r in reads:
            if r.w is not None:
                add(*r.w)
        for w in writes:
            if w.w is not None:
                add(*w.w)
            for k, v in w.r.items():
                add(k, v)
        return need

    def _waits(self, ename, reads, writes):
        need = self._need(reads, writes)
        seen = self.seen[ename]
        for k, v in need.items():
            if k == ename and ename == "pe":
                continue
            if seen.get(k, 0) >= v:
                continue
            self.plan[ename].append(("w", self.sems[k], v))
            seen[k] = v

    def _mark(self, t, reads, writes):
        for r in reads:
            if r.r.get(t[0], 0) < t[1]:
                r.r[t[0]] = t[1]
        for w in writes:
            w.w = t
            w.r = {}

    def op(self, ename, fn, reads=(), writes=(), inc=True):
        self._waits(ename, reads, writes)
        if inc:
            self.cnt[ename] += 1
            t = (ename, self.cnt[ename])
            self.plan[ename].append(("i", fn, self.sems[ename], 1))
        else:
            t = (ename, self.cnt[ename] + 1)
            self.plan[ename].append(("i", fn, None, 0))
        self._mark(t, reads, writes)
        return t

    def dma(self, out, in_, reads, writes, chan, q="sp", **kw):
        self._waits(q, reads, writes)
        if chan.chan is None:
            key = ("d", self.nuniq)
            self.nuniq += 1
            self.sems[key] = self.stack.enter_context(self.nc.semaphore(name="d_%d" % key[1]))
            chan.chan = [key, 0]
            self.chans.append(chan)
        chan.chan[1] += 16
        self.plan[q].append(("i", ("dma_start", dict(out=out, in_=in_, **kw)), self.sems[chan.chan[0]], 16))
        t = (chan.chan[0], chan.chan[1])
        self._mark(t, reads, writes)
        return t

    def barrier(self):
        for e in ENGS:
            seen = self.seen[e]
            for f in ENGS:
                if f == "sp" or (f == e and e == "pe"):
                    continue
                v = self.cnt[f]
                if v > 0 and seen.get(f, 0) < v:
                    self.plan[e].append(("w", self.sems[f], v))
                    seen[f] = v
            for ch in self.chans:
                k, v = ch.chan
                if v > 0 and seen.get(k, 0) < v:
                    self.plan[e].append(("w", self.sems[k], v))
                    seen[k] = v

    def check_deadlock(self, plan):
        semv = getattr(self, "_semv", {})
        pos = {e: 0 for e in ENGS}
        rev = {id(v): k for k, v in self.sems.items()}
        progress = True
        while progress:
            progress = False
            for e in ENGS:
                items = plan[e]
                while pos[e] < len(items):
                    it = items[pos[e]]
                    if it[0] == "w":
                        k = rev[id(it[1])]
                        if semv.get(k, 0) >= it[2]:
                            pos[e] += 1
                            progress = True
                        else:
                            break
                    else:
                        if it[2] is not None:
                            k = rev[id(it[2])]
                            semv[k] = semv.get(k, 0) + it[3]
                        pos[e] += 1
                        progress = True
        self._semv = semv
        for e in ENGS:
            if pos[e] < len(plan[e]):
                it = plan[e][pos[e]]
                raise RuntimeError("DEADLOCK: engine %s stuck at item %d/%d waiting %s >= %s (have %s); next=%s" % (
                    e, pos[e], len(plan[e]), rev[id(it[1])], it[2], semv.get(rev[id(it[1])], 0),
                    [x[1][0] if x[0] == "i" else "w" for x in plan[e][pos[e]:pos[e] + 4]]))

    def flush(self):
        plan = self.plan
        self.plan = {e: [] for e in ENGS}
        self.check_deadlock(plan)

        def mk(ename):
            def body(eng):
                for item in plan[ename]:
                    if item[0] == "w":
                        eng.wait_ge(item[1], item[2])
                    else:
                        inst = getattr(eng, item[1][0])(**item[1][1])
                        if item[2] is not None:
                            inst.then_inc(item[2], item[3])
            return body

        with self.nc.Block() as block:
            block.tensor(mk("pe"))
            block.scalar(mk("act"))
            block.vector(mk("dve"))
            block.gpsimd(mk("pool"))
            block.sync(mk("sp"))


def build_program():
    nc = bass.Bass("TRN2", target_bir_lowering=False)

    def din(name, shape):
        return nc.dram_tensor(name, list(shape), F32, kind="ExternalInput").ap()

    def dout(name, shape):
        return nc.dram_tensor(name, list(shape), F32, kind="ExternalOutput").ap()

    xp = din("xp", [NT * TT, D])
    xs = din("xs", [128, D])
    ck = din("ck", [16, 2048, DATT])
    cv = din("cv", [16, 2048, DATT])
    sca = din("sca", [16, 30, DCONV])
    scf = din("scf", [16, 2, DFF])
    win = din("win", [D, DIN])
    wout = din("wout", [D, D])
    wg = din("wg", [D, DFF])
    wu = din("wu", [D, DFF])
    wd = din("wd", [DFF, D])
    gmix = din("gmix", [128, 8])
    gffn = din("gffn", [128, 8])
    caw = din("caw", [128, 2 * 31])
    cab = din("cab", [128, 2])
    lng = din("lng", [128, 2])
    lnb = din("lnb", [128, 2])
    fcw = din("fcw", [128, NFF * 3])
    fcb = din("fcb", [128, NFF])
    gfin = din("gfin", [128, D])
    m2d = din("m2", [128, 1024])
    maskrd = din("maskr", [128, 4 * 32])
    lw32d = din("lw32", [128, 32])
    lw3d = din("lw3", [128, 32])
    identd = din("ident", [128, 128])
    flagsd = din("flags", [128, 8])
    smaskd = din("smask", [128, 16 * 96])

    y = dout("y", [NMAIN * TT, D])
    ys = dout("ys", [128, D])
    kwin = dout("kwin", [2048, DATT])
    vwin = dout("vwin", [2048, DATT])
    cap = dout("cap", [30, DCONV])
    cfp = dout("cfp", [2, DFF])
    ksn = dout("ksn", [128, DATT])
    vsn = dout("vsn", [128, DATT])
    cas = dout("cas", [16, 30, DCONV])
    cfs = dout("cfs", [16, 2, DFF])
    xmid = nc.dram_tensor("xmid", [34 * 128, D], F32, kind="Internal").ap()

    top = ExitStack()
    P = Planner(nc, top)
    out_tickets = []

    def sb(stack, name, shape, dt):
        return stack.enter_context(nc.sbuf_tensor(name, list(shape), dt))

    def ps(stack, name, shape, dt):
        return stack.enter_context(nc.psum_tensor(name, list(shape), dt))

    xmid_res = [Res("xmid%d" % i) for i in range(34)]

    with ExitStack() as SA:
        win_sb = sb(SA, "win_sb", [128, 8, DIN], BF16)
        wouta = sb(SA, "wouta", [128, 2, D], BF16)
        wouto = sb(SA, "wouto", [128, 6, D], BF16)
        mbA = sb(SA, "mbA", [128, 512], BF16)
        mbB = sb(SA, "mbB", [128, 512], BF16)
        mbR = sb(SA, "mbR", [128, 4, 32], BF16)
        mbL32 = sb(SA, "mbL32", [128, 32], BF16)
        mbL3 = sb(SA, "mbL3", [128, 32], BF16)
        identb = sb(SA, "identb", [128, 128], BF16)
        onesf = sb(SA, "onesf", [128, 128], F32)
        ones256 = sb(SA, "ones256", [128, 128], F32)
        flagb = sb(SA, "flagb", [128, 8], BF16)
        gmix_sb = sb(SA, "gmix_sb", [128, 8], F32)
        caw_sb = sb(SA, "caw_sb", [128, 2, 31], F32)
        cab_sb = sb(SA, "cab_sb", [128, 2], F32)
        lng_sb = sb(SA, "lng_sb", [128, 2], F32)
        lnb_sb = sb(SA, "lnb_sb", [128, 2], F32)
        R_w = Res("weights1")
        R_c = Res("consts1")

        with ExitStack() as S0:
            stg = [sb(S0, "stg%d" % i, [128, DIN], F32) for i in range(2)]
            R_stg = [Res("stg0"), Res("stg1")]
            cst = sb(S0, "cst", [128, 1024], F32)
            cst2 = sb(S0, "cst2", [128, 1024], F32)
            R_cst2 = Res("cst2")
            R_cst = Res("cst")
            for dst, src, n in ((gmix_sb, gmix, 8), (cab_sb, cab, 2), (lng_sb, lng, 2), (lnb_sb, lnb, 2)):
                r = Res("v")
                P.dma(dst[:, :], src[:, :], [], [r, R_c], r)
            r = Res("v")
            P.dma(caw_sb[:, :, :].rearrange("p a b -> p (a b)"), caw[:, :], [], [r, R_c], r)
            offs = 0
            def load_bias(dst_ap, src_ap, c0, n):
                P.dma(cst[:, c0:c0 + n], src_ap, [], [R_cst], R_cst)
                P.op("dve", _mk("tensor_scalar", out=dst_ap, in0=cst[:, c0:c0 + n], scalar1=30000.0, scalar2=-30000.0,
                                op0=ALU.mult, op1=ALU.add), [R_cst], [R_c])
            load_bias(mbA[:, :], m2d[:, 0:512], 0, 512)
            load_bias(mbB[:, :], m2d[:, 512:1024], 512, 512)
            load_bias(mbR[:, :, :].rearrange("p a b -> p (a b)"), maskrd[:, :], 0, 128)
            load_bias(mbL32[:, :], lw32d[:, :], 128, 32)
            load_bias(mbL3[:, :], lw3d[:, :], 160, 32)
            P.dma(cst[:, 416:544], identd[:, :], [], [R_cst], R_cst)
            P.op("dve", _mk("tensor_copy", out=identb[:, :], in_=cst[:, 416:544]), [R_cst], [R_c])
            P.dma(cst[:, 544:552], flagsd[:, :], [], [R_cst], R_cst)
            P.op("dve", _mk("tensor_copy", out=flagb[:, :], in_=cst[:, 544:552]), [R_cst], [R_c])
            P.op("pool", _mk("memset", onesf[:, :], 1.0), [], [R_c])
            P.op("pool", _mk("memset", ones256[:, :], 1.0 / 256.0), [], [R_c])
            R_woq = Res("woq")
            P.dma(wouta[:, :, :], wout[0:256, :].rearrange("(a p) d -> p a d", p=128), [], [], R_woq, q="pool")
            for c in range(0, 6, 2):
                P.dma(wouto[:, c:c + 2, :], wout[256 + c * 128:256 + (c + 2) * 128, :].rearrange("(a p) d -> p a d", p=128),
                      [], [], R_woq, q="pool")
            for c in range(8):
                s = c % 2
                P.dma(stg[s][:, :], win[c * 128:(c + 1) * 128, :], [], [R_stg[s]], R_stg[s])
                if c % 2 == 0:
                    P.op("act", _mk("activation", out=win_sb[:, c, :], in_=stg[s][:, :], func=AF.Copy,
                                                                 scale=gmix_sb[:, c:c + 1]), [R_stg[s], R_c], [])
                else:
                    P.op("dve", _mk("tensor_scalar", out=win_sb[:, c, :], in0=stg[s][:, :],
                                                                    scalar1=gmix_sb[:, c:c + 1], scalar2=None,
                                                                    op0=ALU.mult), [R_stg[s], R_c], [])
            P.barrier()
            P.flush()

        with ExitStack() as S1:
            xt = [sb(S1, "xt%d" % i, [128, D], F32) for i in range(2)]
            R_xt = [Res("xt0"), Res("xt1")]
            hb = sb(S1, "hb", [128, D], BF16)
            R_hb = Res("hb")
            junk = hb
            R_junk = R_hb
            ssq = sb(S1, "ssq", [128, 8], F32)
            R_ssq = Res("ssq")
            hT = sb(S1, "hT", [128, 8, TT], BF16)
            R_hT = Res("hT")
            QT = sb(S1, "QT", [128, 6, TT], BF16)
            R_QT = Res("QT")
            KT = [sb(S1, "KT%d" % i, [128, 6, TT], BF16) for i in range(2)]
            R_KT = [Res("KT0"), Res("KT1")]
            kt_cur = {"c": 0}
            K16r = sb(S1, "K16r", [128, 6, 16, 128], BF16)
            R_K16r = Res("K16r")
            KS = sb(S1, "KS", [128, 16, 64], BF16)
            R_KS = Res("KS")
            VS = sb(S1, "VS", [128, 16, 64], BF16)
            R_VS = Res("VS")
            VTall = sb(S1, "VTall", [128, 2, 6, TT], BF16)
            VT = [VTall[:, 0], VTall[:, 1]]
            R_VT = [Res("VT0"), Res("VT1")]
            V16r = sb(S1, "V16r", [128, 6, 16 * 2 * 65], BF16)
            R_V16r = [Res("V16r%d" % i) for i in range(6)]
            V16c = sb(S1, "V16c", [128, 16, 2, 65], BF16)
            R_V16c = Res("V16c")
            va1 = sb(S1, "va1", [128, 5, 2, 65], BF16)
            R_va1 = Res("va1")
            va4 = sb(S1, "va4", [128, 8, 2, 65], BF16)
            R_va4 = Res("va4")
            pt = [sb(S1, "pt%d" % i, [128, 512], BF16) for i in range(3)]
            R_pt = [Res("pt%d" % i) for i in range(3)]
            glu = sb(S1, "glu", [128, 2, 30 + TT], F32)
            R_glu = Res("glu")
            cacc = sb(S1, "cacc", [128, 2, TT], F32)
            R_cacc = Res("cacc")
            lnA = sb(S1, "lnA", [128, TT], F32)
            R_lnA = Res("lnA")
            lnB = sb(S1, "lnB", [128, TT], F32)
            R_lnB = Res("lnB")
            aT = sb(S1, "aT", [128, 2, TT], BF16)
            R_aT = Res("aT")
            OT = sb(S1, "OT", [128, 6, TT], BF16)
            R_OT = [Res("OT%d" % i) for i in range(6)]
            mbRc = sb(S1, "mbRc", [128, 512], BF16)
            mbLc = sb(S1, "mbLc", [128, 512], BF16)
            R_mbc = Res("mbc")
            otmp = sb(S1, "otmp", [64, TT], BF16)
            R_otmp = Res("otmp")
            numsb = sb(S1, "numsb", [64, TT], F32)
            R_numsb = Res("numsb")
            rden = sb(S1, "rden", [65, TT], F32)
            R_rden = Res("rden")
            kvo = lnA
            R_kvo = R_lnA

            pj = [ps(S1, "pj%d" % i, [128, 512], F32) for i in range(2)]
            R_pj = [Res("pj0"), Res("pj1")]
            sc = [ps(S1, "sc%d" % i, [128, 512], F32) for i in range(3)]
            R_sc = [Res("sc0"), Res("sc1"), Res("sc2")]
            ac = [ps(S1, "ac%d" % i, [128, 512], F32) for i in range(2)]
            R_ac = [Res("ac0"), Res("ac1")]
            tr = [ps(S1, "tr%d" % i, [128, 1024], BF16) for i in range(1)]
            R_tr = [Res("tr0")]

            st = {"pj": 0, "sc": 0, "ac": 0, "tr": 0, "pt": 0, "xt": 0}

            def nxt(k, n):
                v = st[k]
                st[k] = (v + 1) % n
                return v

            P.op("pool", _mk("memset", K16r[:, :, :, :].rearrange("p a b c -> p (a b c)"), 0.0), [], [R_K16r])
            for q_ in range(2):
                P.op("pool", _mk("memset", KT[q_][:, :, :].rearrange("p a b -> p (a b)"), 0.0), [], [R_KT[q_]])
            P.op("pool", _mk("memset", KS[:, :, :].rearrange("p a b -> p (a b)"), 0.0), [], [R_KS])
            P.op("pool", _mk("memset", VS[:, :, :].rearrange("p a b -> p (a b)"), 0.0), [], [R_VS])
            P.op("pool", _mk("memset", V16r[:, :, :].rearrange("p a b -> p (a b)"), 0.0), [], R_V16r)
            P.op("pool", _mk("memset", glu[:, :, :].rearrange("p a b -> p (a b)"), 0.0), [], [R_glu])
            P.op("pool", _mk("memset", VTall[:, :, :, :].rearrange("p a b c -> p (a b c)"), 0.0), [], R_VT)
            P.op("pool", _mk("memset", V16c[:, :, :, :].rearrange("p a b c -> p (a b c)"), 0.0), [], [R_V16c])
            P.op("pool", _mk("memset", va1[:, :, :, :].rearrange("p a b c -> p (a b c)"), 0.0), [], [R_va1])
            P.op("pool", _mk("memset", va4[:, :, :, :].rearrange("p a b c -> p (a b c)"), 0.0), [], [R_va4])

            def norm_transpose(src_rows_ap, j, ntok_cols=TT, hT_=None, R_hT_=None):
                hT_ = hT if hT_ is None else hT_
                R_hT_ = R_hT if R_hT_ is None else R_hT_
                b = nxt("xt", 2)
                P.dma(xt[b][:, :], src_rows_ap, [], [R_xt[b]], R_xt[b])
                P.op("act", _mk("activation", out=junk[:, :], in_=xt[b][:, :], func=AF.Square,
                                                   accum_out=ssq[:, 0:1]), [R_xt[b]], [R_junk, R_ssq])
                P.op("dve", _mk("tensor_scalar", out=ssq[:, 1:2], in0=ssq[:, 0:1], scalar1=1.0 / D, scalar2=EPS,
                                                      op0=ALU.mult, op1=ALU.add), [R_ssq], [R_ssq])
                P.op("act", _mk("activation", out=ssq[:, 3:4], in_=ssq[:, 1:2], func=AF.Ln), [R_ssq], [R_ssq])
                P.op("act", _mk("activation", out=ssq[:, 2:3], in_=ssq[:, 3:4], func=AF.Exp, scale=-0.5), [R_ssq], [R_ssq])
                P.op("act", _mk("activation", out=hb[:, :], in_=xt[b][:, :], func=AF.Copy, scale=ssq[:, 2:3]),
                     [R_xt[b], R_ssq], [R_hb])
                t = nxt("tr", 1)
                for c in range(8):
                    P.op("pe", _mk("transpose", out=tr[t][:, c * 128:(c + 1) * 128],
                                                          in_=hb[:, c * 128:(c + 1) * 128], identity=identb[:, :]),
                         [R_hb, R_c], [R_tr[t]], inc=(c == 7))
                P.op("dve", _mk("tensor_copy", out=hT_[:, :, 128 * j:128 * j + 128],
                                                    in_=tr[t][:, :].rearrange("p (c k) -> p c k", c=8)),
                     [R_tr[t]], [R_hT_])
                return b

            def proj_fm(fchunk, ncols=TT):
                pb = nxt("pj", 2)
                for c in range(8):
                    P.op("pe", _mk("matmul", pj[pb][:, 0:ncols], win_sb[:, c, fchunk * 128:(fchunk + 1) * 128],
                                                       hT[:, c, 0:ncols], start=(c == 0), stop=(c == 7)),
                         [R_w, R_hT], [R_pj[pb]], inc=(c == 7))
                return pb

            def evac(pb, dst_ap, R_dst, which):
                if which == "act":
                    P.op("act", _mk("activation", out=dst_ap, in_=pj[pb][:, 0:dst_ap.shape[-1]], func=AF.Copy),
                         [R_pj[pb]], [R_dst])
                else:
                    P.op("dve", _mk("tensor_copy", out=dst_ap, in_=pj[pb][:, 0:dst_ap.shape[-1]]),
                         [R_pj[pb]], [R_dst])

            def kv_project(par):
                for hp in range(6):
                    pb = proj_fm(10 + hp)
                    evac(pb, KT[kt_cur["c"]][:, hp, :], R_KT[kt_cur["c"]], "act" if hp % 2 == 0 else "dve")
                for hp in range(6):
                    pb = proj_fm(16 + hp)
                    evac(pb, VT[par][:, hp, :], R_VT[par], "dve" if hp % 2 == 0 else "act")

            def v16_build(par, hp, j):
                js = slice(64, 128) if j == 3 else slice(32 * j, 32 * j + 32)
                if j == 3:
                    P.op("pool", _mk("tensor_copy", out=VS[:, :, 0:32],
                                     in_=VT[1 - par][:, hp, :].rearrange("p (u r) -> p r u", r=16)), [R_VT[1 - par]], [R_VS])
                    P.op("pool", _mk("tensor_copy", out=VS[:, :, 32:64],
                                     in_=VT[par][:, hp, :].rearrange("p (u r) -> p r u", r=16)), [R_VT[par]], [R_VS])
                for half in range(2):
                    t = nxt("tr", 1)
                    for rr in range(8):
                        r = half * 8 + rr
                        if j == 3:
                            src_ = VS[:, r, :]
                            rds = [R_VS, R_c]
                        else:
                            src_ = VT[par][:, hp, r::16]
                            rds = [R_VT[par], R_c]
                        P.op("pe", _mk("transpose", out=tr[t][js, rr * 128:(rr + 1) * 128], in_=src_, identity=identb[:, :]),
                             rds, [R_tr[t]], inc=(rr == 7))
                    P.op("dve", _mk("tensor_copy",
                        out=V16c[js, half * 8:half * 8 + 8, :, 0:64],
                        in_=tr[t][js, :].rearrange("p (r h d) -> p r h d", r=8, h=2)),
                        [R_tr[t]], [R_V16c])

            def ring_insert_v(hp, j):
                js = slice(64, 128) if j == 3 else slice(32 * j, 32 * j + 32)
                P.op("act", _mk("activation", out=V16r[js, hp, :],
                                in_=V16c[js, :, :, :].rearrange("p a b c -> p (a b c)"), func=AF.Copy),
                     [R_V16c], [R_V16r[hp]])

            def set_aug(i):
                fl = slice(0, 2)
                on = slice(4, 6)
                src = flagb[:, fl] if i < 0 else flagb[:, on]
                P.op("pool", _mk("tensor_copy", out=V16c[:, :, :, 64:65],
                                                     in_=src.unsqueeze(1).unsqueeze(3).broadcast_to([128, 16, 2, 1])),
                     [R_c], [R_V16c])
                if i >= -1:
                    s_prev = flagb[:, fl] if i <= 0 else flagb[:, on]
                    s_cur = flagb[:, fl] if i < 0 else flagb[:, on]
                    P.op("pool", _mk("tensor_copy", out=va1[:, 0:1, :, 64:65],
                                                         in_=s_prev.unsqueeze(1).unsqueeze(3)), [R_c], [R_va1])
                    P.op("pool", _mk("tensor_copy", out=va1[:, 1:5, :, 64:65],
                                                         in_=s_cur.unsqueeze(1).unsqueeze(3).broadcast_to([128, 4, 2, 1])),
                         [R_c], [R_va1])
                    P.op("pool", _mk("tensor_copy", out=va4[:, 0:4, :, 64:65],
                                                         in_=s_prev.unsqueeze(1).unsqueeze(3).broadcast_to([128, 4, 2, 1])),
                         [R_c], [R_va4])
                    P.op("pool", _mk("tensor_copy", out=va4[:, 4:8, :, 64:65],
                                                         in_=s_cur.unsqueeze(1).unsqueeze(3).broadcast_to([128, 4, 2, 1])),
                         [R_c], [R_va4])

            def exp_mask(sb_, ncol0, ncol1, mask_ap, prow=slice(0, 128)):
                pi = nxt("pt", 3)
                P.op("act", _mk("activation", out=pt[pi][prow, ncol0:ncol1], in_=sc[sb_][prow, ncol0:ncol1],
                                                   func=AF.Exp, scale=SCALE), [R_sc[sb_]], [R_pt[pi]])
                pv_ = pt[pi][prow, ncol0:ncol1]
                if mask_ap.ndim == 3:
                    pv_ = pv_.rearrange("p (a b) -> p a b", a=mask_ap.shape[1])
                P.op("pool", _mk("tensor_tensor", out=pv_, in0=pv_, in1=mask_ap, op=ALU.mult), [R_pt[pi], R_c], [R_pt[pi]])
                return pi

            def attention(i, par, hooks=None):
                hooks = hooks or {}
                j = i % 4
                ppar = 1 - par
                js = slice(64, 128) if j == 3 else slice(32 * j, 32 * j + 32)
                nk = 64 if j == 3 else 32
                DEPTH = 2
                steps = []
                P.op("pool", _mk("tensor_copy", out=mbRc[:, :].rearrange("p (a b) -> p a b", a=16),
                                 in_=mbR[:, j, :].unsqueeze(1).broadcast_to([128, 16, 32])), [R_c], [R_mbc])
                mb__ = mbL3 if j == 3 else mbL32
                P.op("pool", _mk("tensor_copy", out=mbLc[:, :].rearrange("p (a b) -> p a b", a=16),
                                 in_=mb__[:, :].unsqueeze(1).broadcast_to([128, 16, 32])), [R_c], [R_mbc])

                def build_v(hp):
                    t = nxt("tr", 1)
                    for bi in range(5):
                        src = VT[ppar][:, hp, 384:512] if bi == 0 else VT[par][:, hp, 128 * (bi - 1):128 * bi]
                        rr_ = [R_VT[ppar]] if bi == 0 else [R_VT[par]]
                        P.op("pe", _mk("transpose", out=tr[t][:, bi * 128:(bi + 1) * 128], in_=src, identity=identb[:, :]),
                             rr_ + [R_c], [R_tr[t]], inc=(bi == 4))
                    P.op("act", _mk("activation", out=va1[:, :, :, 0:64],
                                    in_=tr[t][:, 0:640].rearrange("p (b h d) -> p b h d", b=5, h=2), func=AF.Copy),
                         [R_tr[t]], [R_va1])
                    t = nxt("tr", 1)
                    for bi in range(8):
                        r = bi % 4
                        src = VT[ppar][:, hp, r::4] if bi < 4 else VT[par][:, hp, r::4]
                        rr_ = [R_VT[ppar]] if bi < 4 else [R_VT[par]]
                        P.op("pe", _mk("transpose", out=tr[t][:, bi * 128:(bi + 1) * 128], in_=src, identity=identb[:, :]),
                             rr_ + [R_c], [R_tr[t]], inc=(bi == 7))
                    P.op("act", _mk("activation", out=va4[:, :, :, 0:64],
                                    in_=tr[t][:, :].rearrange("p (b h d) -> p b h d", b=8, h=2), func=AF.Copy),
                         [R_tr[t]], [R_va4])
                    v16_build(par, hp, j)

                def add_head(hp, hh):
                    hs = slice(64 * hh, 64 * hh + 64)
                    a_box = [None]
                    started = [False]

                    def pv(lhsT, rhs, out_ap, reads, last=False, tp=None):
                        stt = not started[0]
                        started[0] = True
                        a = a_box[0]
                        P.op("pe", _mk("matmul", out_ap, lhsT, rhs, start=stt, stop=True, skip_group_check=True,
                                       tile_position=tp), reads, [R_ac[a]], inc=last)

                    def maskmm(s, bias_ap, rows=slice(0, 128)):
                        tp = None if rows.start == 0 and rows.stop == 128 else (rows.start, rows.start)
                        P.op("pe", _mk("matmul", sc[s][rows, 0:512], identb[rows, rows], bias_ap, start=True, stop=False,
                                       skip_group_check=True, tile_position=tp), [R_c, R_mbc], [R_sc[s]], inc=False)

                    def qkmm(s, cols, kap, qap, reads, last, rows=slice(0, 128), tp=None):
                        P.op("pe", _mk("matmul", sc[s][rows, cols], kap, qap, start=False, stop=True, skip_group_check=True,
                                       tile_position=tp), reads, [R_sc[s]], inc=last)

                    def expo(s, pi, rows=slice(0, 128)):
                        P.op("act", _mk("activation", out=pt[pi][rows, :], in_=sc[s][rows, 0:512], func=AF.Exp, scale=SCALE),
                             [R_sc[s]], [R_pt[pi]])

                    def qk1(s):
                        maskmm(s, mbA[:, :])
                        qkmm(s, slice(0, 128), KT[1 - kt_cur["c"]][hs, hp, 384:512], QT[hs, hp, 0:128], [R_KT[1 - kt_cur["c"]], R_QT], False)
                        qkmm(s, slice(128, 384), KT[kt_cur["c"]][hs, hp, 0:128], QT[hs, hp, 0:256], [R_KT[kt_cur["c"]], R_QT], False)
                        qkmm(s, slice(384, 512), KT[kt_cur["c"]][hs, hp, 384:512], QT[hs, hp, 384:512], [R_KT[kt_cur["c"]], R_QT], True)

                    def rest1(s, pi):
                        a_box[0] = nxt("ac", 2)
                        a = a_box[0]
                        expo(s, pi)
                        rd = [R_va1, R_pt[pi]]
                        pv(va1[:, 0, hh, :], pt[pi][:, 0:128], ac[a][0:65, 0:128], rd)
                        pv(va1[:, 1, hh, :], pt[pi][:, 128:256], ac[a][0:65, 0:128], rd)
                        pv(va1[:, 1, hh, :], pt[pi][:, 256:384], ac[a][0:65, 128:256], rd)
                        pv(va1[:, 4, hh, :], pt[pi][:, 384:512], ac[a][0:65, 384:512], rd, last=True)

                    def qk2(s):
                        maskmm(s, mbB[:, :])
                        qkmm(s, slice(0, 256), KT[kt_cur["c"]][hs, hp, 128:256], QT[hs, hp, 128:384], [R_KT[kt_cur["c"]], R_QT], False)
                        qkmm(s, slice(256, 512), KT[kt_cur["c"]][hs, hp, 256:384], QT[hs, hp, 256:512], [R_KT[kt_cur["c"]], R_QT], True)

                    def rest2(s, pi):
                        a = a_box[0]
                        expo(s, pi)
                        rd = [R_va1, R_pt[pi]]
                        pv(va1[:, 2, hh, :], pt[pi][:, 0:128], ac[a][0:65, 128:256], rd)
                        pv(va1[:, 2, hh, :], pt[pi][:, 128:256], ac[a][0:65, 256:384], rd)
                        pv(va1[:, 3, hh, :], pt[pi][:, 256:384], ac[a][0:65, 256:384], rd)
                        pv(va1[:, 3, hh, :], pt[pi][:, 384:512], ac[a][0:65, 384:512], rd, last=True)

                    def mk4(r0):
                        def qk(s):
                            maskmm(s, mbB[:, :])
                            for q_, r in enumerate((r0, r0 + 1)):
                                qkmm(s, slice(256 * q_, 256 * q_ + 128), KT[kt_cur["c"]][hs, hp, r::4], QT[hs, hp, r::4], [R_KT[kt_cur["c"]], R_QT], False)
                                qkmm(s, slice(256 * q_ + 128, 256 * q_ + 256), KT[1 - kt_cur["c"]][hs, hp, r::4], QT[hs, hp, r::4],
                                     [R_KT[1 - kt_cur["c"]], R_QT], q_ == 1)

                        def rest(s, pi):
                            a = a_box[0]
                            expo(s, pi)
                            rd = [R_va4, R_pt[pi]]
                            for q_, r in enumerate((r0, r0 + 1)):
                                pv(va4[:, 4 + r, hh, :], pt[pi][:, 256 * q_:256 * q_ + 128], ac[a][0:65, r::4], rd)
                                pv(va4[:, r, hh, :], pt[pi][:, 256 * q_ + 128:256 * q_ + 256], ac[a][0:65, r::4], rd, last=(q_ == 1))
                        return qk, rest

                    def qk5(s):
                        maskmm(s, mbRc[:, :])
                        for r in range(16):
                            qkmm(s, slice(32 * r, 32 * r + 32), K16r[hs, hp, r, :], QT[hs, hp, r::16], [R_K16r, R_QT], r == 15)

                    def rest5(s, pi):
                        a = a_box[0]
                        expo(s, pi)
                        v16 = V16r[:, hp, :].rearrange("p (r h d) -> p r h d", r=16, h=2)
                        for r in range(16):
                            pv(v16[:, r, hh, :], pt[pi][:, 32 * r:32 * r + 32], ac[a][0:65, r::16], [R_V16r[hp], R_pt[pi]],
                               last=(r == 15))

                    def qk6(s):
                        if j == 3 and hh == 0:
                            P.op("pool", _mk("tensor_copy", out=KS[:, :, 32:64],
                                             in_=KT[kt_cur["c"]][:, hp, :].rearrange("p (u r) -> p r u", r=16)), [R_KT[kt_cur["c"]]], [R_KS])
                        P.op("pe", _mk("matmul", sc[s][js, 0:512], identb[hs, 64 * hh:64 * hh + nk], mbLc[hs, :], start=True,
                                       stop=False, skip_group_check=True, tile_position=(64 * hh, js.start)),
                             [R_c, R_mbc], [R_sc[s]], inc=False)
                        for r in range(16):
                            kap_ = KS[hs, r, :] if j == 3 else KT[kt_cur["c"]][hs, hp, r::16]
                            qkmm(s, slice(32 * r, 32 * r + 32), kap_, QT[hs, hp, r::16], [R_KT[kt_cur["c"]], R_KS, R_QT], r == 15,
                                 rows=js, tp=(64 * hh, js.start))

                    def rest6(s, pi):
                        a = a_box[0]
                        expo(s, pi, rows=js)
                        for r in range(16):
                            pv(V16c[js, r, hh, :], pt[pi][js, 32 * r:32 * r + 32], ac[a][0:65, r::16], [R_V16c, R_pt[pi]],
                               last=(r == 15), tp=(js.start, 0))

                    def norm_a():
                        a = a_box[0]
                        P.op("act", _mk("activation", out=numsb[:, :], in_=ac[a][0:64, :], func=AF.Copy), [R_ac[a]], [R_numsb])
                        P.op("dve", _mk("tensor_scalar", out=rden[64:65, :], in0=ac[a][64:65, :], scalar1=1e-18,
                                        scalar2=None, op0=ALU.add), [R_ac[a]], [R_rden])
                        P.op("act", _mk("activation", out=rden[64:65, :], in_=rden[64:65, :], func=AF.Ln), [R_rden], [R_rden])
                        P.op("act", _mk("activation", out=rden[64:65, :], in_=rden[64:65, :], func=AF.Exp, scale=-1.0),
                             [R_rden], [R_rden])
                        if hh == 1:
                            ring_insert_v(hp, j)

                    def norm_b():
                        pb = nxt("pj", 2)
                        P.op("pe", _mk("matmul", pj[pb][0:64, :], onesf[64:65, 0:64], rden[64:65, :], start=True, stop=True),
                             [R_rden, R_c], [R_pj[pb]])
                        if hh == 0:
                            P.op("dve", _mk("tensor_tensor", out=OT[0:64, hp, :], in0=numsb[:, :], in1=pj[pb][0:64, :],
                                            op=ALU.mult), [R_numsb, R_pj[pb]], [R_OT[hp]])
                        else:
                            P.op("dve", _mk("tensor_tensor", out=otmp[:, :], in0=numsb[:, :], in1=pj[pb][0:64, :],
                                            op=ALU.mult), [R_numsb, R_pj[pb]], [R_otmp])
                            P.dma(OT[64:128, hp, :], otmp[:, :], [R_otmp], [R_OT[hp]], R_otmp)

                    q3, r3 = mk4(0)
                    q4, r4 = mk4(2)
                    pre = (lambda: build_v(hp)) if hh == 0 else None
                    steps.append(dict(qk=qk1, rest=rest1, pre=pre, post=None, post2=None))
                    steps.append(dict(qk=qk2, rest=rest2, pre=None, post=None, post2=None))
                    steps.append(dict(qk=q3, rest=r3, pre=None, post=None, post2=None))
                    steps.append(dict(qk=q4, rest=r4, pre=None, post=None, post2=None))
                    steps.append(dict(qk=qk5, rest=rest5, pre=None, post=None, post2=None))
                    steps.append(dict(qk=qk6, rest=rest6, pre=None, post=norm_a, post2=norm_b))

                for hp in range(6):
                    for hh in range(2):
                        add_head(hp, hh)
                N = len(steps)
                banks = [None] * N
                deferred = {}
                for n in range(N + DEPTH + 4):
                    if n < N:
                        banks[n] = nxt("sc", 3)
                        steps[n]["qk"](banks[n])
                    m = n - DEPTH
                    if 0 <= m < N:
                        stp = steps[m]
                        if stp["pre"] is not None:
                            stp["pre"]()
                        pi = nxt("pt", 3)
                        stp["rest"](banks[m], pi)
                        if stp["post"] is not None:
                            stp["post"]()
                            deferred[m + 3] = stp["post2"]
                    if m in deferred:
                        deferred.pop(m)()
                    for h_ in hooks.pop(n, []):
                        h_()
                assert not deferred and not hooks

            def conv_a(i):
                for f in range(2):
                    pbg = proj_fm(2 + f)
                    P.op("act", _mk("activation", out=lnB[:, :], in_=pj[pbg][:, :], func=AF.Sigmoid),
                         [R_pj[pbg]], [R_lnB])
                    pbv = proj_fm(f)
                    P.op("dve", _mk("tensor_tensor", out=glu[:, f, 30:30 + TT], in0=pj[pbv][:, :], in1=lnB[:, :],
                                                          op=ALU.mult), [R_pj[pbv], R_lnB], [R_glu])

            def conv_taps(f, k0, k1):
                for k in range(k0, k1):
                    if k == 0:
                        P.op("dve", _mk("tensor_scalar", out=cacc[:, f, :], in0=glu[:, f, 0:TT], scalar1=caw_sb[:, f, 0:1],
                                        scalar2=cab_sb[:, f:f + 1], op0=ALU.mult, op1=ALU.add), [R_glu, R_c], [R_cacc])
                    else:
                        P.op("dve", _mk("scalar_tensor_tensor", out=cacc[:, f, :], in0=glu[:, f, k:k + TT],
                                        scalar=caw_sb[:, f, k:k + 1], in1=cacc[:, f, :], op0=ALU.mult, op1=ALU.add),
                             [R_glu, R_c, R_cacc], [R_cacc])

            def conv_hist():
                P.op("pool", _mk("tensor_copy", out=glu[:, :, 0:30], in_=glu[:, :, TT:TT + 30]), [R_glu], [R_glu])

            def conv_b(i):
                pm = nxt("pj", 2)
                for f in range(2):
                    P.op("pe", _mk("matmul", pj[pm][:, :], ones256[:, :], cacc[:, f, :], start=(f == 0), stop=(f == 1)),
                         [R_cacc, R_c], [R_pj[pm]], inc=(f == 1))
                P.op("act", _mk("activation", out=lnA[:, :], in_=cacc[:, 0, :], func=AF.Square), [R_cacc], [R_lnA])
                P.op("act", _mk("activation", out=lnB[:, :], in_=cacc[:, 1, :], func=AF.Square), [R_cacc], [R_lnB])
                pq = nxt("pj", 2)
                P.op("pe", _mk("matmul", pj[pq][:, :], ones256[:, :], lnA[:, :], start=True, stop=False),
                     [R_lnA, R_c], [R_pj[pq]], inc=False)
                P.op("pe", _mk("matmul", pj[pq][:, :], ones256[:, :], lnB[:, :], start=False, stop=True),
                     [R_lnB, R_c], [R_pj[pq]])
                P.op("act", _mk("activation", out=lnA[:, :], in_=pj[pm][:, :], func=AF.Square), [R_pj[pm]], [R_lnA])
                P.op("dve", _mk("tensor_tensor", out=lnA[:, :], in0=pj[pq][:, :], in1=lnA[:, :], op=ALU.subtract),
                     [R_pj[pq], R_lnA], [R_lnA])
                P.op("dve", _mk("tensor_scalar", out=lnA[:, :], in0=lnA[:, :], scalar1=EPS, scalar2=None, op0=ALU.add),
                     [R_lnA], [R_lnA])
                P.op("act", _mk("activation", out=lnA[:, :], in_=lnA[:, :], func=AF.Ln), [R_lnA], [R_lnA])
                P.op("act", _mk("activation", out=lnA[:, :], in_=lnA[:, :], func=AF.Exp, scale=-0.5), [R_lnA], [R_lnA])
                for f in range(2):
                    P.op("dve", _mk("tensor_tensor", out=lnB[:, :], in0=cacc[:, f, :], in1=pj[pm][:, :], op=ALU.subtract),
                         [R_cacc, R_pj[pm]], [R_lnB])
                    P.op("dve", _mk("tensor_tensor", out=lnB[:, :], in0=lnB[:, :], in1=lnA[:, :], op=ALU.mult),
                         [R_lnB, R_lnA], [R_lnB])
                    P.op("act", _mk("activation", out=aT[:, f, :], in_=lnB[:, :], func=AF.Silu,
                                                       scale=lng_sb[:, f:f + 1], bias=lnb_sb[:, f:f + 1]),
                         [R_lnB, R_c], [R_aT])

            def out_proj(i):
                blocks = [3] if i == -1 else [0, 1, 2, 3]
                for jb in blocks:
                    b = nxt("xt", 2)
                    row0 = (i + NHALO + NPRE) * TT + 128 * jb
                    P.dma(xt[b][:, :], xp[row0:row0 + 128, :], [], [R_xt[b]], R_xt[b])
                    for nh in range(2):
                        pb = nxt("pj", 2)
                        for f in range(2):
                            P.op("pe", _mk("matmul", pj[pb][:, :], aT[:, f, 128 * jb:128 * jb + 128],
                                                               wouta[:, f, 512 * nh:512 * nh + 512], start=(f == 0), stop=False),
                                 [R_aT, R_w], [R_pj[pb]], inc=False)
                        for hp in range(6):
                            P.op("pe", _mk("matmul", pj[pb][:, :], OT[:, hp, 128 * jb:128 * jb + 128],
                                                                 wouto[:, hp, 512 * nh:512 * nh + 512], start=False,
                                                                 stop=(hp == 5)),
                                 [R_OT[hp], R_w], [R_pj[pb]], inc=(hp == 5))
                        P.op("dve", _mk("tensor_tensor", out=xt[b][:, 512 * nh:512 * nh + 512],
                                                              in0=xt[b][:, 512 * nh:512 * nh + 512], in1=pj[pb][:, :],
                                                              op=ALU.add), [R_xt[b], R_pj[pb]], [R_xt[b]])
                    blk = 0 if i == -1 else 1 + 4 * i + jb
                    P.dma(xmid[blk * 128:(blk + 1) * 128, :], xt[b][:, :], [R_xt[b]], [xmid_res[blk]], R_xt[b])

            def kv_outputs(i):
                for jb in range(4):
                    row0 = (i - 4) * TT + 128 * jb
                    for g in range(3):
                        pb = nxt("pj", 2)
                        for c in range(8):
                            P.op("pe", _mk("matmul", pj[pb][:, :], hT[:, c, 128 * jb:128 * jb + 128],
                                                               win_sb[:, c, 1280 + 512 * g:1280 + 512 * g + 512],
                                                               start=(c == 0), stop=(c == 7)),
                                 [R_hT, R_w], [R_pj[pb]], inc=(c == 7))
                        P.op("act", _mk("activation", out=kvo[:, :], in_=pj[pb][:, :], func=AF.Copy), [R_pj[pb]], [R_kvo])
                        if g == 0:
                            out_tickets.append(P.dma(kwin[row0:row0 + 128, 0:512], kvo[:, :], [R_kvo], [], R_kvo))
                        elif g == 1:
                            out_tickets.append(P.dma(kwin[row0:row0 + 128, 512:768], kvo[:, 0:256], [R_kvo], [], R_kvo))
                            out_tickets.append(P.dma(vwin[row0:row0 + 128, 0:256], kvo[:, 256:512], [R_kvo], [], R_kvo))
                        else:
                            out_tickets.append(P.dma(vwin[row0:row0 + 128, 256:768], kvo[:, :], [R_kvo], [], R_kvo))

            def conv_a_output():
                pb = nxt("pj", 2)
                for c in range(8):
                    P.op("pe", _mk("matmul", pj[pb][:, :], hT[:, c, 384:512], win_sb[:, c, 0:512],
                                                       start=(c == 0), stop=(c == 7)), [R_hT, R_w], [R_pj[pb]], inc=(c == 7))
                P.op("act", _mk("activation", out=kvo[:, 256:512], in_=pj[pb][:, 256:512], func=AF.Sigmoid),
                     [R_pj[pb]], [R_kvo])
                P.op("dve", _mk("tensor_tensor", out=kvo[:, 0:256], in0=pj[pb][:, 0:256], in1=kvo[:, 256:512], op=ALU.mult),
                     [R_pj[pb], R_kvo], [R_kvo])
                out_tickets.append(P.dma(cap[:, :], kvo[98:128, 0:256], [R_kvo], [], R_kvo))

            def prep_block(ti_, jb):
                norm_transpose(xp[ti_ * TT + 128 * jb:ti_ * TT + 128 * jb + 128, :], jb)

            for jb in range(4):
                prep_block(0, jb)
            for ti in range(NT):
                i = ti - (NHALO + NPRE)
                par = (ti + 1) % 2
                kt_cur["c"] = par
                j = i % 4
                if i in (-(NHALO + NPRE), -1, 0, 1):
                    set_aug(i)
                if i < -1:
                    kv_project(par)
                    for jb in range(4):
                        prep_block(ti + 1, jb)
                    for hp in range(6):
                        v16_build(par, hp, j)
                        ring_insert_v(hp, j)
                else:
                    conv_a(i)
                    for hp in range(6):
                        pb = proj_fm(4 + hp)
                        evac(pb, QT[:, hp, :], R_QT, "act" if hp % 2 == 0 else "dve")
                    kv_project(par)
                    if i >= 4:
                        kv_outputs(i)
                    if i == NMAIN - 1:
                        conv_a_output()
                    hooks = {}
                    hk = 1
                    for f in range(2):
                        for k0 in range(0, 31, 8):
                            hooks.setdefault(hk, []).append(lambda f=f, k0=k0: conv_taps(f, k0, min(31, k0 + 8)))
                            hk += 2
                    hooks.setdefault(hk, []).append(conv_hist)
                    hooks.setdefault(hk + 1, []).append(lambda i=i: conv_b(i))
                    if ti + 1 < NT:
                        for jb in range(4):
                            hooks.setdefault(24 + 12 * jb, []).append(lambda ti=ti, jb=jb: prep_block(ti + 1, jb))
                    attention(i, par, hooks)
                    out_proj(i)
                for hp in range(6):
                    P.op("pool", _mk("tensor_copy", out=K16r[:, hp, :, 32 * j:32 * j + 32],
                                     in_=KT[kt_cur["c"]][:, hp, :].rearrange("p (u r) -> p r u", r=16)), [R_KT[kt_cur["c"]]], [R_K16r])
            P.barrier()
            P.flush()

        with ExitStack() as SS:
            xts = sb(SS, "xts", [128, D], F32)
            R_xts = Res("xts")
            hbs = sb(SS, "hbs", [128, D], BF16)
            R_hbs = Res("hbs")
            sqs = sb(SS, "sqs", [128, 8], F32)
            R_sqs = Res("sqs")
            hTs = sb(SS, "hTs", [128, 8, 128], BF16)
            R_hTs = Res("hTs")
            tmS = sb(SS, "tmS", [128, DIN], F32)
            R_tmS = Res("tmS")
            VN = sb(SS, "VN", [128, DATT], BF16)
            R_VN = Res("VN")
            gluS = sb(SS, "gluS", [128, 512], F32)
            R_gluS = Res("gluS")
            gluSF = sb(SS, "gluSF", [128, 2, 16, 38], F32)
            R_gluSF = Res("gluSF")
            scaS = sb(SS, "scaS", [128, 4, DCONV], F32)
            R_scaS = Res("scaS")
            caccS = sb(SS, "caccS", [128, 2, 128], F32)
            R_caccS = Res("caccS")
            csqS = sb(SS, "csqS", [128, 2, 128], F32)
            R_csqS = Res("csqS")
            lnAs = sb(SS, "lnAs", [128, 128], F32)
            R_lnAs = Res("lnAs")
            lnBs = sb(SS, "lnBs", [128, 128], F32)
            R_lnBs = Res("lnBs")
            aTs = sb(SS, "aTs", [128, 2, 128], BF16)
            R_aTs = Res("aTs")
            Qz = sb(SS, "Qz", [128, 6, 16, 2, 8], BF16)
            R_Qz = Res("Qz")
            KNT = sb(SS, "KNT", [128, 6, 128], BF16)
            R_KNT = Res("KNT")
            KA = [sb(SS, "KA%d" % i, [128, 8, DATT], BF16) for i in range(2)]
            KB = [sb(SS, "KB%d" % i, [128, 4, DATT], BF16) for i in range(2)]
            VA = [sb(SS, "VA%d" % i, [128, 8, DATT], BF16) for i in range(2)]
            VB = [sb(SS, "VB%d" % i, [128, 4, DATT], BF16) for i in range(2)]
            R_KA = [Res("KA0"), Res("KA1")]
            R_KB = [Res("KB0"), Res("KB1")]
            R_VA = [Res("VA0"), Res("VA1")]
            R_VB = [Res("VB0"), Res("VB1")]
            KTs = [sb(SS, "KTs%d" % i, [128, 12, 128], BF16) for i in range(2)]
            R_KTs = [Res("KTs0"), Res("KTs1")]
            pts = [sb(SS, "pts%d" % i, [128, 96], BF16) for i in range(2)]
            R_pts = [Res("pts0"), Res("pts1")]
            smf = sb(SS, "smf", [128, 16 * 96], F32)
            R_smf = Res("smf")
            smb = sb(SS, "smb", [128, 16, 96], BF16)
            onesb = sb(SS, "onesb", [128, 128], BF16)
            identf = sb(SS, "identf", [128, 128], F32)
            R_cs = Res("consts_s")
            OTs = sb(SS, "OTs", [128, 6, 128], BF16)
            R_OTs = Res("OTs")
            tmpo = sb(SS, "tmpo", [128, 96], F32)
            R_tmpo = Res("tmpo")
            rds = sb(SS, "rds", [128, 96], F32)
            R_rds = Res("rds")

            pjs = [ps(SS, "pjs%d" % i, [128, 512], F32) for i in range(2)]
            R_pjs = [Res("pjs0"), Res("pjs1")]
            trs = [ps(SS, "trs%d" % i, [128, 1024], BF16) for i in range(2)]
            R_trs = [Res("trs0"), Res("trs1")]
            scs = [ps(SS, "scs%d" % i, [128, 512], F32) for i in range(2)]
            R_scs = [Res("scs0"), Res("scs1")]
            nums = ps(SS, "nums", [128, 512], F32)
            R_nums = Res("nums")
            dens = ps(SS, "dens", [128, 512], F32)
            R_dens = Res("dens")
            sts = {"pj": 0, "tr": 0, "sc": 0, "pt": 0, "kt": 0}

            def nxs(k, n=2):
                v = sts[k]
                sts[k] = (v + 1) % n
                return v

            P.dma(smf[:, :], smaskd[:, :], [], [R_smf], R_smf)
            P.op("dve", _mk("tensor_copy", out=smb[:, :, :].rearrange("p a b -> p (a b)"), in_=smf[:, :]), [R_smf], [R_cs])
            P.op("pool", _mk("memset", onesb[:, :], 1.0), [], [R_cs])
            P.dma(identf[:, :], identd[:, :], [], [R_cs], R_cs)
            P.op("pool", _mk("memset", Qz[:, :, :, :, :].rearrange("p a b c d -> p (a b c d)"), 0.0), [], [R_Qz])

            def load_cache(s):
                b = s % 2
                ka = ck[s].rearrange("(g q) f -> g q f", q=16)
                va = cv[s].rearrange("(g q) f -> g q f", q=16)
                P.dma(KA[b][:, :, :], ka[:, 0:8, :], [], [R_KA[b]], R_KA[b], q="pool")
                P.dma(KB[b][:, :, :], ck[s, 1536:2048, :].rearrange("(m r) f -> m r f", r=4), [], [R_KB[b]], R_KB[b], q="pool")
                P.dma(VA[b][:, :, :], va[:, 0:8, :], [], [R_VA[b]], R_VA[b], q="pool")
                P.dma(VB[b][:, :, :], cv[s, 1536:2048, :].rearrange("(m r) f -> m r f", r=4), [], [R_VB[b]], R_VB[b], q="pool")

            load_cache(0)
            load_cache(1)

            P.dma(xts[:, :], xs[:, :], [], [R_xts], R_xts)
            P.op("act", _mk("activation", out=hbs[:, :], in_=xts[:, :], func=AF.Square, accum_out=sqs[:, 0:1]),
                 [R_xts], [R_hbs, R_sqs])
            P.op("dve", _mk("tensor_scalar", out=sqs[:, 1:2], in0=sqs[:, 0:1], scalar1=1.0 / D, scalar2=EPS,
                            op0=ALU.mult, op1=ALU.add), [R_sqs], [R_sqs])
            P.op("act", _mk("activation", out=sqs[:, 3:4], in_=sqs[:, 1:2], func=AF.Ln), [R_sqs], [R_sqs])
            P.op("act", _mk("activation", out=sqs[:, 2:3], in_=sqs[:, 3:4], func=AF.Exp, scale=-0.5), [R_sqs], [R_sqs])
            P.op("act", _mk("activation", out=hbs[:, :], in_=xts[:, :], func=AF.Copy, scale=sqs[:, 2:3]),
                 [R_xts, R_sqs], [R_hbs])
            t = nxs("tr")
            for c in range(8):
                P.op("pe", _mk("transpose", out=trs[t][:, c * 128:(c + 1) * 128], in_=hbs[:, c * 128:(c + 1) * 128],
                               identity=identb[:, :]), [R_hbs, R_c], [R_trs[t]], inc=(c == 7))
            P.op("dve", _mk("tensor_copy", out=hTs[:, :, :], in_=trs[t][:, :].rearrange("p (c k) -> p c k", c=8)),
                 [R_trs[t]], [R_hTs])
            for g6 in range(6):
                w0 = g6 * 512
                wn = min(512, DIN - w0)
                pb = nxs("pj")
                for c in range(8):
                    P.op("pe", _mk("matmul", pjs[pb][:, 0:wn], hTs[:, c, :], win_sb[:, c, w0:w0 + wn], start=(c == 0),
                                   stop=(c == 7)), [R_hTs, R_w], [R_pjs[pb]], inc=(c == 7))
                if g6 % 2 == 0:
                    P.op("act", _mk("activation", out=tmS[:, w0:w0 + wn], in_=pjs[pb][:, 0:wn], func=AF.Copy),
                         [R_pjs[pb]], [R_tmS])
                else:
                    P.op("dve", _mk("tensor_copy", out=tmS[:, w0:w0 + wn], in_=pjs[pb][:, 0:wn]), [R_pjs[pb]], [R_tmS])
            out_tickets.append(P.dma(ksn[:, :], tmS[:, 1280:2048], [R_tmS], [], R_tmS))
            out_tickets.append(P.dma(vsn[:, :], tmS[:, 2048:2816], [R_tmS], [], R_tmS))
            P.op("dve", _mk("tensor_copy", out=VN[:, :], in_=tmS[:, 2048:2816]), [R_tmS], [R_VN])
            P.op("act", _mk("activation", out=gluS[:, 256:512], in_=tmS[:, 256:512], func=AF.Sigmoid), [R_tmS], [R_gluS])
            P.op("dve", _mk("tensor_tensor", out=gluS[:, 0:256], in0=tmS[:, 0:256], in1=gluS[:, 256:512], op=ALU.mult),
                 [R_tmS, R_gluS], [R_gluS])
            out_tickets.append(P.dma(cas[:, 0:22, :], sca[:, 8:30, :], [], [], R_gluS))
            for s in range(16):
                out_tickets.append(P.dma(cas[s, 22:30, :], gluS[8 * s:8 * s + 8, 0:256], [R_gluS], [], R_gluS))
            P.dma(scaS[0:120, :, :], sca.rearrange("(a s) t c -> (s t) a c", a=4), [], [R_scaS], R_scaS)
            for a4 in range(4):
                for f in range(2):
                    pb = nxs("pj")
                    P.op("pe", _mk("transpose", out=pjs[pb][:, 0:120], in_=scaS[0:120, a4, f * 128:(f + 1) * 128],
                                   identity=identf[0:120, 0:120]), [R_scaS, R_cs], [R_pjs[pb]])
                    P.op("dve", _mk("tensor_copy", out=gluSF[:, f, 4 * a4:4 * a4 + 4, 0:30],
                                    in_=pjs[pb][:, 0:120].rearrange("p (s t) -> p s t", s=4)), [R_pjs[pb]], [R_gluSF])
            for f in range(2):
                pb = nxs("pj")
                P.op("pe", _mk("transpose", out=pjs[pb][:, 0:128], in_=gluS[:, f * 128:(f + 1) * 128], identity=identf[:, :]),
                     [R_gluS, R_cs], [R_pjs[pb]])
                P.op("dve", _mk("tensor_copy", out=gluSF[:, f, :, 30:38],
                                in_=pjs[pb][:, 0:128].rearrange("p (s t) -> p s t", s=16)), [R_pjs[pb]], [R_gluSF])
            for f in range(2):
                cv_ = caccS[:, f, :].rearrange("p (s t) -> p s t", s=16)
                P.op("dve", _mk("tensor_scalar", out=cv_, in0=gluSF[:, f, :, 0:8], scalar1=caw_sb[:, f, 0:1],
                                scalar2=cab_sb[:, f:f + 1], op0=ALU.mult, op1=ALU.add), [R_gluSF, R_c], [R_caccS])
                for k in range(1, 31):
                    P.op("dve", _mk("scalar_tensor_tensor", out=cv_, in0=gluSF[:, f, :, k:k + 8], scalar=caw_sb[:, f, k:k + 1],
                                    in1=cv_, op0=ALU.mult, op1=ALU.add), [R_gluSF, R_c, R_caccS], [R_caccS])
            pm = nxs("pj")
            for f in range(2):
                P.op("pe", _mk("matmul", pjs[pm][:, 0:128], ones256[:, :], caccS[:, f, :], start=(f == 0), stop=(f == 1)),
                     [R_caccS, R_c], [R_pjs[pm]], inc=(f == 1))
            P.op("act", _mk("activation", out=csqS[:, :, :].rearrange("p a b -> p (a b)"),
                            in_=caccS[:, :, :].rearrange("p a b -> p (a b)"), func=AF.Square), [R_caccS], [R_csqS])
            pq = nxs("pj")
            for f in range(2):
                P.op("pe", _mk("matmul", pjs[pq][:, 0:128], ones256[:, :], csqS[:, f, :], start=(f == 0), stop=(f == 1)),
                     [R_csqS, R_c], [R_pjs[pq]], inc=(f == 1))
            P.op("act", _mk("activation", out=lnAs[:, :], in_=pjs[pm][:, 0:128], func=AF.Square), [R_pjs[pm]], [R_lnAs])
            P.op("dve", _mk("tensor_tensor", out=lnAs[:, :], in0=pjs[pq][:, 0:128], in1=lnAs[:, :], op=ALU.subtract),
                 [R_pjs[pq], R_lnAs], [R_lnAs])
            P.op("dve", _mk("tensor_scalar", out=lnAs[:, :], in0=lnAs[:, :], scalar1=EPS, scalar2=None, op0=ALU.add),
                 [R_lnAs], [R_lnAs])
            P.op("act", _mk("activation", out=lnAs[:, :], in_=lnAs[:, :], func=AF.Ln), [R_lnAs], [R_lnAs])
            P.op("act", _mk("activation", out=lnAs[:, :], in_=lnAs[:, :], func=AF.Exp, scale=-0.5), [R_lnAs], [R_lnAs])
            for f in range(2):
                P.op("dve", _mk("tensor_tensor", out=lnBs[:, :], in0=caccS[:, f, :], in1=pjs[pm][:, 0:128], op=ALU.subtract),
                     [R_caccS, R_pjs[pm]], [R_lnBs])
                P.op("dve", _mk("tensor_tensor", out=lnBs[:, :], in0=lnBs[:, :], in1=lnAs[:, :], op=ALU.mult),
                     [R_lnBs, R_lnAs], [R_lnBs])
                P.op("act", _mk("activation", out=aTs[:, f, :], in_=lnBs[:, :], func=AF.Silu, scale=lng_sb[:, f:f + 1],
                                bias=lnb_sb[:, f:f + 1]), [R_lnBs, R_c], [R_aTs])
            for hp in range(6):
                pb = nxs("pj")
                for c in range(8):
                    P.op("pe", _mk("matmul", pjs[pb][:, 0:128], win_sb[:, c, (4 + hp) * 128:(5 + hp) * 128], hTs[:, c, :],
                                   start=(c == 0), stop=(c == 7)), [R_w, R_hTs], [R_pjs[pb]], inc=(c == 7))
                P.op("act", _mk("activation", out=Qz[0:64, hp, :, 0, :],
                                in_=pjs[pb][0:64, 0:128].rearrange("p (s t) -> p s t", s=16), func=AF.Copy), [R_pjs[pb]], [R_Qz])
                P.op("dve", _mk("tensor_copy", out=Qz[64:128, hp, :, 1, :],
                                in_=pjs[pb][64:128, 0:128].rearrange("p (s t) -> p s t", s=16)), [R_pjs[pb]], [R_Qz])
            for hp in range(6):
                pb = nxs("pj")
                for c in range(8):
                    P.op("pe", _mk("matmul", pjs[pb][:, 0:128], win_sb[:, c, (10 + hp) * 128:(11 + hp) * 128], hTs[:, c, :],
                                   start=(c == 0), stop=(c == 7)), [R_w, R_hTs], [R_pjs[pb]], inc=(c == 7))
                P.op("act", _mk("activation", out=KNT[:, hp, :], in_=pjs[pb][:, 0:128], func=AF.Copy), [R_pjs[pb]], [R_KNT])

            for s in range(16):
                b = s % 2
                first = [True]
                for hp in range(6):
                    fs = slice(128 * hp, 128 * hp + 128)
                    kt = nxs("kt")
                    t0 = nxs("tr")
                    for r in range(4):
                        P.op("pe", _mk("transpose", out=trs[t0][:, r * 128:(r + 1) * 128], in_=KB[b][:, r, fs],
                                       identity=identb[:, :]), [R_KB[b], R_c], [R_trs[t0]], inc=False)
                    for tq in range(4):
                        P.op("pe", _mk("transpose", out=trs[t0][:, (4 + tq) * 128:(5 + tq) * 128], in_=KA[b][:, tq, fs],
                                       identity=identb[:, :]), [R_KA[b], R_c], [R_trs[t0]], inc=(tq == 3))
                    P.op("dve", _mk("tensor_copy", out=KTs[kt][:, 0:8, :].rearrange("p a b -> p (a b)"), in_=trs[t0][:, :]),
                         [R_trs[t0]], [R_KTs[kt]])
                    t1 = nxs("tr")
                    for tq in range(4, 8):
                        P.op("pe", _mk("transpose", out=trs[t1][:, (tq - 4) * 128:(tq - 3) * 128], in_=KA[b][:, tq, fs],
                                       identity=identb[:, :]), [R_KA[b], R_c], [R_trs[t1]], inc=(tq == 7))
                    P.op("act", _mk("activation", out=KTs[kt][:, 8:12, :].rearrange("p a b -> p (a b)"), in_=trs[t1][:, 0:512],
                                    func=AF.Copy), [R_trs[t1]], [R_KTs[kt]])
                    sc_ = nxs("sc")
                    qall = Qz[:, hp, s, :, :].rearrange("p a b -> p (a b)")
                    for r in range(4):
                        P.op("pe", _mk("matmul", scs[sc_][:, 16 * r:16 * r + 16], KTs[kt][:, r, :], qall, start=True, stop=True),
                             [R_KTs[kt], R_Qz], [R_scs[sc_]], inc=False)
                    for tq in range(8):
                        P.op("pe", _mk("matmul", scs[sc_][:, 64 + 2 * tq:66 + 2 * tq], KTs[kt][:, 4 + tq, :], Qz[:, hp, s, :, tq],
                                       start=True, stop=True), [R_KTs[kt], R_Qz], [R_scs[sc_]], inc=False)
                    P.op("pe", _mk("matmul", scs[sc_][:, 80:96], KNT[:, hp, :], qall, start=True, stop=True),
                         [R_KNT, R_Qz], [R_scs[sc_]])
                    pi = nxs("pt")
                    P.op("act", _mk("activation", out=pts[pi][:, :], in_=scs[sc_][:, 0:96], func=AF.Exp, scale=SCALE),
                         [R_scs[sc_]], [R_pts[pi]])
                    P.op("dve", _mk("tensor_tensor", out=pts[pi][:, :], in0=pts[pi][:, :], in1=smb[:, s, :], op=ALU.mult),
                         [R_pts[pi], R_cs], [R_pts[pi]])
                    ncol = nums[:, 16 * hp:16 * hp + 16]
                    dcol = dens[:, 16 * hp:16 * hp + 16]
                    for r in range(4):
                        st_ = first[0]
                        first[0] = False
                        P.op("pe", _mk("matmul", ncol, VB[b][:, r, fs], pts[pi][:, 16 * r:16 * r + 16], start=st_, stop=True,
                                       skip_group_check=True), [R_VB[b], R_pts[pi]], [R_nums], inc=False)
                        P.op("pe", _mk("matmul", dcol, onesb[:, :], pts[pi][:, 16 * r:16 * r + 16], start=st_, stop=True,
                                       skip_group_check=True), [R_cs, R_pts[pi]], [R_dens], inc=False)
                    for tq in range(8):
                        ncol2 = nums[:, 16 * hp + tq:16 * hp + 16:8]
                        dcol2 = dens[:, 16 * hp + tq:16 * hp + 16:8]
                        P.op("pe", _mk("matmul", ncol2, VA[b][:, tq, fs], pts[pi][:, 64 + 2 * tq:66 + 2 * tq], start=False,
                                       stop=True, skip_group_check=True), [R_VA[b], R_pts[pi]], [R_nums], inc=False)
                        P.op("pe", _mk("matmul", dcol2, onesb[:, :], pts[pi][:, 64 + 2 * tq:66 + 2 * tq], start=False,
                                       stop=True, skip_group_check=True), [R_cs, R_pts[pi]], [R_dens], inc=False)
                    P.op("pe", _mk("matmul", ncol, VN[:, fs], pts[pi][:, 80:96], start=False, stop=True, skip_group_check=True),
                         [R_VN, R_pts[pi]], [R_nums], inc=False)
                    P.op("pe", _mk("matmul", dcol, onesb[:, :], pts[pi][:, 80:96], start=False, stop=True, skip_group_check=True),
                         [R_cs, R_pts[pi]], [R_dens, R_nums])
                P.op("dve", _mk("reciprocal", out=rds[:, :], in_=dens[:, 0:96]), [R_dens], [R_rds])
                P.op("dve", _mk("tensor_tensor", out=tmpo[:, :], in0=nums[:, 0:96], in1=rds[:, :], op=ALU.mult),
                     [R_nums, R_rds], [R_tmpo])
                tv = tmpo[:, :].rearrange("p (a h t) -> p a h t", a=6, h=2)
                P.op("act", _mk("activation", out=OTs[0:64, :, 8 * s:8 * s + 8], in_=tv[0:64, :, 0, :], func=AF.Copy),
                     [R_tmpo], [R_OTs])
                P.op("act", _mk("activation", out=OTs[64:128, :, 8 * s:8 * s + 8], in_=tv[64:128, :, 1, :], func=AF.Copy),
                     [R_tmpo], [R_OTs])
                if s + 2 < 16:
                    load_cache(s + 2)
            for nh in range(2):
                pb = nxs("pj")
                for f in range(2):
                    P.op("pe", _mk("matmul", pjs[pb][:, :], aTs[:, f, :], wouta[:, f, 512 * nh:512 * nh + 512], start=(f == 0),
                                   stop=False), [R_aTs, R_w], [R_pjs[pb]], inc=False)
                for hp in range(6):
                    P.op("pe", _mk("matmul", pjs[pb][:, :], OTs[:, hp, :], wouto[:, hp, 512 * nh:512 * nh + 512], start=False,
                                   stop=(hp == 5)), [R_OTs, R_w], [R_pjs[pb]], inc=(hp == 5))
                P.op("dve", _mk("tensor_tensor", out=xts[:, 512 * nh:512 * nh + 512], in0=xts[:, 512 * nh:512 * nh + 512],
                                in1=pjs[pb][:, :], op=ALU.add), [R_xts, R_pjs[pb]], [R_xts])
            P.dma(xmid[33 * 128:34 * 128, :], xts[:, :], [R_xts], [xmid_res[33]], R_xts)
            P.barrier()
            P.flush()

    with ExitStack() as SB:
        wg_sb = sb(SB, "wg_sb", [128, 8, DFF], BF16)
        wu_sb = sb(SB, "wu_sb", [128, 8, DFF], BF16)
        wd_sb = sb(SB, "wd_sb", [128, NFF, D], BF16)
        R_w2 = Res("weights2")
        R_c2 = Res("consts2")
        gffn_sb = sb(SB, "gffn_sb", [128, 8], F32)
        fcw_sb = sb(SB, "fcw_sb", [128, NFF, 3], F32)
        fcb_sb = sb(SB, "fcb_sb", [128, NFF], F32)
        gfin_sb = sb(SB, "gfin_sb", [128, D], F32)
        identb2 = sb(SB, "identb2", [128, 128], BF16)
        identf2 = sb(SB, "identf2", [128, 128], F32)
        flag2 = sb(SB, "flag2", [128, 8], F32)
        hist = sb(SB, "hist", [128, NFF, 2], F32)
        negh = sb(SB, "negh", [128, 8], F32)
        R_hist = Res("hist")

        with ExitStack() as SB0:
            stg2 = [sb(SB0, "stg2_%d" % i, [128, DFF], F32) for i in range(2)]
            R_stg2 = [Res("stg2_0"), Res("stg2_1")]
            for dst, src in ((gffn_sb, gffn), (fcb_sb, fcb), (gfin_sb, gfin), (flag2, flagsd), (identf2, identd)):
                r = Res("v")
                P.dma(dst[:, :], src[:, :], [], [r, R_c2], r)
            r = Res("v")
            P.dma(fcw_sb[:, :, :].rearrange("p a b -> p (a b)"), fcw[:, :], [], [r, R_c2], r)
            P.op("dve", _mk("tensor_copy", out=identb2[:, :], in_=identf2[:, :]), [R_c2], [R_c2])
            R_wdq = Res("wdq")
            for ff in range(0, NFF, 2):
                P.dma(wd_sb[:, ff:ff + 2, :], wd[ff * 128:(ff + 2) * 128, :].rearrange("(a p) d -> p a d", p=128),
                      [], [], R_wdq, q="pool")
            k = 0
            for (wsrc, wdst) in ((wg, wg_sb), (wu, wu_sb)):
                for c in range(8):
                    s = k % 2
                    k += 1
                    P.dma(stg2[s][:, :], wsrc[c * 128:(c + 1) * 128, :], [], [R_stg2[s]], R_stg2[s])
                    if k % 2 == 0:
                        P.op("act", _mk("activation", out=wdst[:, c, :], in_=stg2[s][:, :], func=AF.Copy,
                                        scale=gffn_sb[:, c:c + 1]), [R_stg2[s], R_c2], [])
                    else:
                        P.op("dve", _mk("tensor_scalar", out=wdst[:, c, :], in0=stg2[s][:, :], scalar1=gffn_sb[:, c:c + 1],
                                        scalar2=None, op0=ALU.mult), [R_stg2[s], R_c2], [])
            P.op("pool", _mk("memset", hist[:, :, :].rearrange("p a b -> p (a b)"), 0.0), [], [R_hist])
            P.op("pool", _mk("memset", negh[:, :], -0.5), [], [R_c2])
            P.barrier()
            P.flush()

        def phase2_body(S, tag, sample):
            NXM = 2 if sample else 6
            xm = [sb(S, "xm%s%d" % (tag, i), [128, D], F32) for i in range(NXM)]
            R_xm = [Res("xm%d" % i) for i in range(NXM)]
            NHB = 1 if sample else 2
            hb2s = [sb(S, "hb2%s%d" % (tag, i), [128, D], BF16) for i in range(NHB)]
            R_hb2s = [Res("hb2_%d" % i) for i in range(NHB)]
            ss2 = sb(S, "ss2" + tag, [128, 8], F32)
            R_ss2 = Res("ss2")
            R_ss2f = Res("ss2f")
            NH2 = 1 if sample else 2
            h2T = [sb(S, "h2T%s%d" % (tag, i), [128, 8, 256], BF16) for i in range(NH2)]
            R_h2T = [Res("h2T%d" % i) for i in range(NH2)]
            NEB = 2 if sample else 4
            gbufs = [sb(S, "gbuf%s%d" % (tag, i), [128, 2 + 256], F32) for i in range(NEB)]
            R_gbufs = [Res("gbuf%d" % i) for i in range(NEB)]
            R_ghs = [Res("gh%d" % i) for i in range(NEB)]
            gcvs = [sb(S, "gcv%s%d" % (tag, i), [128, 256], F32) for i in range(NEB)]
            R_gcvs = [Res("gcv%d" % i) for i in range(NEB)]
            upsbs = [sb(S, "upsb%s%d" % (tag, i), [128, 256], F32) for i in range(NEB)]
            R_upsbs = [Res("upsb%d" % i) for i in range(NEB)]
            uT = [sb(S, "uT%s%d" % (tag, i), [128, NFF, 256], BF16) for i in range(NH2)]
            R_uT = [Res("uT%d" % i) for i in range(NH2)]
            if sample:
                scfS = sb(S, "scfS", [32, DFF], F32)
                R_scfS = Res("scfS")
                cfoS = sb(S, "cfoS", [128, DFF], F32)
                R_cfoS = Res("cfoS")
                shist = sb(S, "shist", [128, NFF, 32], F32)
                R_shist = Res("shist")
            NPG, NPU = 2, 4
            pg = [ps(S, "pg%s%d" % (tag, i), [128, 512], F32) for i in range(NPG)]
            R_pg = [Res("pg%d" % i) for i in range(NPG)]
            pu = [ps(S, "pu%s%d" % (tag, i), [128, 512], F32) for i in range(NPU)]
            R_pu = [Res("pu%d" % i) for i in range(NPU)]
            pd = [ps(S, "pd%s%d" % (tag, i), [128, 512], F32) for i in range(1)]
            R_pd = [Res("pd0")]
            tr2 = [ps(S, "tr2%s%d" % (tag, i), [128, 1024], BF16) for i in range(1)]
            R_tr2 = [Res("tr2_0")]
            st2 = {"pg": 0, "pu": 0, "pd": 0, "tr": 0, "xm": 0}

            def nxt2(k_, n=2):
                v = st2[k_]
                st2[k_] = (v + 1) % n
                return v

            def norm_part(blk, jcol, defer=False):
                b = nxt2("xm", NXM)
                hb2, R_hb2 = hb2s[jcol % NHB], R_hb2s[jcol % NHB]
                P.dma(xm[b][:, :], xmid[blk * 128:(blk + 1) * 128, :], [xmid_res[blk]], [R_xm[b]], R_xm[b])
                P.op("act", _mk("activation", out=hb2[:, :], in_=xm[b][:, :], func=AF.Square, accum_out=ss2[:, 0:1]),
                     [R_xm[b]], [R_hb2, R_ss2])
                P.op("dve", _mk("tensor_scalar", out=ss2[:, 1:2], in0=ss2[:, 0:1], scalar1=1.0 / D, scalar2=EPS,
                                op0=ALU.mult, op1=ALU.add), [R_ss2], [R_ss2])
                P.op("pool", _mk("tensor_tensor", out=ss2[:, 2:3], in0=ss2[:, 1:2], in1=negh[:, 0:1], op=ALU.pow),
                     [R_ss2, R_c2], [R_ss2])
                if not defer:
                    norm_part2(b, jcol)
                return b

            def norm_part2(b, jcol):
                hb2, R_hb2 = hb2s[jcol % NHB], R_hb2s[jcol % NHB]
                P.op("act", _mk("activation", out=hb2[:, :], in_=xm[b][:, :], func=AF.Copy, scale=ss2[:, 2:3]),
                     [R_xm[b], R_ss2], [R_hb2])

            def trans_part(jcol, hsel):
                hb2, R_hb2 = hb2s[jcol % NHB], R_hb2s[jcol % NHB]
                t = nxt2("tr", 1)
                for c in range(8):
                    P.op("pe", _mk("transpose", out=tr2[t][:, c * 128:(c + 1) * 128], in_=hb2[:, c * 128:(c + 1) * 128],
                                   identity=identb2[:, :]), [R_hb2, R_c2], [R_tr2[t]], inc=(c == 7))
                P.op("dve", _mk("tensor_copy", out=h2T[hsel][:, :, 128 * jcol:128 * jcol + 128],
                                in_=tr2[t][:, :].rearrange("p (c k) -> p c k", c=8)), [R_tr2[t]], [R_h2T[hsel]])

            def load_norm_T(blk, jcol, hsel):
                b = norm_part(blk, jcol)
                trans_part(jcol, hsel)
                return b

            def gate_rows_only(ncols, hsel):
                for ff in range(NFF):
                    g = nxt2("pg", NPG)
                    for c in range(8):
                        P.op("pe", _mk("matmul", pg[g][:, 0:2], wg_sb[:, c, ff * 128:(ff + 1) * 128], h2T[hsel][:, c, ncols - 2:ncols],
                                       start=(c == 0), stop=(c == 7)), [R_w2, R_h2T[hsel]], [R_pg[g]], inc=(c == 7))
                    P.op("dve", _mk("tensor_scalar", out=hist[:, ff, :], in0=pg[g][:, 0:2], scalar1=flag2[:, 0:1],
                                    scalar2=None, op0=ALU.mult), [R_pg[g], R_c2], [R_hist])

            def gate_up(ncols, hsel, hooks=None):
                hooks = hooks or {}
                tail = [None]
                for ff in range(NFF):
                    g = nxt2("pg", NPG)
                    u = nxt2("pu", NPU)
                    gbuf, R_gbuf = gbufs[ff % NEB], R_gbufs[ff % NEB]
                    gcv, R_gcv = gcvs[ff % NEB], R_gcvs[ff % NEB]
                    upsb, R_upsb = upsbs[ff % NEB], R_upsbs[ff % NEB]
                    for c in range(8):
                        P.op("pe", _mk("matmul", pg[g][:, 0:ncols], wg_sb[:, c, ff * 128:(ff + 1) * 128], h2T[hsel][:, c, 0:ncols],
                                       start=(c == 0), stop=(c == 7)), [R_w2, R_h2T[hsel]], [R_pg[g]], inc=(c == 7))
                    for c in range(8):
                        P.op("pe", _mk("matmul", pu[u][:, 0:ncols], wu_sb[:, c, ff * 128:(ff + 1) * 128], h2T[hsel][:, c, 0:ncols],
                                       start=(c == 0), stop=(c == 7)), [R_w2, R_h2T[hsel]], [R_pu[u]], inc=(c == 7))
                    R_gh = R_ghs[ff % NEB]
                    if not sample:
                        P.op("dve", _mk("tensor_copy", out=gbuf[:, 0:2], in_=hist[:, ff, :]), [R_hist], [R_gh])
                        P.op("act", _mk("activation", out=gbuf[:, 2:2 + ncols], in_=pg[g][:, 0:ncols], func=AF.Copy),
                             [R_pg[g]], [R_gbuf])
                        P.op("dve", _mk("tensor_copy", out=hist[:, ff, :], in_=gbuf[:, ncols:ncols + 2]), [R_gbuf], [R_hist])
                        srcs = [gbuf[:, k_:k_ + ncols] for k_ in range(3)]
                        gout = gcv[:, 0:ncols]
                    else:
                        gv = gbuf[:, 0:160].rearrange("p (s t) -> p s t", s=16)
                        P.op("pool", _mk("tensor_copy", out=gv[:, :, 0:2], in_=shist[:, ff, :].rearrange("p (s t) -> p s t", s=16)),
                             [R_shist], [R_gbuf])
                        P.op("act", _mk("activation", out=gv[:, :, 2:10], in_=pg[g][:, 0:128].rearrange("p (s t) -> p s t", s=16),
                                        func=AF.Copy), [R_pg[g]], [R_gbuf])
                        srcs = [gv[:, :, k_:k_ + 8] for k_ in range(3)]
                        gout = gcv[:, 0:128].rearrange("p (s t) -> p s t", s=16)
                    P.op("dve", _mk("tensor_scalar", out=gout, in0=srcs[0], scalar1=fcw_sb[:, ff, 0:1], scalar2=fcb_sb[:, ff:ff + 1],
                                    op0=ALU.mult, op1=ALU.add), [R_gbuf, R_gh, R_c2], [R_gcv])
                    for k_ in (1, 2):
                        P.op("dve", _mk("scalar_tensor_tensor", out=gout, in0=srcs[k_], scalar=fcw_sb[:, ff, k_:k_ + 1], in1=gout,
                                        op0=ALU.mult, op1=ALU.add), [R_gbuf, R_gh, R_c2, R_gcv], [R_gcv])
                    if tail[0] is not None:
                        tail[0]()

                    def mk_tail(ff=ff, gcv=gcv, R_gcv=R_gcv, u=u):
                        def t_():
                            P.op("act", _mk("activation", out=gcv[:, 0:ncols], in_=gcv[:, 0:ncols], func=AF.Silu), [R_gcv], [R_gcv])
                            P.op("dve", _mk("tensor_tensor", out=uT[hsel][:, ff, 0:ncols], in0=gcv[:, 0:ncols],
                                            in1=pu[u][:, 0:ncols], op=ALU.mult), [R_gcv, R_pu[u]], [R_uT[hsel]])
                        return t_
                    tail[0] = mk_tail()
                    for h_ in hooks.pop(ff, []):
                        h_()
                tail[0]()

            def down_group(jcol, b, nh, hsel):
                d = nxt2("pd", 1)
                for ff in range(NFF):
                    P.op("pe", _mk("matmul", pd[d][:, :], uT[hsel][:, ff, 128 * jcol:128 * jcol + 128],
                                   wd_sb[:, ff, 512 * nh:512 * nh + 512], start=(ff == 0), stop=(ff == NFF - 1)),
                         [R_uT[hsel], R_w2], [R_pd[d]], inc=(ff == NFF - 1))
                P.op("dve", _mk("tensor_tensor", out=xm[b][:, 512 * nh:512 * nh + 512], in0=xm[b][:, 512 * nh:512 * nh + 512],
                                in1=pd[d][:, :], op=ALU.add), [R_xm[b], R_pd[d]], [R_xm[b]])

            def final_part(b, out_ap, jsel=0, defer=False):
                hb2, R_hb2 = hb2s[jsel % NHB], R_hb2s[jsel % NHB]
                P.op("act", _mk("activation", out=hb2[:, :], in_=xm[b][:, :], func=AF.Square, accum_out=ss2[:, 4:5]),
                     [R_xm[b]], [R_hb2, R_ss2f])
                P.op("dve", _mk("tensor_scalar", out=ss2[:, 5:6], in0=ss2[:, 4:5], scalar1=1.0 / D, scalar2=EPS,
                                op0=ALU.mult, op1=ALU.add), [R_ss2f], [R_ss2f])
                P.op("pool", _mk("tensor_tensor", out=ss2[:, 6:7], in0=ss2[:, 5:6], in1=negh[:, 0:1], op=ALU.pow),
                     [R_ss2f, R_c2], [R_ss2f])
                if not defer:
                    final_part2(b, out_ap)

            def final_part2(b, out_ap):
                P.op("dve", _mk("scalar_tensor_tensor", out=xm[b][:, :], in0=xm[b][:, :], scalar=ss2[:, 6:7], in1=gfin_sb[:, :],
                                op0=ALU.mult, op1=ALU.mult), [R_xm[b], R_ss2f, R_c2], [R_xm[b]])
                out_tickets.append(P.dma(out_ap, xm[b][:, :], [R_xm[b]], [], R_xm[b]))

            def down_final(bufs, hsel, out_ap_fn):
                for jcol, b in enumerate(bufs):
                    for nh in range(2):
                        down_group(jcol, b, nh, hsel)
                    final_part(b, out_ap_fn(jcol), jcol)

            if not sample:
                load_norm_T(0, 0, 1)
                bufs_next = [load_norm_T(1, 0, 0), load_norm_T(2, 1, 0)]
                gate_rows_only(128, 1)
                prev = None
                for t2 in range(16):
                    hsel = t2 % 2
                    bufs_cur = bufs_next
                    box = {"b": [None, None]}
                    hooks = {}
                    if t2 + 1 < 16:
                        nsel = (t2 + 1) % 2
                        for jc in range(2):
                            hooks.setdefault(1 + 2 * jc, []).append(
                                lambda jc=jc, t2=t2, box=box: box["b"].__setitem__(jc, norm_part(3 + 2 * t2 + jc, jc, defer=True)))
                            hooks.setdefault(2 + 2 * jc, []).append(lambda jc=jc, box=box: norm_part2(box["b"][jc], jc))
                            hooks.setdefault(5 + 2 * jc, []).append(lambda jc=jc, nsel=nsel: trans_part(jc, nsel))
                    if prev is not None:
                        pbufs, phsel, pout = prev
                        for jc in range(2):
                            for nh in range(2):
                                hooks.setdefault(7 + 6 * jc + 3 * nh, []).append(
                                    lambda jc=jc, nh=nh, pbufs=pbufs, phsel=phsel: down_group(jc, pbufs[jc], nh, phsel))
                            hooks.setdefault(13 + 6 * jc, []).append(
                                lambda jc=jc, pbufs=pbufs, pout=pout: final_part(pbufs[jc], pout(jc), jc, defer=True))
                            hooks.setdefault(14 + 6 * jc, []).append(
                                lambda jc=jc, pbufs=pbufs, pout=pout: final_part2(pbufs[jc], pout(jc)))
                    gate_up(256, hsel, hooks=hooks)
                    assert not hooks
                    prev = (bufs_cur, hsel, (lambda jcol, t2=t2: y[(2 * t2 + jcol) * 128:(2 * t2 + jcol + 1) * 128, :]))
                    bufs_next = box["b"]
                bfree = [i_ for i_ in range(NXM) if i_ not in prev[0]][0]
                cfo, R_cfo = xm[bfree], R_xm[bfree]
                for g6 in range(6):
                    w0 = g6 * 512
                    wn = min(512, DFF - w0)
                    g = nxt2("pg", NPG)
                    for c in range(8):
                        P.op("pe", _mk("matmul", pg[g][:, 0:wn], h2T[1][:, c, 128:256], wg_sb[:, c, w0:w0 + wn],
                                       start=(c == 0), stop=(c == 7)), [R_h2T[1], R_w2], [R_pg[g]], inc=(c == 7))
                    P.op("act", _mk("activation", out=cfo[:, 0:wn], in_=pg[g][:, 0:wn], func=AF.Copy), [R_pg[g]], [R_cfo])
                    out_tickets.append(P.dma(cfp[:, w0:w0 + wn], cfo[126:128, 0:wn], [R_cfo], [], R_cfo))
                down_final(*prev)
            else:
                P.dma(scfS[0:32, :], scf.rearrange("s t f -> (s t) f"), [], [R_scfS], R_scfS)
                for ff in range(NFF):
                    g = 0 if ff < 16 else 1
                    col = (ff % 16) * 32
                    P.op("pe", _mk("transpose", out=pg[g][:, col:col + 32], in_=scfS[0:32, ff * 128:(ff + 1) * 128],
                                   identity=identf2[0:32, 0:32]), [R_scfS, R_c2], [R_pg[g]], inc=(ff == 15 or ff == NFF - 1))
                P.op("dve", _mk("tensor_copy", out=shist[:, 0:16, :].rearrange("p a b -> p (a b)"), in_=pg[0][:, 0:512]),
                     [R_pg[0]], [R_shist])
                P.op("dve", _mk("tensor_copy", out=shist[:, 16:NFF, :].rearrange("p a b -> p (a b)"), in_=pg[1][:, 0:192]),
                     [R_pg[1]], [R_shist])
                st2["pg"] = 0
                bufs = [load_norm_T(33, 0, 0)]
                gate_up(128, 0)
                down_final(bufs, 0, lambda jcol: ys[:, :])
                for g6 in range(6):
                    w0 = g6 * 512
                    wn = min(512, DFF - w0)
                    g = nxt2("pg", NPG)
                    for c in range(8):
                        P.op("pe", _mk("matmul", pg[g][:, 0:wn], h2T[0][:, c, 0:128], wg_sb[:, c, w0:w0 + wn],
                                       start=(c == 0), stop=(c == 7)), [R_h2T[0], R_w2], [R_pg[g]], inc=(c == 7))
                    P.op("act", _mk("activation", out=cfoS[:, w0:w0 + wn], in_=pg[g][:, 0:wn], func=AF.Copy), [R_pg[g]], [R_cfoS])
                for s in range(16):
                    out_tickets.append(P.dma(cfs[s, :, :], cfoS[8 * s + 6:8 * s + 8, :], [R_cfoS], [], R_cfoS))

        with ExitStack() as SB1:
            phase2_body(SB1, "p", False)
            P.barrier()
            P.flush()
        with ExitStack() as SB2:
            phase2_body(SB2, "s", True)
            need = {}
            for k_, v_ in out_tickets:
                if need.get(k_, 0) < v_:
                    need[k_] = v_
            for k_, v_ in need.items():
                P.plan["sp"].append(("w", P.sems[k_], v_))
            P.barrier()
            P.flush()
    top.close()
    return nc


def _consts():
    k = np.arange(128)[:, None]
    q = np.arange(128)[None, :]
    Lw = (k <= q).astype(np.float32)
    U = (k >= q).astype(np.float32)
    m2 = np.concatenate([U, Lw, U, Lw, Lw, U, Lw, U], axis=1)
    maskr = np.zeros((128, 4, 32), np.float32)
    for j in range(4):
        kn = (np.arange(128) - 32 * j) % 128
        maskr[:, j, :] = (kn[:, None] >= np.arange(32)[None, :]).astype(np.float32)
    L32 = Lw[0:32, 0:32]
    lw32 = np.ones((128, 32), np.float32)
    lw3 = np.zeros((128, 32), np.float32)
    for base in (0, 64):
        lw32[base:base + 32] = L32
        lw3[base + 32:base + 64] = L32
    ident = np.eye(128, dtype=np.float32)
    sm = np.zeros((128, 16, 96), np.float32)
    m_ = np.arange(128)
    for r in range(4):
        for t in range(8):
            d4 = ((t % 4) == r) * np.where(t >= 4, m_ >= 1, True)
            d1 = (4 * m_ + r >= 384 + t)
            for hh in range(2):
                sm[:, :, 16 * r + 8 * hh + t] = (d4.astype(np.float32) + d1.astype(np.float32))[:, None]
    sm[:, :, 64:80] = 1.0
    for s in range(16):
        for u in range(8):
            for t in range(8):
                mult = float(u <= t) + float(u <= t and (t - u) % 4 == 0) + float(u == t)
                for hh in range(2):
                    sm[8 * s + u, s, 80 + 8 * hh + t] = mult
    return m2, maskr.reshape(128, 128), lw32, lw3, ident, sm.reshape(128, 16 * 96)


_NC_CACHE = {}


def kernel(x_prompt, x_sample, cache_k_win, cache_v_win, state_conv_a, state_conv_ffn,
           norm_mix_g, w_in, conv_a_w, conv_a_b, ln_a_g, ln_a_b, w_out,
           norm_ffn_g, w_ffn_gate, w_ffn_up, ffn_conv_w, ffn_conv_b, w_ffn_down, norm_final_g):
    f = np.float32
    x_prompt = np.asarray(x_prompt, f)
    x_sample = np.asarray(x_sample, f)
    ckw = np.asarray(cache_k_win, f)[0].reshape(128, 2048, DATT)
    cvw = np.asarray(cache_v_win, f)[0].reshape(128, 2048, DATT)
    sca_ = np.asarray(state_conv_a, f)[0]
    scf_ = np.asarray(state_conv_ffn, f)[0]
    m2, maskr, lw32, lw3, ident, smask = _consts()

    def pc(v, n):
        return np.ascontiguousarray(np.asarray(v, f).reshape(n, 128).T)

    common = {
        "win": np.ascontiguousarray(np.asarray(w_in, f)[0]),
        "wout": np.ascontiguousarray(np.asarray(w_out, f)[0]),
        "wg": np.ascontiguousarray(np.asarray(w_ffn_gate, f)[0]),
        "wu": np.ascontiguousarray(np.asarray(w_ffn_up, f)[0]),
        "wd": np.ascontiguousarray(np.asarray(w_ffn_down, f)[0]),
        "gmix": pc(np.asarray(norm_mix_g)[0], 8),
        "gffn": pc(np.asarray(norm_ffn_g)[0], 8),
        "caw": np.ascontiguousarray(np.asarray(conv_a_w, f)[0].T.reshape(2, 128, 31).transpose(1, 0, 2).reshape(128, 62)),
        "cab": pc(np.asarray(conv_a_b)[0], 2),
        "lng": pc(np.asarray(ln_a_g)[0], 2),
        "lnb": pc(np.asarray(ln_a_b)[0], 2),
        "fcw": np.ascontiguousarray(np.asarray(ffn_conv_w, f)[0].T.reshape(NFF, 128, 3).transpose(1, 0, 2).reshape(128, NFF * 3)),
        "fcb": pc(np.asarray(ffn_conv_b)[0], NFF),
        "gfin": np.ascontiguousarray(np.broadcast_to(np.asarray(norm_final_g, f)[None, :], (128, D))),
        "m2": m2, "maskr": maskr, "lw32": lw32, "lw3": lw3, "ident": ident,
        "smask": smask,
    }
    in_maps = []
    for c in range(NCORES):
        b, half = c // 2, c % 2
        xpc = np.zeros((NT * TT, D), f)
        pre = (NHALO + NPRE) * TT
        if half == 0:
            xpc[pre:] = x_prompt[b, 0:4096]
        else:
            xpc[:] = x_prompt[b, 4096 - pre:8192]
        flags = np.zeros((128, 8), f)
        flags[:, 0:4] = float(half)
        flags[:, 4:8] = 1.0
        m = dict(common)
        m.update({
            "xp": xpc,
            "xs": np.ascontiguousarray(x_sample[16 * c:16 * c + 16].reshape(128, D)),
            "ck": np.ascontiguousarray(ckw[16 * c:16 * c + 16]),
            "cv": np.ascontiguousarray(cvw[16 * c:16 * c + 16]),
            "sca": np.ascontiguousarray(sca_[16 * c:16 * c + 16]),
            "scf": np.ascontiguousarray(scf_[16 * c:16 * c + 16]),
            "flags": flags,
        })
        in_maps.append(m)
    if "nc" not in _NC_CACHE:
        _NC_CACHE["nc"] = build_program()
    nc = _NC_CACHE["nc"]
    res = run_bass_kernel_spmd(nc, in_maps, core_ids=list(range(NCORES)))
    R = res.results
    y_prompt = np.stack([np.concatenate([R[2 * b]["y"], R[2 * b + 1]["y"]], axis=0) for b in range(4)])
    y_sample = np.concatenate([R[c]["ys"] for c in range(NCORES)], axis=0).reshape(128, 8, D)
    kp = np.stack([R[2 * b + 1]["kwin"].reshape(2048, NH, 64) for b in range(4)])[None]
    vp = np.stack([R[2 * b + 1]["vwin"].reshape(2048, NH, 64) for b in range(4)])[None]
    capo = np.stack([R[2 * b + 1]["cap"] for b in range(4)])[None]
    cfpo = np.stack([R[2 * b + 1]["cfp"] for b in range(4)])[None]
    ksno = np.concatenate([R[c]["ksn"] for c in range(NCORES)], axis=0).reshape(1, 128, 8, NH, 64)
    vsno = np.concatenate([R[c]["vsn"] for c in range(NCORES)], axis=0).reshape(1, 128, 8, NH, 64)
    caso = np.concatenate([R[c]["cas"] for c in range(NCORES)], axis=0)[None]
    cfso = np.concatenate([R[c]["cfs"] for c in range(NCORES)], axis=0)[None]
    return tuple(np.asarray(a, f) for a in (y_prompt, y_sample, kp, vp, capo, cfpo, ksno, vsno, caso, cfso))
```
